# Optimizing a Trainium2 kernel written in Bass

```python
import jax
import jax.numpy as jnp
from jax import lax
import numpy as np

D_MODEL = 1024
BATCH = 4
SEQ = 8192
DEPTH = 2
DEC_BATCH = 8
DEC_SEQ = 8192
PAST_LEN = 128

GRID_W = 64
EPS = 1e-6
NEG = -1e30
TINY = 1e-30
HEAD_DIM = 64
ROPE_THETA = 10000.0
N_BRANCH = 4
D_FF = 4 * D_MODEL

POOL_WIDTH = D_MODEL // 4
POOL_WINDOWS = (2, 4, 8, 16)
POOL_GW = POOL_WIDTH // len(POOL_WINDOWS)

GQA_HEADS = D_MODEL // 128
GQA_KV_HEADS = GQA_HEADS // 4
GQA_WINDOW = 128
GQA_BLOCK = 128

HGRN_HEADS = D_MODEL // 256
HGRN_DK = 64
HGRN_DV = 64
HGRN_CHUNK = 16

NAT_HEADS = D_MODEL // 256
NAT_KR_MAX = 8
NAT_KC = 16
NAT_QCB = 16
NAT_KCB = NAT_QCB + NAT_KC

GQA_QW = GQA_HEADS * HEAD_DIM
GQA_KVW = GQA_KV_HEADS * HEAD_DIM
HGRN_KW = HGRN_HEADS * HGRN_DK
HGRN_VW = HGRN_HEADS * HGRN_DV
NAT_W = NAT_HEADS * HEAD_DIM
IN_SPLITS = (POOL_WIDTH, GQA_QW, GQA_KVW, GQA_KVW, HGRN_KW, HGRN_KW, HGRN_VW, HGRN_KW, HGRN_VW, NAT_W, NAT_W, NAT_W, N_BRANCH * D_MODEL)
IN_WIDTH = sum(IN_SPLITS)

kernel_name = 'hybrid_bidir_encoder_trunk'


def _split_points():
    pts, acc = [], 0
    for w in IN_SPLITS[:-1]:
        acc += w
        pts.append(acc)
    return pts


def _rmsnorm(x, g):
    xf = x.astype(jnp.float32)
    y = xf * lax.rsqrt(jnp.mean(xf * xf, axis=-1, keepdims=True) + EPS)
    return (y * g.astype(jnp.float32)).astype(x.dtype)


def _rope(x, pos):
    half = x.shape[-1] // 2
    inv = ROPE_THETA ** (-jnp.arange(half, dtype=jnp.float32) / half)
    ang = pos.astype(jnp.float32)[:, None] * inv[None, :]
    cos = jnp.cos(ang)[:, None, :]
    sin = jnp.sin(ang)[:, None, :]
    xf = x.astype(jnp.float32)
    x1, x2 = xf[..., :half], xf[..., half:]
    return jnp.concatenate([x1 * cos - x2 * sin, x2 * cos + x1 * sin], axis=-1).astype(x.dtype)


def _pool_mixer(u, w_pool, scale):
    B, T, C = u.shape
    uf = u.astype(jnp.float32)
    cs = jnp.concatenate([jnp.zeros((B, 1, C), jnp.float32), jnp.cumsum(uf, axis=1)], axis=1)
    t = jnp.arange(T)
    means = []
    for g, w in enumerate(POOL_WINDOWS):
        lo = jnp.clip(t - w // 2, 0, T)
        hi = jnp.clip(t + w // 2, 0, T)
        csg = cs[..., g * POOL_GW:(g + 1) * POOL_GW]
        cnt = (hi - lo).astype(jnp.float32)[None, :, None]
        means.append((jnp.take(csg, hi, axis=1) - jnp.take(csg, lo, axis=1)) / cnt)
    mixed = (jnp.concatenate(means, axis=-1) - uf).reshape(B, T, len(POOL_WINDOWS), POOL_GW)
    out = jnp.einsum('btgc,gcd->btgd', mixed, w_pool.astype(jnp.float32)).reshape(B, T, C)
    return (out * scale.astype(jnp.float32)).astype(u.dtype)


def _window_gqa(q, k, v, sink):
    B, T, Hq, dh = q.shape
    Hkv = k.shape[2]
    G = Hq // Hkv
    Bk = GQA_BLOCK
    nb = T // Bk
    pad = ((0, 0), (Bk, Bk), (0, 0), (0, 0))
    kp = jnp.pad(k, pad).reshape(B, nb + 2, Bk, Hkv, dh)
    vp = jnp.pad(v, pad).reshape(B, nb + 2, Bk, Hkv, dh)
    band = lambda a: jnp.concatenate([a[:, :-2], a[:, 1:-1], a[:, 2:]], axis=2)
    kb, vb = band(kp), band(vp)
    qb = q.reshape(B, nb, Bk, Hkv, G, dh)
    s = jnp.einsum('bnqhgd,bnkhd->bnhgqk', qb, kb, preferred_element_type=jnp.float32) * (dh ** -0.5)
    blk = jnp.arange(nb)[:, None] * Bk
    qpos = blk + jnp.arange(Bk)[None, :]
    kpos = blk - Bk + jnp.arange(3 * Bk)[None, :]
    kp3 = kpos[:, None, :]
    valid = (jnp.abs(kp3 - qpos[:, :, None]) <= GQA_WINDOW) & (kp3 >= 0) & (kp3 < T)
    s = jnp.where(valid[None, :, None, None], s, NEG)
    sink_l = sink.astype(jnp.float32).reshape(Hkv, G)[:, :, None, None]
    m = jnp.maximum(jnp.max(s, axis=-1, keepdims=True), sink_l)
    p = jnp.exp(s - m)
    p = p / (jnp.sum(p, axis=-1, keepdims=True) + jnp.exp(sink_l - m))
    o = jnp.einsum('bnhgqk,bnkhd->bnqhgd', p.astype(v.dtype), vb)
    return o.reshape(B, T, Hq * dh)


def _gla_causal(q, k, v, logf):
    B, T, H, dk = q.shape
    dv = v.shape[-1]
    C = HGRN_CHUNK
    nc = T // C
    q = q.reshape(B, nc, C, H, dk)
    k = k.reshape(B, nc, C, H, dk)
    logf = logf.reshape(B, nc, C, H, dk)
    v = v.reshape(B, nc, C, H, dv)
    b = jnp.cumsum(logf, axis=2)
    causal = jnp.tril(jnp.ones((C, C), dtype=bool))[None, None, :, :, None, None]
    diff = jnp.where(causal, b[:, :, :, None] - b[:, :, None, :], 0.0)
    decay = jnp.where(causal, jnp.exp(diff), 0.0)
    attn = jnp.einsum('bntshd,bnthd,bnshd->bnhts', decay, q, k)
    o_intra = jnp.einsum('bnhts,bnshv->bnthv', attn, v)
    b_last = b[:, :, -1]
    u = jnp.einsum('bnshd,bnshv->bnhdv', k * jnp.exp(b_last[:, :, None] - b), v)
    a = jnp.exp(b_last)

    def step(state, inp):
        a_n, u_n = inp
        return a_n[..., None] * state + u_n, state

    s0 = jnp.zeros((B, H, dk, dv), jnp.float32)
    _, s_start = lax.scan(step, s0, (jnp.moveaxis(a, 1, 0), jnp.moveaxis(u, 1, 0)))
    s_start = jnp.moveaxis(s_start, 0, 1)
    o_inter = jnp.einsum('bnthd,bnhdv->bnthv', q * jnp.exp(b), s_start)
    return (o_intra + o_inter).reshape(B, T, H, dv)


def _hgrn_mixer(zff, zfb, zi, zq, zg, lb, onorm_g):
    B, T, _ = zi.shape
    f32 = jnp.float32
    shp = lambda z, d: z.astype(f32).reshape(B, T, HGRN_HEADS, d)
    q = jax.nn.silu(shp(zq, HGRN_DK))
    v = shp(zi, HGRN_DV)

    def gates(zf, lbd):
        f = lbd + (1.0 - lbd) * jax.nn.sigmoid(zf.astype(f32))
        logf = jnp.log(jnp.maximum(f, TINY))
        key = 1.0 - f
        return shp(key, HGRN_DK), shp(logf, HGRN_DK)

    k_f, lf_f = gates(zff, lb[0])
    k_b, lf_b = gates(zfb, lb[1])
    flip = lambda a: jnp.flip(a, axis=1)
    o = _gla_causal(q, k_f, v, lf_f) + flip(_gla_causal(flip(q), flip(k_b), flip(v), flip(lf_b)))
    o = _rmsnorm(o, onorm_g) * jax.nn.silu(shp(zg, HGRN_DV))
    return o.reshape(B, T, HGRN_VW).astype(zi.dtype)


def _neighborhood_attention(q, k, v, rpb):
    B, T, H, dh = q.shape
    rows = T // GRID_W
    kr = min(NAT_KR_MAX, rows)
    ncb = GRID_W // NAT_QCB
    r = jnp.arange(rows)
    row_idx = jnp.clip(r - kr // 2, 0, rows - kr)[:, None] + jnp.arange(kr)[None, :]
    qcol = jnp.arange(GRID_W).reshape(ncb, NAT_QCB)
    qc0 = jnp.clip(qcol - NAT_KC // 2, 0, GRID_W - NAT_KC)
    col_idx = jnp.clip(jnp.arange(ncb) * NAT_QCB - NAT_KC // 2, 0, GRID_W - NAT_KCB)[:, None] + jnp.arange(NAT_KCB)[None, :]
    kgrid = k.reshape(B, rows, GRID_W, H, dh)
    vgrid = v.reshape(B, rows, GRID_W, H, dh)
    ri = row_idx[:, None, :, None]
    ci = col_idx[None, :, None, :]
    kb = kgrid[:, ri, ci]
    vb = vgrid[:, ri, ci]
    qb = q.reshape(B, rows, ncb, NAT_QCB, H, dh)
    s = jnp.einsum('brcqhd,brcikhd->brchqik', qb, kb, preferred_element_type=jnp.float32) * (dh ** -0.5)
    drow = row_idx - r[:, None] + (NAT_KR_MAX - 1)
    dcol = jnp.clip(col_idx[:, None, :] - qcol[:, :, None] + (NAT_KC - 1), 0, 2 * NAT_KC - 2)
    bias = rpb.astype(jnp.float32)[:, drow[:, None, None, :, None], dcol[None, :, :, None, :]]
    s = s + jnp.transpose(bias, (1, 2, 0, 3, 4, 5))[None]
    in_win = (col_idx[:, None, :] >= qc0[:, :, None]) & (col_idx[:, None, :] < qc0[:, :, None] + NAT_KC)
    s = jnp.where(in_win[None, None, :, None, :, None, :], s, NEG)
    p = jax.nn.softmax(s.reshape(B, rows, ncb, H, NAT_QCB, kr * NAT_KCB), axis=-1)
    o = jnp.einsum('brchqn,brcnhd->brcqhd', p.astype(v.dtype), vb.reshape(B, rows, ncb, kr * NAT_KCB, H, dh))
    return o.reshape(B, T, H * dh)


def _trunk(x, norm1_g, w_in, pool_w, pool_scale, gqa_qnorm, gqa_knorm, gqa_sink, hgrn_lb, hgrn_onorm,
           nat_qnorm, nat_knorm, nat_rpb, w_br_pool, w_br_gqa, w_br_hgrn, w_br_nat, w_o, norm2_g, w_up, w_down):
    B, T, D = x.shape
    pos = jnp.arange(T)
    sm = jax.nn.softmax(hgrn_lb.astype(jnp.float32), axis=0)
    lower = jnp.cumsum(sm, axis=0) - sm[:1]
    pts = _split_points()
    for l in range(DEPTH):
        h = _rmsnorm(x, norm1_g[l])
        z = jnp.einsum('btd,de->bte', h, w_in[l])
        (zp, zq, zk, zv, zff, zfb, zi, zhq, zhg, znq, znk, znv, zgate) = jnp.split(z, pts, axis=-1)
        oa = _pool_mixer(zp, pool_w[l], pool_scale[l])
        q = _rope(_rmsnorm(zq.reshape(B, T, GQA_HEADS, HEAD_DIM), gqa_qnorm[l]), pos)
        k = _rope(_rmsnorm(zk.reshape(B, T, GQA_KV_HEADS, HEAD_DIM), gqa_knorm[l]), pos)
        ob = _window_gqa(q, k, zv.reshape(B, T, GQA_KV_HEADS, HEAD_DIM), gqa_sink[l])
        oc = _hgrn_mixer(zff, zfb, zi, zhq, zhg, lower[l], hgrn_onorm[l])
        nq = _rmsnorm(znq.reshape(B, T, NAT_HEADS, HEAD_DIM), nat_qnorm[l])
        nk = _rmsnorm(znk.reshape(B, T, NAT_HEADS, HEAD_DIM), nat_knorm[l])
        od = _neighborhood_attention(nq, nk, znv.reshape(B, T, NAT_HEADS, HEAD_DIM), nat_rpb[l])
        g = jax.nn.sigmoid(zgate.astype(jnp.float32)).astype(x.dtype).reshape(B, T, N_BRANCH, D)
        merged = (g[:, :, 0] * jnp.einsum('btc,cd->btd', oa, w_br_pool[l])
                  + g[:, :, 1] * jnp.einsum('btc,cd->btd', ob, w_br_gqa[l])
                  + g[:, :, 2] * jnp.einsum('btc,cd->btd', oc, w_br_hgrn[l])
                  + g[:, :, 3] * jnp.einsum('btc,cd->btd', od, w_br_nat[l]))
        x = x + jnp.einsum('btd,de->bte', merged, w_o[l])
        h2 = _rmsnorm(x, norm2_g[l])
        hid = jnp.square(jax.nn.relu(jnp.einsum('btd,df->btf', h2, w_up[l])))
        x = x + jnp.einsum('btf,fd->btd', hid, w_down[l])
    return x


def setup_inputs(seed: int = 0) -> dict:
    key = jax.random.key(seed)
    ks = jax.random.split(key, 22)
    nrm = lambda kk, shape, scale: scale * jax.random.normal(kk, shape, jnp.float32)
    L = DEPTH
    return {
        'x_prompt': nrm(ks[0], (BATCH, SEQ, D_MODEL), 1.0),
        'x_sample': nrm(ks[1], (DEC_BATCH, DEC_SEQ, D_MODEL), 1.0),
        'norm1_g': 1.0 + nrm(ks[2], (L, D_MODEL), 0.1),
        'w_in': nrm(ks[3], (L, D_MODEL, IN_WIDTH), D_MODEL ** -0.5),
        'pool_w': nrm(ks[4], (L, len(POOL_WINDOWS), POOL_GW, POOL_GW), POOL_GW ** -0.5),
        'pool_scale': 1.0 + nrm(ks[5], (L, POOL_WIDTH), 0.1),
        'gqa_qnorm': 1.0 + nrm(ks[6], (L, HEAD_DIM), 0.1),
        'gqa_knorm': 1.0 + nrm(ks[7], (L, HEAD_DIM), 0.1),
        'gqa_sink': nrm(ks[8], (L, GQA_HEADS), 0.5),
        'hgrn_lb': nrm(ks[9], (L, 2, HGRN_KW), 0.5),
        'hgrn_onorm': 1.0 + nrm(ks[10], (L, HGRN_DV), 0.1),
        'nat_qnorm': 1.0 + nrm(ks[11], (L, HEAD_DIM), 0.1),
        'nat_knorm': 1.0 + nrm(ks[12], (L, HEAD_DIM), 0.1),
        'nat_rpb': nrm(ks[13], (L, NAT_HEADS, 2 * NAT_KR_MAX - 1, 2 * NAT_KC - 1), 0.5),
        'w_br_pool': nrm(ks[14], (L, POOL_WIDTH, D_MODEL), POOL_WIDTH ** -0.5),
        'w_br_gqa': nrm(ks[15], (L, GQA_QW, D_MODEL), GQA_QW ** -0.5),
        'w_br_hgrn': nrm(ks[16], (L, HGRN_VW, D_MODEL), HGRN_VW ** -0.5),
        'w_br_nat': nrm(ks[17], (L, NAT_W, D_MODEL), NAT_W ** -0.5),
        'w_o': nrm(ks[18], (L, D_MODEL, D_MODEL), D_MODEL ** -0.5),
        'norm2_g': 1.0 + nrm(ks[19], (L, D_MODEL), 0.1),
        'w_up': nrm(ks[20], (L, D_MODEL, D_FF), D_MODEL ** -0.5),
        'w_down': nrm(ks[21], (L, D_FF, D_MODEL), D_FF ** -0.5),
    }


def reference(x_prompt, x_sample, norm1_g, w_in, pool_w, pool_scale, gqa_qnorm, gqa_knorm, gqa_sink, hgrn_lb,
              hgrn_onorm, nat_qnorm, nat_knorm, nat_rpb, w_br_pool, w_br_gqa, w_br_hgrn, w_br_nat, w_o, norm2_g,
              w_up, w_down):
    y_prompt = _trunk(x_prompt, norm1_g, w_in, pool_w, pool_scale, gqa_qnorm, gqa_knorm, gqa_sink, hgrn_lb,
                      hgrn_onorm, nat_qnorm, nat_knorm, nat_rpb, w_br_pool, w_br_gqa, w_br_hgrn, w_br_nat, w_o,
                      norm2_g, w_up, w_down)
    y_sample = _trunk(x_sample, norm1_g, w_in, pool_w, pool_scale, gqa_qnorm, gqa_knorm, gqa_sink, hgrn_lb,
                      hgrn_onorm, nat_qnorm, nat_knorm, nat_rpb, w_br_pool, w_br_gqa, w_br_hgrn, w_br_nat, w_o,
                      norm2_g, w_up, w_down)
    return (y_prompt, y_sample)
```

```python
import numpy as np
from contextlib import ExitStack
import concourse.bass as bass
import concourse.mybir as mybir
from concourse.bass_utils import run_bass_kernel_spmd

F32 = mybir.dt.float32
BF16 = mybir.dt.bfloat16
AF = mybir.ActivationFunctionType
ALU = mybir.AluOpType
AX = mybir.AxisListType

D = 1024
DEPTH = 2
NCORES = 8
EPS = 1e-6
NEGM = -30000.0
SAME_ENGINE_SYNC = True
SAME_ENGINE_DIST = 3


class Sem:
    def __init__(self, handle, sid):
        self.h = handle
        self.cnt = 0
        self.id = sid


class Eng:
    def __init__(self, name, h, sem):
        self.name = name
        self.h = h
        self.sem = sem
        self.ms = 0
        self.waited = {}
        self.last = None
        self.nseq = 0


class Op:
    __slots__ = ("eng", "fn", "deps", "ms", "dma", "dsem", "dval", "marked", "seq")

    def __init__(self, eng, fn, deps, dma=False, dsem=None, dval=0):
        self.eng = eng
        self.fn = fn
        self.deps = deps
        self.ms = 0
        self.dma = dma
        self.dsem = dsem
        self.dval = dval
        self.marked = False
        self.seq = 0


class Buf:
    __slots__ = ("w", "rs", "dsem", "name")

    def __init__(self, name=""):
        self.w = None
        self.rs = {}
        self.dsem = None
        self.name = name


class Prog:
    def __init__(self, nc):
        self.nc = nc
        self.nsem = 0
        self.ops = []
        mk = lambda n, h: Eng(n, h, self.new_sem(n))
        self.pe = mk("pe", nc.tensor)
        self.act = mk("act", nc.scalar)
        self.dve = mk("dve", nc.vector)
        self.pool = mk("pool", nc.gpsimd)
        self.sp = mk("sp", nc.sync)
        self.engs = [self.pe, self.act, self.dve, self.pool, self.sp]
        self.dsems = []
        self.free_dsems = []
        self.gsem = self.new_dsem()
        self.n_ins = 0

    def new_sem(self, name):
        h = self.nc.alloc_semaphore(name=f"s_{name}_{self.nsem}")
        self.nsem += 1
        return Sem(h, self.nsem)

    def new_dsem(self):
        if self.free_dsems:
            return self.free_dsems.pop()
        s = self.new_sem("d")
        self.dsems.append(s)
        return s

    def bufs(self, n, name=""):
        return [Buf(f"{name}{i}") for i in range(n)]

    def release(self, bufs):
        for b in bufs:
            if b.dsem is not None:
                self.free_dsems.append(b.dsem)
                b.dsem = None

    def _deps(self, r, w):
        deps = []
        for b in r:
            if b.w is not None:
                deps.append(b.w)
        for b in w:
            if b.w is not None:
                deps.append(b.w)
            deps.extend(b.rs.values())
        return deps

    def _upd(self, op, key, r, w):
        for b in r:
            b.rs[key] = op
        for b in w:
            b.w = op
            b.rs = {}

    def op(self, eng, fn, r=(), w=()):
        o = Op(eng, fn, self._deps(r, w))
        eng.nseq += 1
        o.seq = eng.nseq
        self._upd(o, eng.name, r, w)
        self.ops.append(o)
        eng.last = o
        return o

    def dma(self, out, in_, r=(), w=(), sb=None, q=None, slow=False):
        q = q or self.sp
        if sb is None:
            ds = self.gsem
        else:
            if sb.dsem is None:
                sb.dsem = self.new_dsem()
            ds = sb.dsem
        ds.cnt += 16
        if slow:
            fn = lambda: q.h.dma_start(out=out, in_=in_, allow_slow_non_contiguous=True)
        else:
            fn = lambda: q.h.dma_start(out=out, in_=in_)
        o = Op(q, fn, self._deps(r, w), dma=True, dsem=ds, dval=ds.cnt)
        self._upd(o, ("d", ds.id), r, w)
        self.ops.append(o)
        ds.last = o
        return o

    def barrier(self):
        deps = [e.last for e in self.engs if e.last is not None]
        deps += [s.last for s in self.dsems if getattr(s, "last", None) is not None]
        for e in self.engs:
            o = Op(e, None, list(deps))
            self.ops.append(o)

    def _skip_same(self, o, d):
        if d.eng is not o.eng or o.dma or d.dma:
            return False
        if d.eng is self.pe or not SAME_ENGINE_SYNC:
            return True
        return (o.seq - d.seq) > SAME_ENGINE_DIST

    def flush(self):
        for o in self.ops:
            for d in o.deps:
                if not d.dma:
                    if self._skip_same(o, d):
                        continue
                    d.marked = True
        for o in self.ops:
            e = o.eng
            for d in o.deps:
                if d.dma:
                    sem, val = d.dsem, d.dval
                else:
                    if not d.marked:
                        continue
                    if self._skip_same(o, d):
                        continue
                    sem, val = d.eng.sem, d.ms
                if e.waited.get(sem.id, 0) >= val:
                    continue
                e.waited[sem.id] = val
                e.h.wait_ge(sem.h, val)
            if o.fn is None:
                continue
            ins = o.fn()
            self.n_ins += 1
            if o.dma:
                ins.then_inc(o.dsem.h, 16)
            elif o.marked:
                e.ms += 1
                o.ms = e.ms
                ins.then_inc(e.sem.h, 1)
        self.ops = []


def DAP(t, off, dims):
    return bass.AP(t, off, [[int(s), int(n)] for s, n in dims])


class Builder:
    def __init__(self, T, nseq, dbg=None):
        self.T = T
        self.NT = T // 128
        self.nseq = nseq
        self.dbg = dbg
        nc = bass.Bass("TRN2", target_bir_lowering=False)
        self.nc = nc
        self.P = Prog(nc)
        self.dram = {}
        self.pass_id = 0

    def din(self, name, shape, dt=F32):
        t = self.nc.dram_tensor(name, list(shape), dt, kind="ExternalInput")
        self.dram[name] = t
        return t

    def dscr(self, name, shape, dt, out=False):
        kind = "ExternalOutput" if (out or (self.dbg and name in self.dbg)) else "Internal"
        t = self.nc.dram_tensor(name, list(shape), dt, kind=kind)
        self.dram[name] = t
        return t

    def sb(self, st, name, shape, dt):
        self.pass_id += 1
        return st.enter_context(self.nc.sbuf_tensor(f"{name}_{self.pass_id}", list(shape), dt))

    def psum_banks(self, st):
        self.pass_id += 1
        banks = [st.enter_context(self.nc.psum_tensor(f"ps{i}_{self.pass_id}", [128, 512], F32)) for i in range(8)]
        return banks

    def declare(self):
        T, nseq = self.T, self.nseq
        L = DEPTH
        self.x_in = self.din("x", [nseq, T, D])
        self.y_out = self.dscr("y", [nseq, T, D], F32, out=True)
        w = {}
        w["norm1_g"] = self.din("norm1_g", [L, D])
        w["w_in"] = self.din("w_in", [L, D, 7168])
        w["pool_w"] = self.din("pool_w", [L, 4, 64, 64])
        w["pool_scale"] = self.din("pool_scale", [L, 256])
        w["gqa_qnorm"] = self.din("gqa_qnorm", [L, 64])
        w["gqa_knorm"] = self.din("gqa_knorm", [L, 64])
        w["gqa_sink"] = self.din("gqa_sink", [L, 8])
        w["hgrn_lb"] = self.din("hgrn_lb", [L, 2, 256])
        w["hgrn_onorm"] = self.din("hgrn_onorm", [L, 64])
        w["nat_qnorm"] = self.din("nat_qnorm", [L, 64])
        w["nat_knorm"] = self.din("nat_knorm", [L, 64])
        w["nat_rpb"] = self.din("nat_rpb", [L, 4, 15, 31])
        w["w_br_pool"] = self.din("w_br_pool", [L, 256, D])
        w["w_br_gqa"] = self.din("w_br_gqa", [L, 512, D])
        w["w_br_hgrn"] = self.din("w_br_hgrn", [L, 256, D])
        w["w_br_nat"] = self.din("w_br_nat", [L, 256, D])
        w["w_o"] = self.din("w_o", [L, D, D])
        w["norm2_g"] = self.din("norm2_g", [L, D])
        w["w_up"] = self.din("w_up", [L, D, 4096])
        w["w_down"] = self.din("w_down", [L, 4096, D])
        self.w = w
        self.c_ident = self.din("c_ident", [128, 128], BF16)
        self.wb_in = self.dscr("wb_in", [L, D, 7168], BF16)
        self.wb_br = self.dscr("wb_br", [L, 1280, D], BF16)
        self.wb_o = self.dscr("wb_o", [L, D, D], BF16)
        self.wb_up = self.dscr("wb_up", [L, D, 4096], BF16)
        self.wb_down = self.dscr("wb_down", [L, 4096, D], BF16)
        self.xa = self.dscr("xa", [nseq, T, D], F32)
        self.xb = self.dscr("xb", [nseq, T, D], F32)
        self.oT = self.dscr("oT", [1280, T], BF16)

    def prologue(self):
        nc, P = self.nc, self.P
        with ExitStack() as st:
            CW = 4096
            stg = [self.sb(st, f"wst{i}", [128, CW], F32) for i in range(3)]
            stgb = [self.sb(st, f"wsb{i}", [128, CW], BF16) for i in range(3)]
            sB = P.bufs(3, "wst")
            sBb = P.bufs(3, "wsb")
            gt = self.sb(st, "gt", [128, 2, DEPTH, 8], F32)
            gB = Buf("gt")
            for i, nm in enumerate(["norm1_g", "norm2_g"]):
                for l in range(DEPTH):
                    P.dma(gt[:, i, l, :], DAP(self.w[nm], l * D, [(1, 128), (128, 8)]), w=[gB], sb=gB, slow=True)
            jobs = []
            for l in range(DEPTH):
                jobs.append((self.w["w_in"], l * D * 7168, 7168, D, self.wb_in, l * D * 7168, (0, l)))
                srcs = [("w_br_pool", 256), ("w_br_gqa", 512), ("w_br_hgrn", 256), ("w_br_nat", 256)]
                ro = 0
                for nm, rows in srcs:
                    jobs.append((self.w[nm], l * rows * D, D, rows, self.wb_br, (l * 1280 + ro) * D, None))
                    ro += rows
                jobs.append((self.w["w_o"], l * D * D, D, D, self.wb_o, l * D * D, None))
                jobs.append((self.w["w_up"], l * D * 4096, 4096, D, self.wb_up, l * D * 4096, (1, l)))
                jobs.append((self.w["w_down"], l * 4096 * D, D, 4096, self.wb_down, l * 4096 * D, None))
            k = 0
            for (src, soff, ncols, nrows, dst, doff, gsel) in jobs:
                nrb_tot = nrows // 128
                if gsel is not None or ncols >= CW:
                    chunks = [(rb, 1, c0, min(CW, ncols - c0)) for rb in range(nrb_tot) for c0 in range(0, ncols, CW)]
                else:
                    per = CW // ncols
                    chunks = [(rb, min(per, nrb_tot - rb), 0, ncols) for rb in range(0, nrb_tot, per)]
                for (rb, nrb, c0, cw) in chunks:
                    i = k % 3
                    n = nrb * cw
                    sap = DAP(src, soff + rb * 128 * ncols + c0, [(ncols, 128), (128 * ncols, nrb), (1, cw)])
                    dap = DAP(dst, doff + rb * 128 * ncols + c0, [(ncols, 128), (128 * ncols, nrb), (1, cw)])
                    v32 = stg[i][:, 0:n].rearrange("p (r c) -> p r c", r=nrb)
                    v16 = stgb[i][:, 0:n].rearrange("p (r c) -> p r c", r=nrb)
                    P.dma(v32, sap, w=[sB[i]], sb=sB[i])
                    if gsel is not None:
                        sc = gt[:, gsel[0], gsel[1], rb:rb + 1]
                        rr = [sB[i], gB]
                    else:
                        sc = 1.0
                        rr = [sB[i]]
                    if k % 2 == 0:
                        P.op(P.act, lambda o=stgb[i][:, 0:n], a=stg[i][:, 0:n], s=sc:
                             nc.scalar.activation(out=o, in_=a, func=AF.Copy, scale=s), r=rr, w=[sBb[i]])
                    else:
                        if gsel is not None:
                            P.op(P.dve, lambda o=stgb[i][:, 0:n], a=stg[i][:, 0:n], s=sc:
                                 nc.vector.tensor_scalar(out=o, in0=a, scalar1=s, scalar2=None, op0=ALU.mult),
                                 r=rr, w=[sBb[i]])
                        else:
                            P.op(P.dve, lambda o=stgb[i][:, 0:n], a=stg[i][:, 0:n]:
                                 nc.vector.tensor_copy(out=o, in_=a), r=rr, w=[sBb[i]])
                    P.dma(dap, v16, r=[sBb[i]], sb=sBb[i])
                    k += 1
            P.barrier()
            P.flush()
            P.release(sB + sBb + [gB])

    def norm_transpose(self, x_ap, xB, hT, hTB, col0, tmp):
        nc, P = self.nc, self.P
        junk, jB, ss, ssB, hb, hbB, tps, tpsB, ident, eps_t, cB = tmp
        P.op(P.act, lambda: nc.scalar.activation(out=junk[:], in_=x_ap, func=AF.Square, accum_out=ss[:, 0:1]),
             r=[xB], w=[jB, ssB])
        P.op(P.act, lambda: nc.scalar.activation(out=ss[:, 1:2], in_=ss[:, 0:1], func=AF.Sqrt, scale=1.0 / D,
                                                 bias=eps_t[:, 0:1]), r=[ssB, cB], w=[ssB])
        P.op(P.dve, lambda: nc.vector.reciprocal(out=ss[:, 2:3], in_=ss[:, 1:2]), r=[ssB], w=[ssB])
        P.op(P.act, lambda: nc.scalar.activation(out=hb[:], in_=x_ap, func=AF.Copy, scale=ss[:, 2:3]),
             r=[xB, ssB], w=[hbB])
        tp = tps[:].bitcast(BF16)
        for c in range(8):
            P.op(P.pe, lambda c=c: nc.tensor.transpose(tp[:, c * 128:(c + 1) * 128], hb[:, c * 128:(c + 1) * 128],
                                                      ident[:]), r=[hbB, cB], w=[tpsB])
        P.op(P.dve, lambda: nc.vector.tensor_copy(out=hT[:, :, col0:col0 + 128],
                                                  in_=tp.rearrange("p (c t) -> p c t", c=8)),
             r=[tpsB], w=[hTB])

    def ffn(self, l, s, src, dst):
        nc, P, T = self.nc, self.P, self.T
        TT = 256
        with ExitStack() as st:
            wup = self.sb(st, "wup", [128, 8, 4096], BF16)
            wdn = self.sb(st, "wdn", [128, 32, 1024], BF16)
            wupB, wdnB = Buf("wup"), Buf("wdn")
            ident = self.sb(st, "ident", [128, 128], BF16)
            eps_t = self.sb(st, "eps", [128, 1], F32)
            cB = Buf("c")
            xt = [self.sb(st, f"xt{i}", [128, 2, D], F32) for i in range(2)]
            xB = P.bufs(2, "xt")
            hT = self.sb(st, "hT", [128, 8, TT], BF16)
            hTB = Buf("hT")
            hid = self.sb(st, "hid", [128, 32, TT], BF16)
            hidB = Buf("hid")
            junk = self.sb(st, "junk", [128, D], BF16)
            ss = self.sb(st, "ss", [128, 4], F32)
            hb = self.sb(st, "hb", [128, D], BF16)
            rl = [self.sb(st, f"rl{i}", [128, TT], F32) for i in range(2)]
            rlB = P.bufs(2, "rl")
            ot = [self.sb(st, f"ot{i}", [128, 2, D], F32) for i in range(2)]
            otB = P.bufs(2, "ot")
            banks = self.psum_banks(st)
            bB = P.bufs(8, "bank")
            tmp = (junk, Buf(), ss, Buf(), hb, Buf(), banks[7], bB[7], ident, eps_t, cB)
            P.dma(ident[:], self.c_ident.ap(), w=[cB], sb=cB)
            P.op(P.dve, lambda: nc.vector.memset(eps_t[:], EPS), w=[cB])
            for k0 in range(0, 8, 4):
                P.dma(wup[:, k0:k0 + 4, :], DAP(self.wb_up, (l * D + k0 * 128) * 4096,
                                                 [(4096, 128), (128 * 4096, 4), (1, 4096)]), w=[wupB], sb=wupB)
            for f0 in range(0, 32, 8):
                P.dma(wdn[:, f0:f0 + 8, :], DAP(self.wb_down, (l * 4096 + f0 * 128) * D,
                                                 [(D, 128), (128 * D, 8), (1, D)]), w=[wdnB], sb=wdnB)
            ntile = T // TT

            def load(i):
                b = i % 2
                P.dma(xt[b][:], DAP(src, (s * T + i * TT) * D, [(D, 128), (128 * D, 2), (1, D)]), w=[xB[b]], sb=xB[b])

            load(0)
            for i in range(ntile):
                b = i % 2
                if i + 1 < ntile:
                    load(i + 1)
                for su in range(2):
                    self.norm_transpose(xt[b][:, su, :], xB[b], hT, hTB, su * 128, tmp)
                for fc in range(32):
                    pb = fc % 3
                    for kc in range(8):
                        P.op(P.pe, lambda fc=fc, kc=kc, pb=pb: nc.tensor.matmul(
                            banks[pb][:, 0:TT], lhsT=wup[:, kc, fc * 128:(fc + 1) * 128], rhs=hT[:, kc, :],
                            start=(kc == 0), stop=(kc == 7)), r=[wupB, hTB], w=[bB[pb]])
                    rb = fc % 2
                    P.op(P.act, lambda pb=pb, rb=rb: nc.scalar.activation(out=rl[rb][:], in_=banks[pb][:, 0:TT],
                                                                      func=AF.Relu), r=[bB[pb]], w=[rlB[rb]])
                    P.op(P.dve, lambda fc=fc, rb=rb: nc.vector.tensor_tensor(out=hid[:, fc, :], in0=rl[rb][:],
                                                                           in1=rl[rb][:], op=ALU.mult),
                         r=[rlB[rb]], w=[hidB])
                for su in range(2):
                    for hf in range(2):
                        pb = 3 + (su * 2 + hf)
                        for fc in range(32):
                            P.op(P.pe, lambda fc=fc, su=su, hf=hf, pb=pb: nc.tensor.matmul(
                                banks[pb][:], lhsT=hid[:, fc, su * 128:(su + 1) * 128],
                                rhs=wdn[:, fc, hf * 512:(hf + 1) * 512], start=(fc == 0), stop=(fc == 31)),
                                r=[hidB, wdnB], w=[bB[pb]])
                        P.op(P.dve, lambda su=su, hf=hf, pb=pb, b=b: nc.vector.tensor_tensor(
                            out=ot[b][:, su, hf * 512:(hf + 1) * 512], in0=banks[pb][:],
                            in1=xt[b][:, su, hf * 512:(hf + 1) * 512], op=ALU.add), r=[bB[pb], xB[b]], w=[otB[b]])
                P.dma(DAP(dst, (s * T + i * TT) * D, [(D, 128), (128 * D, 2), (1, D)]), ot[b][:], r=[otB[b]],
                      sb=otB[b])
            P.barrier()
            P.flush()
            P.release(xB + otB + [wupB, wdnB, cB])


def _bf16(a):
    import ml_dtypes
    return np.asarray(a, dtype=np.float32).astype(ml_dtypes.bfloat16)


def make_consts(T):
    c = {}
    c["c_ident"] = _bf16(np.eye(128))
    return c


def _merge(self, l, s, src, dst):
    nc, P, T = self.nc, self.P, self.T
    TT = 512
    with ExitStack() as st:
        wg = self.sb(st, "wg", [128, 8, 4096], BF16)
        wbr = self.sb(st, "wbr", [128, 10, 1024], BF16)
        wo = self.sb(st, "wo", [128, 8, 1024], BF16)
        wB = Buf("w")
        ident = self.sb(st, "ident", [128, 128], BF16)
        eps_t = self.sb(st, "eps", [128, 1], F32)
        cB = Buf("c")
        xt = [self.sb(st, f"xt{i}", [128, 4, D], F32) for i in range(2)]
        xB = P.bufs(2, "xt")
        ot = [self.sb(st, f"oTt{i}", [128, 10, TT], BF16) for i in range(2)]
        oB = P.bufs(2, "oTt")
        hT = self.sb(st, "hT", [128, 8, TT], BF16)
        hTB = Buf("hT")
        mg = self.sb(st, "mg", [128, 8, TT], BF16)
        mgB = Buf("mg")
        junk = self.sb(st, "junk", [128, D], BF16)
        ss = self.sb(st, "ss", [128, 4], F32)
        hb = self.sb(st, "hb", [128, D], BF16)
        sg = [self.sb(st, f"sg{i}", [128, TT], F32) for i in range(2)]
        sgB = P.bufs(2, "sg")
        tt = [self.sb(st, f"tt{i}", [128, TT], F32) for i in range(2)]
        ttB = P.bufs(2, "tt")
        acc = self.sb(st, "acc", [128, TT], F32)
        accB = Buf("acc")
        banks = self.psum_banks(st)
        bB = P.bufs(8, "bank")
        tmp = (junk, Buf(), ss, Buf(), hb, Buf(), banks[7], bB[7], ident, eps_t, cB)
        P.dma(ident[:], self.c_ident.ap(), w=[cB], sb=cB)
        P.op(P.dve, lambda: nc.vector.memset(eps_t[:], EPS), w=[cB])
        for k0 in range(0, 8, 4):
            P.dma(wg[:, k0:k0 + 4, :], DAP(self.wb_in, (l * D + k0 * 128) * 7168 + 3072,
                                           [(7168, 128), (128 * 7168, 4), (1, 4096)]), w=[wB], sb=wB)
        P.dma(wbr[:], DAP(self.wb_br, l * 1280 * D, [(D, 128), (128 * D, 10), (1, D)]), w=[wB], sb=wB)
        P.dma(wo[:], DAP(self.wb_o, l * D * D, [(D, 128), (128 * D, 8), (1, D)]), w=[wB], sb=wB)
        ntile = T // TT
        brk = [(0, 2), (2, 6), (6, 8), (8, 10)]

        def load(i):
            b = i % 2
            P.dma(xt[b][:], DAP(src, (s * T + i * TT) * D, [(D, 128), (128 * D, 4), (1, D)]), w=[xB[b]], sb=xB[b])
            P.dma(ot[b][:], DAP(self.oT, i * TT, [(T, 128), (128 * T, 10), (1, TT)]), w=[oB[b]], sb=oB[b])

        load(0)
        k = 0
        for i in range(ntile):
            b = i % 2
            if i + 1 < ntile:
                load(i + 1)
            for su in range(4):
                self.norm_transpose(xt[b][:, su, :], xB[b], hT, hTB, su * 128, tmp)
            for dc in range(8):
                for br in range(4):
                    gb = k % 2
                    bb = 2 + k % 2
                    k += 1
                    for kc in range(8):
                        P.op(P.pe, lambda kc=kc, br=br, dc=dc, gb=gb: nc.tensor.matmul(
                            banks[gb][:], lhsT=wg[:, kc, br * 1024 + dc * 128: br * 1024 + (dc + 1) * 128],
                            rhs=hT[:, kc, :], start=(kc == 0), stop=(kc == 7)), r=[wB, hTB], w=[bB[gb]])
                    k0, k1 = brk[br]
                    for kc in range(k0, k1):
                        P.op(P.pe, lambda kc=kc, dc=dc, bb=bb, b=b, k0=k0, k1=k1: nc.tensor.matmul(
                            banks[bb][:], lhsT=wbr[:, kc, dc * 128:(dc + 1) * 128], rhs=ot[b][:, kc, :],
                            start=(kc == k0), stop=(kc == k1 - 1)), r=[wB, oB[b]], w=[bB[bb]])
                    P.op(P.act, lambda gb=gb: nc.scalar.activation(out=sg[gb][:], in_=banks[gb][:], func=AF.Sigmoid),
                         r=[bB[gb]], w=[sgB[gb]])
                    if br == 0:
                        P.op(P.dve, lambda gb=gb, bb=bb: nc.vector.tensor_tensor(
                            out=acc[:], in0=banks[bb][:], in1=sg[gb][:], op=ALU.mult),
                            r=[bB[bb], sgB[gb]], w=[accB])
                    else:
                        P.op(P.dve, lambda gb=gb, bb=bb: nc.vector.tensor_tensor(
                            out=tt[gb][:], in0=banks[bb][:], in1=sg[gb][:], op=ALU.mult),
                            r=[bB[bb], sgB[gb]], w=[ttB[gb]])
                        if br < 3:
                            P.op(P.pool, lambda gb=gb: nc.gpsimd.tensor_tensor(
                                out=acc[:], in0=acc[:], in1=tt[gb][:], op=ALU.add), r=[ttB[gb], accB], w=[accB])
                        else:
                            P.op(P.pool, lambda gb=gb, dc=dc: nc.gpsimd.tensor_tensor(
                                out=mg[:, dc, :], in0=acc[:], in1=tt[gb][:], op=ALU.add),
                                r=[ttB[gb], accB], w=[mgB])
            for su in range(4):
                for hf in range(2):
                    pb = 4 + (su * 2 + hf) % 3
                    for dc in range(8):
                        P.op(P.pe, lambda dc=dc, su=su, hf=hf, pb=pb: nc.tensor.matmul(
                            banks[pb][:], lhsT=mg[:, dc, su * 128:(su + 1) * 128],
                            rhs=wo[:, dc, hf * 512:(hf + 1) * 512], start=(dc == 0), stop=(dc == 7)),
                            r=[mgB, wB], w=[bB[pb]])
                    P.op(P.dve, lambda su=su, hf=hf, pb=pb, b=b: nc.vector.tensor_tensor(
                        out=xt[b][:, su, hf * 512:(hf + 1) * 512], in0=banks[pb][:],
                        in1=xt[b][:, su, hf * 512:(hf + 1) * 512], op=ALU.add), r=[bB[pb], xB[b]], w=[xB[b]])
            P.dma(DAP(dst, (s * T + i * TT) * D, [(D, 128), (128 * D, 4), (1, D)]), xt[b][:], r=[xB[b]], sb=xB[b])
        P.barrier()
        P.flush()
        P.release(xB + oB + [wB, cB])


Builder.merge = _merge


def _declare_mix(self):
    T = self.T
    self.c_cos = self.din("c_cos", [128, self.NT, 32])
    self.c_sin = self.din("c_sin", [128, self.NT, 32])
    self.c_rmask = self.din("c_rmask", [128, 512])
    self.zp = self.dscr("zp", [T, 256], BF16)
    self.qT = self.dscr("qT", [512, T], BF16)
    self.kT = self.dscr("kT", [128, T], BF16)
    self.gv = self.dscr("gv", [T, 128], BF16)
    self.nqkT = self.dscr("nqkT", [512, T], BF16)
    self.nv = self.dscr("nv", [T, 256], BF16)
    self.hv = self.dscr("hv", [T, 256], BF16)
    self.QT = self.dscr("QT", [2, 256, T], BF16)
    self.KT = self.dscr("KT", [2, 256, T], BF16)
    self.Kp = self.dscr("Kp", [2, T, 256], BF16)
    self.aT = self.dscr("aT", [2, 256, T // 32], F32)
    self.gateT = self.dscr("gateT", [256, T], BF16)
    self.ofw = self.dscr("ofw", [256, T], F32)
    self.obw = self.dscr("obw", [256, T], F32)


Builder.declare_mix = _declare_mix


def _proj(self, l, s, src):
    nc, P, T = self.nc, self.P, self.T
    TT = 512
    W = self.w
    with ExitStack() as st:
        w1 = self.sb(st, "w1", [128, 8, 3072], BF16)
        wB = Buf("w1")
        ident = self.sb(st, "ident", [128, 128], BF16)
        eps_t = self.sb(st, "eps", [128, 1], F32)
        cosT = self.sb(st, "cosT", [128, self.NT, 32], F32)
        sinT = self.sb(st, "sinT", [128, self.NT, 32], F32)
        rmask = self.sb(st, "rmask", [128, 512], F32)
        GQ = self.sb(st, "GQ", [128, 10, 64], F32)
        GN = self.sb(st, "GN", [128, 8, 64], F32)
        lbr = self.sb(st, "lbr", [128, 2, 2, 2], F32)
        lbv = self.sb(st, "lbv", [128, 2, 2], F32)
        oml = self.sb(st, "oml", [128, 2, 2], F32)
        ong = self.sb(st, "ong", [128, 1], F32)
        cB = Buf("c")
        xt = [self.sb(st, f"xt{i}", [128, 4, D], F32) for i in range(2)]
        xB = P.bufs(2, "xt")
        hT = self.sb(st, "hT", [128, 8, TT], BF16)
        hTB = Buf("hT")
        junk = self.sb(st, "junk", [128, D], BF16)
        ss = self.sb(st, "ss", [128, 4], F32)
        hb = self.sb(st, "hb", [128, D], BF16)
        sq = self.sb(st, "sq", [128, 640], F32)
        st8 = self.sb(st, "st8", [128, 3, 16], F32)
        xn = self.sb(st, "xn", [128, 640], F32)
        xg = self.sb(st, "xg", [128, 640], F32)
        t1 = self.sb(st, "t1", [128, 640], F32)
        t2 = self.sb(st, "t2", [128, 640], F32)
        qr = self.sb(st, "qr", [128, 640], BF16)
        sqB, st8B, xnB, xgB, t1B, t2B, qrB = P.bufs(7, "tm")
        qTs = self.sb(st, "qTs", [128, 5, TT], BF16)
        nTs = self.sb(st, "nTs", [128, 4, TT], BF16)
        zps = self.sb(st, "zps", [128, 4, 256], BF16)
        gvs = self.sb(st, "gvs", [128, 4, 128], BF16)
        s3 = self.sb(st, "s3", [128, 4, 512], BF16)
        qTsB, nTsB, zpsB, gvsB, s3B = P.bufs(5, "stg")
        hf = [self.sb(st, f"hf{i}", [128, TT], F32) for i in range(8)]
        hfB = P.bufs(8, "hf")
        hbf = [self.sb(st, f"hbf{i}", [128, TT], BF16) for i in range(4)]
        hbB = P.bufs(4, "hbf")
        a_s = self.sb(st, "a_s", [128, 16], F32)
        a_sB = Buf("a_s")
        kps = self.sb(st, "kps", [128, 4, 128], BF16)
        kpsB = Buf("kps")
        banks = self.psum_banks(st)
        bB = P.bufs(8, "bank")
        tmp = (junk, Buf(), ss, Buf(), hb, Buf(), banks[7], bB[7], ident, eps_t, cB)

        P.dma(ident[:], self.c_ident.ap(), w=[cB], sb=cB)
        P.dma(cosT[:], self.c_cos.ap(), w=[cB], sb=cB)
        P.dma(sinT[:], self.c_sin.ap(), w=[cB], sb=cB)
        P.dma(rmask[:], self.c_rmask.ap(), w=[cB], sb=cB)
        P.dma(GQ[:, 0:8, :], DAP(W["gqa_qnorm"], l * 64, [(0, 128), (0, 8), (1, 64)]), w=[cB], sb=cB)
        P.dma(GQ[:, 8:10, :], DAP(W["gqa_knorm"], l * 64, [(0, 128), (0, 2), (1, 64)]), w=[cB], sb=cB)
        P.dma(GN[:, 0:4, :], DAP(W["nat_qnorm"], l * 64, [(0, 128), (0, 4), (1, 64)]), w=[cB], sb=cB)
        P.dma(GN[:, 4:8, :], DAP(W["nat_knorm"], l * 64, [(0, 128), (0, 4), (1, 64)]), w=[cB], sb=cB)
        P.dma(lbr[:], DAP(W["hgrn_lb"], 0, [(1, 128), (512, 2), (256, 2), (128, 2)]), w=[cB], sb=cB, slow=True)
        for hh in range(2):
            P.dma(ong[hh * 64:(hh + 1) * 64, :], DAP(W["hgrn_onorm"], l * 64, [(1, 64), (1, 1)]), w=[cB], sb=cB,
                  slow=True)
        P.op(P.dve, lambda: nc.vector.memset(eps_t[:], EPS), w=[cB])
        P.op(P.dve, lambda: nc.vector.tensor_scalar(out=GQ[:, 0:8, :], in0=GQ[:, 0:8, :], scalar1=0.125,
                                                    scalar2=None, op0=ALU.mult), r=[cB], w=[cB])
        P.op(P.dve, lambda: nc.vector.tensor_scalar(out=GN[:, 0:4, :], in0=GN[:, 0:4, :], scalar1=0.125,
                                                    scalar2=None, op0=ALU.mult), r=[cB], w=[cB])
        if l == 0:
            P.op(P.dve, lambda: nc.vector.memset(lbv[:], 0.0), w=[cB])
            P.op(P.dve, lambda: nc.vector.memset(oml[:], 1.0), w=[cB])
        else:
            P.op(P.dve, lambda: nc.vector.tensor_tensor(out=oml[:], in0=lbr[:, 1, :, :], in1=lbr[:, 0, :, :],
                                                        op=ALU.subtract), r=[cB], w=[cB])
            P.op(P.act, lambda: nc.scalar.activation(out=lbv[:], in_=oml[:], func=AF.Sigmoid), r=[cB], w=[cB])
            P.op(P.dve, lambda: nc.vector.tensor_scalar(out=oml[:], in0=lbv[:], scalar1=-1.0, scalar2=1.0,
                                                        op0=ALU.mult, op1=ALU.add), r=[cB], w=[cB])
        for kc in range(8):
            P.dma(w1[:, kc, :], DAP(self.wb_in, (l * D + kc * 128) * 7168, [(7168, 128), (1, 3072)]), w=[wB], sb=wB)
        ntile = T // TT

        def load(i):
            b = i % 2
            P.dma(xt[b][:], DAP(src, (s * T + i * TT) * D, [(D, 128), (128 * D, 4), (1, D)]), w=[xB[b]], sb=xB[b])

        def mm_tok(bank, c0, c1, col0, ncol, su):
            for kc in range(8):
                P.op(P.pe, lambda kc=kc: nc.tensor.matmul(
                    banks[bank][:, c0:c1], lhsT=hT[:, kc, su * 128:(su + 1) * 128],
                    rhs=w1[:, kc, col0:col0 + ncol], start=(kc == 0), stop=(kc == 7)),
                    r=[wB, hTB], w=[bB[bank]])

        def headnorm(src_ap, srcB, H, gtab, o_off):
            n = H * 64
            P.op(P.act, lambda: nc.scalar.activation(out=sq[:, 0:n], in_=src_ap, func=AF.Square), r=[srcB], w=[sqB])
            P.op(P.dve, lambda: nc.vector.tensor_reduce(out=st8[:, 0, 0:H], in_=sq[:, 0:n].rearrange(
                "p (h d) -> p h d", h=H), op=ALU.add, axis=AX.X), r=[sqB], w=[st8B])
            P.op(P.act, lambda: nc.scalar.activation(out=st8[:, 1, 0:H], in_=st8[:, 0, 0:H], func=AF.Sqrt,
                                                     scale=1.0 / 64, bias=eps_t[:, 0:1]), r=[st8B, cB], w=[st8B])
            P.op(P.dve, lambda: nc.vector.reciprocal(out=st8[:, 2, 0:H], in_=st8[:, 1, 0:H]), r=[st8B], w=[st8B])
            P.op(P.dve, lambda: nc.vector.tensor_tensor(
                out=xn[:, 0:n].rearrange("p (h d) -> p h d", h=H), in0=src_ap.rearrange("p (h d) -> p h d", h=H),
                in1=st8[:, 2, 0:H].unsqueeze(2).broadcast_to([128, H, 64]), op=ALU.mult),
                r=[srcB, st8B], w=[xnB])
            P.op(P.pool, lambda: nc.gpsimd.tensor_tensor(
                out=xg[:, o_off * 64:(o_off + H) * 64], in0=xn[:, 0:n],
                in1=gtab.rearrange("p h d -> p (h d)"), op=ALU.mult), r=[xnB, cB], w=[xgB])

        load(0)
        for i in range(ntile):
            b = i % 2
            if i + 1 < ntile:
                load(i + 1)
            for su in range(4):
                self.norm_transpose(xt[b][:, su, :], xB[b], hT, hTB, su * 128, tmp)
            for su in range(4):
                nt = i * 4 + su
                mm_tok(0, 0, 512, 256, 512, su)
                mm_tok(1, 0, 256, 0, 256, su)
                mm_tok(1, 256, 512, 768, 256, su)
                mm_tok(2, 0, 512, 2304, 512, su)
                mm_tok(3, 0, 256, 1536, 256, su)
                mm_tok(3, 256, 512, 2816, 256, su)
                P.op(P.act, lambda su=su: nc.scalar.copy(out=zps[:, su, :], in_=banks[1][:, 0:256]),
                     r=[bB[1]], w=[zpsB])
                P.op(P.act, lambda su=su: nc.scalar.copy(out=gvs[:, su, :], in_=banks[1][:, 384:512]),
                     r=[bB[1]], w=[gvsB])
                P.op(P.act, lambda su=su: nc.scalar.copy(out=s3[:, su, :], in_=banks[3][:]), r=[bB[3]], w=[s3B])
                headnorm(banks[0][:, 0:512], bB[0], 8, GQ[:, 0:8, :], 0)
                headnorm(banks[1][:, 256:384], bB[1], 2, GQ[:, 8:10, :], 8)
                xg4 = xg[:, 0:640].rearrange("p (h t d) -> p h t d", h=10, t=2)
                t14 = t1[:, 0:640].rearrange("p (h t d) -> p h t d", h=10, t=2)
                t24 = t2[:, 0:640].rearrange("p (h t d) -> p h t d", h=10, t=2)
                qr4 = qr[:, 0:640].rearrange("p (h t d) -> p h t d", h=10, t=2)
                cb4 = cosT[:, nt, :].unsqueeze(1).unsqueeze(1).broadcast_to([128, 10, 2, 32])
                sb3 = sinT[:, nt, :].unsqueeze(1).broadcast_to([128, 10, 32])
                P.op(P.dve, lambda xg4=xg4, t14=t14, cb4=cb4: nc.vector.tensor_tensor(
                    out=t14, in0=xg4, in1=cb4, op=ALU.mult), r=[xgB, cB], w=[t1B])
                P.op(P.pool, lambda xg4=xg4, t24=t24, sb3=sb3: nc.gpsimd.tensor_tensor(
                    out=t24[:, :, 0, :], in0=xg4[:, :, 1, :], in1=sb3, op=ALU.mult), r=[xgB, cB], w=[t2B])
                P.op(P.pool, lambda xg4=xg4, t24=t24, sb3=sb3: nc.gpsimd.tensor_tensor(
                    out=t24[:, :, 1, :], in0=xg4[:, :, 0, :], in1=sb3, op=ALU.mult), r=[xgB, cB, t2B], w=[t2B])
                P.op(P.dve, lambda t14=t14, t24=t24, qr4=qr4: nc.vector.tensor_tensor(
                    out=qr4[:, :, 0, :], in0=t14[:, :, 0, :], in1=t24[:, :, 0, :], op=ALU.subtract),
                    r=[t1B, t2B], w=[qrB])
                P.op(P.dve, lambda t14=t14, t24=t24, qr4=qr4: nc.vector.tensor_tensor(
                    out=qr4[:, :, 1, :], in0=t14[:, :, 1, :], in1=t24[:, :, 1, :], op=ALU.add),
                    r=[t1B, t2B, qrB], w=[qrB])
                tp = banks[6][:].bitcast(BF16)
                for pr in range(5):
                    P.op(P.pe, lambda pr=pr, tp=tp: nc.tensor.transpose(
                        tp[:, pr * 128:(pr + 1) * 128], qr[:, pr * 128:(pr + 1) * 128], ident[:]),
                        r=[qrB, cB], w=[bB[6]])
                P.op(P.act, lambda su=su, tp=tp: nc.scalar.copy(
                    out=qTs[:, :, su * 128:(su + 1) * 128], in_=tp[:, 0:640].rearrange("p (c t) -> p c t", c=5)),
                    r=[bB[6]], w=[qTsB])
                headnorm(banks[2][:, 0:512], bB[2], 8, GN[:], 0)
                P.op(P.act, lambda: nc.scalar.copy(out=qr[:, 0:512], in_=xg[:, 0:512]), r=[xgB], w=[qrB])
                for pr in range(4):
                    P.op(P.pe, lambda pr=pr, tp=tp: nc.tensor.transpose(
                        tp[:, pr * 128:(pr + 1) * 128], qr[:, pr * 128:(pr + 1) * 128], ident[:]),
                        r=[qrB, cB], w=[bB[6]])
                P.op(P.act, lambda su=su, tp=tp: nc.scalar.copy(
                    out=nTs[:, :, su * 128:(su + 1) * 128], in_=tp[:, 0:512].rearrange("p (c t) -> p c t", c=4)),
                    r=[bB[6]], w=[nTsB])
            t0 = i * TT
            P.dma(DAP(self.zp, t0 * 256, [(256, 128), (128 * 256, 4), (1, 256)]), zps[:], r=[zpsB], sb=zpsB)
            P.dma(DAP(self.gv, t0 * 128, [(128, 128), (128 * 128, 4), (1, 128)]), gvs[:], r=[gvsB], sb=gvsB)
            P.dma(DAP(self.hv, t0 * 256, [(256, 128), (128 * 256, 4), (1, 256)]), s3[:, :, 0:256], r=[s3B], sb=s3B)
            P.dma(DAP(self.nv, t0 * 256, [(256, 128), (128 * 256, 4), (1, 256)]), s3[:, :, 256:512], r=[s3B], sb=s3B)
            P.dma(DAP(self.qT, t0, [(T, 128), (128 * T, 4), (1, TT)]), qTs[:, 0:4, :], r=[qTsB], sb=qTsB)
            P.dma(DAP(self.kT, t0, [(T, 128), (1, TT)]), qTs[:, 4, :], r=[qTsB], sb=qTsB)
            P.dma(DAP(self.nqkT, t0, [(T, 128), (128 * T, 4), (1, TT)]), nTs[:], r=[nTsB], sb=nTsB)
            def mm_feat(bank, col0):
                for kc in range(8):
                    P.op(P.pe, lambda kc=kc: nc.tensor.matmul(
                        banks[bank][:], lhsT=w1[:, kc, col0:col0 + 128], rhs=hT[:, kc, :],
                        start=(kc == 0), stop=(kc == 7)), r=[wB, hTB], w=[bB[bank]])

            for hp in range(2):
                silq, silqB = hf[0], hfB[0]
                mm_feat(4, 1792 + hp * 128)
                P.op(P.act, lambda: nc.scalar.activation(out=hf[0][:], in_=banks[4][:], func=AF.Silu),
                     r=[bB[4]], w=[hfB[0]])
                mm_feat(5, 2048 + hp * 128)
                P.op(P.act, lambda: nc.scalar.activation(out=hf[1][:], in_=banks[5][:], func=AF.Silu),
                     r=[bB[5]], w=[hfB[1]])
                P.op(P.dve, lambda: nc.vector.tensor_scalar(out=hbf[0][:], in0=hf[1][:], scalar1=ong[:, 0:1],
                                                            scalar2=None, op0=ALU.mult), r=[hfB[1], cB], w=[hbB[0]])
                P.dma(DAP(self.gateT, hp * 128 * T + t0, [(T, 128), (1, TT)]), hbf[0][:], r=[hbB[0]], sb=hbB[0])
                for d in range(2):
                    bk = 4 + d
                    mm_feat(bk, 1024 + d * 256 + hp * 128)
                    sig, f_, lg, ky, bb, eb, enb, kt = hf[1], hf[2], hf[3], hf[4], hf[5], hf[6], hf[7], hf[1]
                    P.op(P.act, lambda bk=bk: nc.scalar.activation(out=hf[1][:], in_=banks[bk][:], func=AF.Sigmoid),
                         r=[bB[bk]], w=[hfB[1]])
                    P.op(P.dve, lambda d=d, hp=hp: nc.vector.tensor_scalar(
                        out=hf[2][:], in0=hf[1][:], scalar1=oml[:, d, hp:hp + 1], scalar2=lbv[:, d, hp:hp + 1],
                        op0=ALU.mult, op1=ALU.add), r=[hfB[1], cB], w=[hfB[2]])
                    P.op(P.act, lambda: nc.scalar.activation(out=hf[3][:], in_=hf[2][:], func=AF.Ln),
                         r=[hfB[2]], w=[hfB[3]])
                    P.op(P.pool, lambda: nc.gpsimd.tensor_scalar(out=hf[4][:], in0=hf[2][:], scalar1=-1.0,
                                                                 scalar2=1.0, op0=ALU.mult, op1=ALU.add),
                         r=[hfB[2]], w=[hfB[4]])
                    P.op(P.dve, lambda: nc.vector.tensor_tensor_scan(
                        out=hf[5][:], data0=rmask[:], data1=hf[3][:], initial=0.0, op0=ALU.mult, op1=ALU.add),
                        r=[hfB[3], cB], w=[hfB[5]])
                    if d == 1:
                        P.op(P.pool, lambda: nc.gpsimd.tensor_tensor(out=hf[3][:], in0=hf[3][:], in1=hf[5][:],
                                                                     op=ALU.subtract), r=[hfB[3], hfB[5]], w=[hfB[3]])
                        P.op(P.dve, lambda: nc.vector.tensor_tensor(
                            out=hf[5][:].rearrange("p (n c) -> p n c", c=32),
                            in0=hf[3][:].rearrange("p (n c) -> p n c", c=32),
                            in1=hf[5][:].rearrange("p (n c) -> p n c", c=32)[:, :, 31:32].broadcast_to([128, 16, 32]),
                            op=ALU.add), r=[hfB[3], hfB[5]], w=[hfB[5]])
                    P.op(P.act, lambda: nc.scalar.activation(out=hf[6][:], in_=hf[5][:], func=AF.Exp),
                         r=[hfB[5]], w=[hfB[6]])
                    P.op(P.act, lambda: nc.scalar.activation(out=hf[7][:], in_=hf[5][:], func=AF.Exp, scale=-1.0),
                         r=[hfB[5]], w=[hfB[7]])
                    P.op(P.dve, lambda: nc.vector.tensor_tensor(out=hbf[1][:], in0=hf[0][:], in1=hf[6][:],
                                                                op=ALU.mult), r=[hfB[0], hfB[6]], w=[hbB[1]])
                    P.op(P.pool, lambda: nc.gpsimd.tensor_tensor(out=hf[1][:], in0=hf[4][:], in1=hf[7][:],
                                                                 op=ALU.mult), r=[hfB[4], hfB[7]], w=[hfB[1]])
                    P.op(P.act, lambda: nc.scalar.copy(out=hbf[2][:], in_=hf[1][:]), r=[hfB[1]], w=[hbB[2]])
                    aidx = 31 if d == 0 else 0
                    eb3 = hf[6][:].rearrange("p (n c) -> p n c", c=32)
                    P.op(P.dve, lambda eb3=eb3, aidx=aidx: nc.vector.tensor_tensor(
                        out=hbf[3][:].rearrange("p (n c) -> p n c", c=32),
                        in0=hf[1][:].rearrange("p (n c) -> p n c", c=32),
                        in1=eb3[:, :, aidx:aidx + 1].broadcast_to([128, 16, 32]), op=ALU.mult),
                        r=[hfB[1], hfB[6]], w=[hbB[3]])
                    P.op(P.act, lambda eb3=eb3, aidx=aidx: nc.scalar.copy(
                        out=a_s[:].unsqueeze(2), in_=eb3[:, :, aidx:aidx + 1]), r=[hfB[6]], w=[a_sB])
                    NCH = T // 32
                    P.dma(DAP(self.aT, (d * 256 + hp * 128) * NCH + i * 16, [(NCH, 128), (1, 16)]), a_s[:],
                          r=[a_sB], sb=a_sB)
                    P.dma(DAP(self.QT, (d * 256 + hp * 128) * T + t0, [(T, 128), (1, TT)]), hbf[1][:],
                          r=[hbB[1]], sb=hbB[1])
                    P.dma(DAP(self.KT, (d * 256 + hp * 128) * T + t0, [(T, 128), (1, TT)]), hbf[2][:],
                          r=[hbB[2]], sb=hbB[2])
                    tp = banks[6][:].bitcast(BF16)
                    for su in range(4):
                        P.op(P.pe, lambda su=su, tp=tp: nc.tensor.transpose(
                            tp[:, su * 128:(su + 1) * 128], hbf[3][:, su * 128:(su + 1) * 128], ident[:]),
                            r=[hbB[3], cB], w=[bB[6]])
                    P.op(P.act, lambda tp=tp: nc.scalar.copy(
                        out=kps[:], in_=tp[:, 0:512].rearrange("p (c t) -> p c t", c=4)), r=[bB[6]], w=[kpsB])
                    P.dma(DAP(self.Kp, (d * T + t0) * 256 + hp * 128, [(256, 128), (128 * 256, 4), (1, 128)]),
                          kps[:], r=[kpsB], sb=kpsB)
        P.barrier()
        P.flush()
        P.release(xB + [wB, cB, zpsB, gvsB, s3B, qTsB, nTsB, a_sB, kpsB] + hbB)


Builder.proj = _proj


def make_consts(T):
    NT = T // 128
    c = {}
    c["c_ident"] = _bf16(np.eye(128))
    pos = (np.arange(NT)[None, :] * 128 + np.arange(128)[:, None]).astype(np.float32)
    half = 32
    inv = (10000.0 ** (-np.arange(half, dtype=np.float32) / half)).astype(np.float32)
    ang = pos[:, :, None] * inv[None, None, :]
    c["c_cos"] = np.cos(ang).astype(np.float32)
    c["c_sin"] = np.sin(ang).astype(np.float32)
    rm = np.ones((128, 512), np.float32)
    rm[:, ::32] = 0.0
    c["c_rmask"] = rm
    return c


def _declare_p2(self):
    self.c_poolA = self.din("c_poolA", [128, 36, 128], BF16)
    self.c_gmask = self.din("c_gmask", [128, 2, 512], BF16)
    self.c_nmask = self.din("c_nmask", [128, 25, 128], BF16)
    self.c_identf = self.din("c_identf", [128, 128], F32)
    self.c_hmask = self.din("c_hmask", [128, 2, 512], BF16)
    self.c_cm = self.din("c_cm", [128, 4], BF16)
    self.c_bd = self.din("c_bd", [128, 128], BF16)
    self.rep2 = self.dscr("rep2", [60, 64, 128], F32)
    self.EtD = self.dscr("EtD", [DEPTH, 128, 12800], BF16)


Builder.declare_p2 = _declare_p2


def _pool(self, l):
    nc, P, T, NT = self.nc, self.P, self.T, self.NT
    TT = 512
    W = self.w
    with ExitStack() as st:
        A = self.sb(st, "A", [128, 36, 128], BF16)
        pwf = self.sb(st, "pwf", [128, 2, 128], F32)
        pwb = self.sb(st, "pwb", [128, 2, 128], BF16)
        psc = self.sb(st, "psc", [128, 2], F32)
        cB = Buf("c")
        zt = [self.sb(st, f"zt{i}", [128, 6, 256], BF16) for i in range(2)]
        zB = P.bufs(2, "zt")
        mx = self.sb(st, "mx", [128, 2, TT], BF16)
        mxB = Buf("mx")
        oas = [self.sb(st, f"oas{i}", [128, 2, TT], BF16) for i in range(2)]
        oaB = P.bufs(2, "oas")
        banks = self.psum_banks(st)
        bB = P.bufs(8, "bank")
        P.dma(A[:], self.c_poolA.ap(), w=[cB], sb=cB)
        P.op(P.dve, lambda: nc.vector.memset(pwf[:], 0.0), w=[cB])
        for g in range(4):
            gp, gg = g // 2, g % 2
            P.dma(pwf[gg * 64:(gg + 1) * 64, gp, gg * 64:(gg + 1) * 64],
                  DAP(W["pool_w"], (l * 4 + g) * 4096, [(64, 64), (1, 64)]), r=[cB], w=[cB], sb=cB)
        P.dma(psc[:], DAP(W["pool_scale"], l * 256, [(1, 128), (128, 2)]), w=[cB], sb=cB, slow=True)
        P.op(P.dve, lambda: nc.vector.tensor_copy(out=pwb[:], in_=pwf[:]), r=[cB], w=[cB])
        ntile = T // TT

        def load(i):
            b = i % 2
            n0 = i * 4 - 1
            lo = max(n0, 0)
            hi = min(n0 + 6, NT)
            P.dma(zt[b][:, lo - n0:hi - n0, :], DAP(self.zp, lo * 128 * 256, [(256, 128), (128 * 256, hi - lo), (1, 256)]),
                  w=[zB[b]], sb=zB[b])

        load(0)
        for i in range(ntile):
            b = i % 2
            if i + 1 < ntile:
                load(i + 1)
            for gp in range(2):
                for su in range(4):
                    n = i * 4 + su
                    cls = 0 if n == 0 else (2 if n == NT - 1 else 1)
                    for gg in range(2):
                        g = gp * 2 + gg
                        js = [j for j in (-1, 0, 1) if 0 <= n + j < NT]
                        for j in js:
                            P.op(P.pe, lambda gp=gp, su=su, gg=gg, g=g, j=j, cls=cls, b=b, js=js: nc.tensor.matmul(
                                banks[gp][gg * 64:(gg + 1) * 64, su * 128:(su + 1) * 128],
                                lhsT=zt[b][:, su + 1 + j, g * 64:(g + 1) * 64], rhs=A[:, cls * 12 + g * 3 + j + 1, :],
                                start=(j == js[0]), stop=(j == js[-1])), r=[zB[b], cB], w=[bB[gp]])
                P.op(P.act, lambda gp=gp: nc.scalar.copy(out=mx[:, gp, :], in_=banks[gp][:]), r=[bB[gp]], w=[mxB])
                P.op(P.pe, lambda gp=gp: nc.tensor.matmul(banks[2 + gp][:], lhsT=pwb[:, gp, :], rhs=mx[:, gp, :],
                                                          start=True, stop=True), r=[mxB, cB], w=[bB[2 + gp]])
                P.op(P.act, lambda gp=gp, b=b: nc.scalar.activation(out=oas[b][:, gp, :], in_=banks[2 + gp][:],
                                                                   func=AF.Copy, scale=psc[:, gp:gp + 1]),
                     r=[bB[2 + gp], cB], w=[oaB[b]])
            P.dma(DAP(self.oT, i * TT, [(T, 128), (128 * T, 2), (1, TT)]), oas[b][:], r=[oaB[b]], sb=oaB[b])
        P.barrier()
        P.flush()
        P.release(zB + oaB + [cB])


Builder.pool = _pool


def _gqa(self, l):
    nc, P, T, NT = self.nc, self.P, self.T, self.NT
    TT = 512
    W = self.w
    with ExitStack() as st:
        ident = self.sb(st, "ident", [128, 128], BF16)
        gmask = self.sb(st, "gmask", [128, 2, 512], BF16)
        ones = self.sb(st, "ones", [128, 64], BF16)
        skf = self.sb(st, "skf", [1, 8], F32)
        esr = self.sb(st, "esr", [1, 8, 128], BF16)
        cB = Buf("c")
        qt = [self.sb(st, f"qt{i}", [64, 8, TT], BF16) for i in range(2)]
        kt = [self.sb(st, f"kt{i}", [64, 2, 6 * 128], BF16) for i in range(2)]
        vt = [self.sb(st, f"vt{i}", [128, 6, 128], BF16) for i in range(2)]
        qB, kB, vB = P.bufs(2, "q"), P.bufs(2, "k"), P.bufs(2, "v")
        NP = 6
        pt = [self.sb(st, f"pt{i}", [128, 512], BF16) for i in range(NP)]
        ptB = P.bufs(NP, "pt")
        rec = [self.sb(st, f"rec{i}", [64, 512], F32) for i in range(2)]
        recB = P.bufs(2, "rec")
        obs = [self.sb(st, f"obs{i}", [64, 8, TT], BF16) for i in range(2)]
        obB = P.bufs(2, "obs")
        banks = self.psum_banks(st)
        bB = P.bufs(8, "bank")
        P.dma(ident[:], self.c_ident.ap(), w=[cB], sb=cB)
        P.dma(gmask[:], self.c_gmask.ap(), w=[cB], sb=cB)
        P.dma(skf[:], DAP(W["gqa_sink"], l * 8, [(8, 1), (1, 8)]), w=[cB], sb=cB)
        P.op(P.dve, lambda: nc.vector.memset(ones[:], 1.0), w=[cB])
        P.op(P.act, lambda: nc.scalar.activation(out=skf[:], in_=skf[:], func=AF.Exp), r=[cB], w=[cB])
        P.op(P.dve, lambda: nc.vector.tensor_copy(out=esr[:], in_=skf[:].unsqueeze(2).broadcast_to([1, 8, 128])),
             r=[cB], w=[cB])
        ntile = T // TT

        def load(i):
            b = i % 2
            n0 = i * 4 - 1
            lo, hi = max(n0, 0), min(n0 + 6, NT)
            P.dma(qt[b][:], DAP(self.qT, i * TT, [(T, 64), (64 * T, 8), (1, TT)]), w=[qB[b]], sb=qB[b])
            P.dma(kt[b][:, :, (lo - n0) * 128:(hi - n0) * 128],
                  DAP(self.kT, lo * 128, [(T, 64), (64 * T, 2), (1, (hi - lo) * 128)]), w=[kB[b]], sb=kB[b])
            P.dma(vt[b][:, lo - n0:hi - n0, :], DAP(self.gv, lo * 128 * 128, [(128, 128), (128 * 128, hi - lo), (1, 128)]),
                  w=[vB[b]], sb=vB[b])

        load(0)
        cnt = 0
        pcnt = 0
        for i in range(ntile):
            b = i % 2
            if i + 1 < ntile:
                load(i + 1)
            for su in range(4):
                n = i * 4 + su
                js = [j for j in (-1, 0, 1) if 0 <= n + j < NT]
                for g in range(2):
                    pts = []
                    for j in js:
                        slot = su + 1 + j
                        sbk = pcnt % 4
                        pi = pcnt % NP
                        pcnt += 1
                        P.op(P.pe, lambda g=g, slot=slot, sbk=sbk, b=b, su=su, j=j: nc.tensor.matmul(
                            banks[sbk][:], lhsT=kt[b][:, g, slot * 128:(slot + 1) * 128],
                            rhs=qt[b][:, g * 4:(g + 1) * 4, su * 128:(su + 1) * 128], start=True, stop=(j == 0)),
                            r=[kB[b], qB[b]], w=[bB[sbk]])
                        if j != 0:
                            P.op(P.pe, lambda sbk=sbk, j=j: nc.tensor.matmul(
                                banks[sbk][:], lhsT=ident[:], rhs=gmask[:, 0 if j < 0 else 1, :], start=False, stop=True),
                                r=[cB], w=[bB[sbk]])
                        P.op(P.act, lambda sbk=sbk, pi=pi: nc.scalar.activation(out=pt[pi][:], in_=banks[sbk][:],
                                                                              func=AF.Exp), r=[bB[sbk]], w=[ptB[pi]])
                        pts.append((slot, pi))
                    ob = 4 + (cnt % 2) * 2
                    rb = cnt % 2
                    cnt += 1
                    for idx, (slot, pi) in enumerate(pts):
                        P.op(P.pe, lambda ob=ob, slot=slot, pi=pi, g=g, b=b, idx=idx, npt=len(pts): nc.tensor.matmul(
                            banks[ob][0:64, :], lhsT=vt[b][:, slot, g * 64:(g + 1) * 64], rhs=pt[pi][:],
                            start=(idx == 0), stop=(idx == npt - 1)), r=[vB[b], ptB[pi]], w=[bB[ob]])
                    for idx, (slot, pi) in enumerate(pts):
                        P.op(P.pe, lambda ob=ob, pi=pi, idx=idx: nc.tensor.matmul(
                            banks[ob + 1][0:64, :], lhsT=ones[:, 0:64], rhs=pt[pi][:], start=(idx == 0), stop=False),
                            r=[cB, ptB[pi]], w=[bB[ob + 1]])
                    P.op(P.pe, lambda ob=ob, g=g: nc.tensor.matmul(
                        banks[ob + 1][0:64, :], lhsT=ones[0:1, 0:64], rhs=esr[0:1, g * 4:(g + 1) * 4, :],
                        start=False, stop=True), r=[cB], w=[bB[ob + 1]])
                    P.op(P.act, lambda ob=ob, rb=rb: nc.scalar.activation(out=rec[rb][:], in_=banks[ob + 1][0:64, :],
                                                                      func=AF.Ln), r=[bB[ob + 1]], w=[recB[rb]])
                    P.op(P.act, lambda rb=rb: nc.scalar.activation(out=rec[rb][:], in_=rec[rb][:], func=AF.Exp,
                                                               scale=-1.0), r=[recB[rb]], w=[recB[rb]])
                    P.op(P.dve, lambda ob=ob, rb=rb, g=g, su=su, b=b: nc.vector.tensor_tensor(
                        out=obs[b][:, g * 4:(g + 1) * 4, su * 128:(su + 1) * 128],
                        in0=banks[ob][0:64, :].rearrange("p (h t) -> p h t", h=4),
                        in1=rec[rb][:].rearrange("p (h t) -> p h t", h=4), op=ALU.mult),
                        r=[bB[ob], recB[rb]], w=[obB[b]])
            P.dma(DAP(self.oT, 256 * T + i * TT, [(T, 64), (64 * T, 8), (1, TT)]), obs[b][:], r=[obB[b]], sb=obB[b])
        P.barrier()
        P.flush()
        P.release(qB + kB + vB + obB + [cB])


Builder.gqa = _gqa


def _nat(self, l, s=0):
    nc, P, T, NT = self.nc, self.P, self.T, self.NT
    W = self.w
    build = (s == 0)
    with ExitStack() as st:
        ident = self.sb(st, "ident", [128, 128], BF16)
        identf = self.sb(st, "identf", [128, 128], F32)
        nmask = self.sb(st, "nmask", [128, 25, 128], BF16)
        ones = self.sb(st, "ones", [128, 64], BF16)
        R = self.sb(st, "R", [60, 128], F32)
        Tb = self.sb(st, "Tb", [128, 9, 4, 128], BF16)
        bt = [self.sb(st, f"bt{i}", [128, 128], F32) for i in range(2)]
        btB = P.bufs(2, "bt")
        cB, rB, tbB = Buf("c"), Buf("R"), Buf("Tb")
        Et = self.sb(st, "Et", [128, 25, 4, 128], BF16)
        etB = Buf("Et")
        etmp = [self.sb(st, f"etmp{i}", [128, 512], F32) for i in range(2)]
        etmpB = P.bufs(2, "etmp")
        pe_ = [self.sb(st, f"pe{i}", [128, 512], BF16) for i in range(4)]
        peB = P.bufs(4, "pe")
        NB = 2
        qt = [self.sb(st, f"qt{i}", [64, 4, 512], BF16) for i in range(NB)]
        kt = [self.sb(st, f"kt{i}", [64, 4, 1024], BF16) for i in range(NB)]
        vt = [self.sb(st, f"vt{i}", [128, 8, 256], BF16) for i in range(NB)]
        qB, kB, vB = P.bufs(NB, "q"), P.bufs(NB, "k"), P.bufs(NB, "v")
        NP = 10
        pt = [self.sb(st, f"pt{i}", [128, 512], BF16) for i in range(NP)]
        ptB = P.bufs(NP, "pt")
        rec = [self.sb(st, f"rec{i}", [64, 512], F32) for i in range(2)]
        recB = P.bufs(2, "rec")
        ods = [self.sb(st, f"ods{i}", [64, 4, 512], BF16) for i in range(2)]
        odB = P.bufs(2, "ods")
        banks = self.psum_banks(st)
        bB = P.bufs(8, "bank")
        P.dma(ident[:], self.c_ident.ap(), w=[cB], sb=cB)
        P.dma(identf[:], self.c_identf.ap(), w=[cB], sb=cB)
        P.dma(nmask[:], self.c_nmask.ap(), w=[cB], sb=cB)
        P.op(P.dve, lambda: nc.vector.memset(ones[:], 1.0), w=[cB])
        if not build:
            P.dma(Et[:].rearrange("p a h t -> p (a h t)"), DAP(self.EtD, l * 128 * 12800, [(12800, 128), (1, 12800)]),
                  w=[etB], sb=etB)
        P.op(P.dve, lambda: nc.vector.memset(R[:], 0.0), w=[rB])
        if build:
            P.dma(R[:, 48:79], DAP(W["nat_rpb"], l * 60 * 31, [(31, 60), (1, 31)]), r=[rB], w=[rB], sb=rB)
            P.dma(self.rep2.ap(), R[:].unsqueeze(1).broadcast_to([60, 64, 128]), r=[rB], sb=rB)
            P.barrier()
            k = 0
            for dl in range(-4, 5):
                for h in range(4):
                    i2 = k % 2
                    k += 1
                    P.op(P.dve, lambda i2=i2: nc.vector.memset(bt[i2][:], 0.0), w=[btB[i2]])
                    for qr in range(2):
                        for kr in range(2):
                            dr = 2 * dl + kr - qr + 7
                            if 0 <= dr < 15:
                                P.dma(bt[i2][qr * 64:(qr + 1) * 64, kr * 64:(kr + 1) * 64],
                                      DAP(self.rep2, (h * 15 + dr) * 8192 + 63, [(127, 64), (1, 64)]),
                                      r=[btB[i2]], w=[btB[i2]], sb=btB[i2])
                    P.op(P.pe, lambda i2=i2: nc.tensor.transpose(banks[7][:, 0:128], bt[i2][:], identf[:]),
                         r=[btB[i2], cB], w=[bB[7]])
                    P.op(P.act, lambda dl=dl, h=h: nc.scalar.copy(out=Tb[:, dl + 4, h, :], in_=banks[7][:, 0:128]),
                         r=[bB[7]], w=[tbB])

            reps = {0: 0, 1: 1, 2: 2, 3: NT - 2, 4: NT - 1}
            for cls in range(5):
                m_ = reps[cls]
                kt0_ = min(max(m_ - 2, 0), NT - 5)
                for slot in range(5):
                    dl = kt0_ + slot - m_
                    e2 = (cls * 5 + slot) % 2
                    P.op(P.dve, lambda dl=dl, cls=cls, slot=slot, e2=e2: nc.vector.tensor_tensor(
                        out=etmp[e2][:].rearrange("p (h t) -> p h t", h=4), in0=Tb[:, dl + 4, :, :],
                        in1=nmask[:, cls * 5 + slot, :].unsqueeze(1).broadcast_to([128, 4, 128]), op=ALU.add),
                        r=[tbB, cB], w=[etmpB[e2]])
                    P.op(P.act, lambda cls=cls, slot=slot, e2=e2: nc.scalar.activation(
                        out=Et[:, cls * 5 + slot, :, :].rearrange("p h t -> p (h t)"), in_=etmp[e2][:], func=AF.Exp),
                        r=[etmpB[e2]], w=[etB])
            P.dma(DAP(self.EtD, l * 128 * 12800, [(12800, 128), (1, 12800)]), Et[:].rearrange("p a h t -> p (a h t)"),
                  r=[etB], sb=etB)

        kt0f = lambda m: min(max(m - 2, 0), NT - 5)

        def load(g):
            b = g % NB
            lo = kt0f(4 * g)
            hi = kt0f(4 * g + 3) + 5
            nk = hi - lo
            P.dma(qt[b][:], DAP(self.nqkT, g * 512, [(T, 64), (64 * T, 4), (1, 512)]), w=[qB[b]], sb=qB[b])
            P.dma(kt[b][:, :, 0:nk * 128], DAP(self.nqkT, 256 * T + lo * 128, [(T, 64), (64 * T, 4), (1, nk * 128)]),
                  w=[kB[b]], sb=kB[b])
            P.dma(vt[b][:, 0:nk, :], DAP(self.nv, lo * 128 * 256, [(256, 128), (128 * 256, nk), (1, 256)]),
                  w=[vB[b]], sb=vB[b])

        assert NT % 4 == 0
        load(0)
        pcnt = 0
        for m in range(NT):
            g = m // 4
            b = g % NB
            lq = m % 4
            if lq == 1 and g + 1 < NT // 4:
                load(g + 1)
            kt0 = kt0f(m)
            ko = kt0 - kt0f(4 * g)
            cls = 0 if m == 0 else 1 if m == 1 else 3 if m == NT - 2 else 4 if m == NT - 1 else 2
            pis = []
            for slot in range(5):
                dl = kt0 + slot - m
                sbk = pcnt % 5
                pi = pcnt % NP
                pcnt += 1
                for h in range(4):
                    P.op(P.pe, lambda h=h, slot=slot, sbk=sbk, b=b, ko=ko, lq=lq: nc.tensor.matmul(
                        banks[sbk][:, h * 128:(h + 1) * 128], lhsT=kt[b][:, h, (ko + slot) * 128:(ko + slot + 1) * 128],
                        rhs=qt[b][:, h, lq * 128:(lq + 1) * 128], start=True, stop=True), r=[kB[b], qB[b]], w=[bB[sbk]])
                p4 = pcnt % 4
                P.op(P.act, lambda sbk=sbk, p4=p4: nc.scalar.activation(out=pe_[p4][:], in_=banks[sbk][:], func=AF.Exp),
                     r=[bB[sbk]], w=[peB[p4]])
                P.op(P.dve, lambda pi=pi, p4=p4, cls=cls, slot=slot: nc.vector.tensor_tensor(
                    out=pt[pi][:], in0=pe_[p4][:], in1=Et[:, cls * 5 + slot, :, :].rearrange("p h t -> p (h t)"),
                    op=ALU.mult), r=[peB[p4], etB], w=[ptB[pi]])
                pis.append(pi)
            rb = m % 2
            for h in range(4):
                for slot in range(5):
                    P.op(P.pe, lambda h=h, slot=slot, b=b, pi=pis[slot], ko=ko: nc.tensor.matmul(
                        banks[5][0:64, h * 128:(h + 1) * 128], lhsT=vt[b][:, ko + slot, h * 64:(h + 1) * 64],
                        rhs=pt[pi][:, h * 128:(h + 1) * 128], start=(slot == 0), stop=(slot == 4)),
                        r=[vB[b], ptB[pis[slot]]], w=[bB[5]])
            for slot in range(5):
                P.op(P.pe, lambda slot=slot, pi=pis[slot]: nc.tensor.matmul(
                    banks[6][0:64, :], lhsT=ones[:, 0:64], rhs=pt[pi][:], start=(slot == 0), stop=(slot == 4)),
                    r=[cB, ptB[pis[slot]]], w=[bB[6]])
            P.op(P.act, lambda rb=rb: nc.scalar.activation(out=rec[rb][:], in_=banks[6][0:64, :], func=AF.Ln),
                 r=[bB[6]], w=[recB[rb]])
            P.op(P.act, lambda rb=rb: nc.scalar.activation(out=rec[rb][:], in_=rec[rb][:], func=AF.Exp, scale=-1.0),
                 r=[recB[rb]], w=[recB[rb]])
            P.op(P.dve, lambda rb=rb, g=g, lq=lq: nc.vector.tensor_tensor(
                out=ods[g % 2][:, :, lq * 128:(lq + 1) * 128], in0=banks[5][0:64, :].rearrange("p (h t) -> p h t", h=4),
                in1=rec[rb][:].rearrange("p (h t) -> p h t", h=4), op=ALU.mult),
                r=[bB[5], recB[rb]], w=[odB[g % 2]])
            if lq == 3:
                P.dma(DAP(self.oT, 1024 * T + g * 512, [(T, 64), (64 * T, 4), (1, 512)]), ods[g % 2][:],
                      r=[odB[g % 2]], sb=odB[g % 2])
        P.barrier()
        P.flush()
        P.release(qB + kB + vB + odB + btB + [cB, rB])


Builder.nat = _nat


def _nat_masks(T):
    NT = T // 128
    rows = T // 64
    kr = 8
    out = np.full((25, 128, 128), NEGM, np.float32)
    reps = {0: 0, 1: 1, 2: 2, 3: NT - 2, 4: NT - 1}
    for cls, m in reps.items():
        kt0 = min(max(m - 2, 0), NT - 5)
        for slot in range(5):
            ktile = kt0 + slot
            for qr in range(2):
                r = 2 * m + qr
                row0 = min(max(r - kr // 2, 0), rows - kr)
                for krr in range(2):
                    krow = 2 * ktile + krr
                    if not (row0 <= krow < row0 + kr):
                        continue
                    for qc in range(64):
                        qc0 = min(max(qc - 8, 0), 64 - 16)
                        out[cls * 5 + slot, krr * 64 + qc0: krr * 64 + qc0 + 16, qr * 64 + qc] = 0.0
    return out


def _pool_A(T):
    out = np.zeros((3, 4, 3, 128, 128), np.float32)
    for cls in range(3):
        for g, w in enumerate((2, 4, 8, 16)):
            for t in range(128):
                lo, hi = t - w // 2, t + w // 2
                if cls == 0:
                    lo = max(lo, 0)
                if cls == 2:
                    hi = min(hi, 128)
                cnt = hi - lo
                for srel in range(lo, hi):
                    j = srel // 128
                    out[cls, g, j + 1, srel - j * 128, t] += 1.0 / cnt
                out[cls, g, 1, t, t] -= 1.0
    return out.reshape(36, 128, 128)


_make_consts_p1 = make_consts


def make_consts(T):
    c = _make_consts_p1(T)
    c["c_poolA"] = _bf16(_pool_A(T).transpose(1, 0, 2))
    gm = np.zeros((2, 128, 4, 128), np.float32)
    jj = np.arange(128)[:, None]
    ii = np.arange(128)[None, :]
    gm[0] = np.where(jj >= ii, 0.0, NEGM)[:, None, :]
    gm[1] = np.where(jj <= ii, 0.0, NEGM)[:, None, :]
    c["c_gmask"] = _bf16(gm.reshape(2, 128, 512).transpose(1, 0, 2))
    c["c_nmask"] = _bf16(_nat_masks(T).transpose(1, 0, 2))
    c["c_identf"] = np.eye(128, dtype=np.float32)
    s = np.arange(128)[:, None]
    t = np.arange(128)[None, :]
    same = (s // 32) == (t // 32)
    hm = np.stack([np.tile((same & (s <= t)).astype(np.float32), (1, 4)),
                   np.tile((same & (s >= t)).astype(np.float32), (1, 4))], 0)
    c["c_hmask"] = _bf16(hm.transpose(1, 0, 2))
    c["c_cm"] = _bf16((np.arange(128)[:, None] // 32 == np.arange(4)[None, :]).astype(np.float32))
    bd = np.zeros((128, 128), np.float32)
    bd[:64, :64] = 1
    bd[64:, 64:] = 1
    c["c_bd"] = _bf16(bd)
    return c


def _hgrn(self, l):
    nc, P, T, NT = self.nc, self.P, self.T, self.NT
    NCH = T // 32
    with ExitStack() as st:
        hmask = self.sb(st, "hmask", [128, 2, 512], BF16)
        cm = self.sb(st, "cm", [128, 4], BF16)
        ones = self.sb(st, "ones", [128, 64], BF16)
        eps_t = self.sb(st, "eps", [128, 1], F32)
        cB = Buf("c")
        NB = 4
        qt = [self.sb(st, f"qt{i}", [64, 4, 128], BF16) for i in range(NB)]
        kt = [self.sb(st, f"kt{i}", [64, 4, 128], BF16) for i in range(NB)]
        kp = [self.sb(st, f"kp{i}", [128, 256], BF16) for i in range(NB)]
        vt = [self.sb(st, f"vt{i}", [128, 256], BF16) for i in range(NB)]
        at = [self.sb(st, f"at{i}", [64, 4, 4], F32) for i in range(NB)]
        of = [self.sb(st, f"of{i}", [64, 4, 128], F32) for i in range(NB)]
        gt = [self.sb(st, f"gt{i}", [64, 4, 128], BF16) for i in range(NB)]
        ldB = P.bufs(NB, "ld")
        kpm = [self.sb(st, f"kpm{i}", [128, 4, 256], BF16) for i in range(2)]
        kpmB = P.bufs(2, "kpm")
        Sf = [self.sb(st, f"Sf{j}", [64, 4, 64], F32) for j in range(2)]
        SfB = P.bufs(2, "Sf")
        SfhB = [P.bufs(4, f"Sfh{j}") for j in range(2)]
        stmp = self.sb(st, "stmp", [64, 4, 64], F32)
        stB = Buf("stmp")
        sthB = P.bufs(4, "sth")
        Sb = [self.sb(st, f"Sb{i}", [64, 4, 4, 64], BF16) for i in range(2)]
        SbB = P.bufs(2, "Sb")
        Am = [self.sb(st, f"Am{i}", [128, 4, 128], BF16) for i in range(2)]
        AmB = P.bufs(2, "Am")
        osm = [self.sb(st, f"osm{i}", [64, 512], F32) for i in range(2)]
        osB = P.bufs(2, "osm")
        sqb = self.sb(st, "sqb", [64, 512], BF16)
        sqB = Buf("sqb")
        rr = self.sb(st, "rr", [64, 512], F32)
        rrB = Buf("rr")
        rr2 = self.sb(st, "rr2", [64, 512], F32)
        rr2B = Buf("rr2")
        ocs = [self.sb(st, f"ocs{i}", [64, 4, 128], BF16) for i in range(2)]
        ocB = P.bufs(2, "ocs")
        banks = self.psum_banks(st)
        bB = P.bufs(8, "bank")
        P.dma(hmask[:], self.c_hmask.ap(), w=[cB], sb=cB)
        P.dma(cm[:], self.c_cm.ap(), w=[cB], sb=cB)
        P.op(P.dve, lambda: nc.vector.memset(ones[:], 1.0), w=[cB])
        P.op(P.dve, lambda: nc.vector.memset(eps_t[:], EPS), w=[cB])

        for d in range(2):
            order = list(range(NT)) if d == 0 else list(range(NT - 1, -1, -1))
            corder = [0, 1, 2, 3] if d == 0 else [3, 2, 1, 0]
            cur = [0]
            P.op(P.dve, lambda: nc.vector.memset(Sf[0][:], 0.0), w=SfhB[0])

            def load(step):
                n = order[step]
                b = step % NB
                t0 = n * 128
                hd = [(T, 64), (64 * T, 4), (1, 128)]
                P.dma(qt[b][:], DAP(self.QT, d * 256 * T + t0, hd), w=[ldB[b]], sb=ldB[b])
                P.dma(kt[b][:], DAP(self.KT, d * 256 * T + t0, hd), w=[ldB[b]], sb=ldB[b])
                P.dma(kp[b][:], DAP(self.Kp, (d * T + t0) * 256, [(256, 128), (1, 256)]), w=[ldB[b]], sb=ldB[b])
                P.dma(vt[b][:], DAP(self.hv, t0 * 256, [(256, 128), (1, 256)]), w=[ldB[b]], sb=ldB[b])
                P.dma(at[b][:], DAP(self.aT, d * 256 * NCH + n * 4, [(NCH, 64), (64 * NCH, 4), (1, 4)]),
                      w=[ldB[b]], sb=ldB[b])
                if d == 1:
                    P.dma(of[b][:], DAP(self.ofw, t0, hd), w=[ldB[b]], sb=ldB[b])
                    P.dma(gt[b][:], DAP(self.gateT, t0, hd), w=[ldB[b]], sb=ldB[b])

            def stage1(step):
                b = step % NB
                p2 = step % 2
                ub = [0, 1] if p2 == 0 else [2, 3]
                ab = 4
                P.op(P.pool, lambda: nc.gpsimd.tensor_tensor(
                    out=kpm[p2][:], in0=kp[b][:].unsqueeze(1).broadcast_to([128, 4, 256]),
                    in1=cm[:].unsqueeze(2).broadcast_to([128, 4, 256]), op=ALU.mult), r=[ldB[b], cB], w=[kpmB[p2]])
                for c in range(4):
                    for h in range(4):
                        col = ((c % 2) * 4 + h) * 64
                        P.op(P.pe, lambda c=c, h=h, col=col: nc.tensor.matmul(
                            banks[ub[c // 2]][0:64, col:col + 64], lhsT=kpm[p2][:, c, h * 64:(h + 1) * 64],
                            rhs=vt[b][:, h * 64:(h + 1) * 64], start=True, stop=True),
                            r=[kpmB[p2], ldB[b]], w=[bB[ub[c // 2]]])
                for h in range(4):
                    P.op(P.pe, lambda h=h: nc.tensor.matmul(
                        banks[ab][:, h * 128:(h + 1) * 128], lhsT=kt[b][:, h, :], rhs=qt[b][:, h, :],
                        start=True, stop=True), r=[ldB[b]], w=[bB[ab]])
                P.op(P.dve, lambda: nc.vector.tensor_tensor(
                    out=Am[p2][:].rearrange("p h t -> p (h t)"), in0=banks[ab][:], in1=hmask[:, d, :], op=ALU.mult),
                    r=[bB[ab], cB], w=[AmB[p2]])
                for c in corder:
                    cu = cur[0]
                    nx = 1 - cu
                    P.op(P.act, lambda c=c, cu=cu: nc.scalar.copy(out=Sb[p2][:, c, :, :], in_=Sf[cu][:]),
                         r=SfhB[cu], w=[SbB[p2]])
                    ucol = (c % 2) * 256
                    for h in range(4):
                        P.op(P.dve, lambda c=c, cu=cu, h=h: nc.vector.tensor_tensor(
                            out=stmp[:, h, :], in0=Sf[cu][:, h, :], in1=at[b][:, h, c:c + 1].broadcast_to([64, 64]),
                            op=ALU.mult), r=[SfhB[cu][h], ldB[b]], w=[sthB[h]])
                    for h in range(4):
                        P.op(P.dve, lambda c=c, nx=nx, ucol=ucol, h=h: nc.vector.tensor_tensor(
                            out=Sf[nx][:, h, :], in0=banks[ub[c // 2]][0:64, ucol + h * 64: ucol + (h + 1) * 64],
                            in1=stmp[:, h, :], op=ALU.add), r=[sthB[h], bB[ub[c // 2]]], w=[SfhB[nx][h]])
                    cur[0] = nx

            def stage2(step):
                n = order[step]
                b = step % NB
                p2 = step % 2
                ob = 6 + p2
                for h in range(4):
                    P.op(P.pe, lambda h=h: nc.tensor.matmul(
                        banks[ob][0:64, h * 128:(h + 1) * 128], lhsT=vt[b][:, h * 64:(h + 1) * 64],
                        rhs=Am[p2][:, h, :], start=True, stop=False), r=[ldB[b], AmB[p2]], w=[bB[ob]])
                    for c in range(4):
                        P.op(P.pe, lambda h=h, c=c: nc.tensor.matmul(
                            banks[ob][0:64, h * 128 + c * 32: h * 128 + (c + 1) * 32],
                            lhsT=Sb[p2][:, c, h, :], rhs=qt[b][:, h, c * 32:(c + 1) * 32],
                            start=False, stop=(c == 3)), r=[ldB[b], SbB[p2]], w=[bB[ob]])
                hd = [(T, 64), (64 * T, 4), (1, 128)]
                if d == 0:
                    P.op(P.act, lambda: nc.scalar.copy(out=osm[p2][:], in_=banks[ob][0:64, :]), r=[bB[ob]], w=[osB[p2]])
                    P.dma(DAP(self.ofw, n * 128, hd), osm[p2][:].rearrange("p (h t) -> p h t", h=4),
                          r=[osB[p2]], sb=osB[p2])
                else:
                    P.op(P.dve, lambda: nc.vector.tensor_tensor(
                        out=osm[p2][:], in0=banks[ob][0:64, :], in1=of[b][:].rearrange("p h t -> p (h t)"),
                        op=ALU.add), r=[bB[ob], ldB[b]], w=[osB[p2]])
                    P.op(P.act, lambda: nc.scalar.activation(out=sqb[:], in_=osm[p2][:], func=AF.Square),
                         r=[osB[p2]], w=[sqB])
                    P.op(P.pe, lambda: nc.tensor.matmul(banks[5][0:64, :], lhsT=ones[0:64, 0:64], rhs=sqb[:],
                                                        start=True, stop=True), r=[cB, sqB], w=[bB[5]])
                    P.op(P.act, lambda: nc.scalar.activation(out=rr[:], in_=banks[5][0:64, :], func=AF.Ln,
                                                             scale=1.0 / 64, bias=eps_t[0:64, 0:1]),
                         r=[bB[5], cB], w=[rrB])
                    P.op(P.act, lambda: nc.scalar.activation(out=rr2[:], in_=rr[:], func=AF.Exp, scale=-0.5),
                         r=[rrB], w=[rr2B])
                    P.op(P.pool, lambda: nc.gpsimd.tensor_tensor(
                        out=rr[:], in0=rr2[:], in1=gt[b][:].rearrange("p h t -> p (h t)"), op=ALU.mult),
                        r=[rr2B, ldB[b]], w=[rrB])
                    P.op(P.dve, lambda: nc.vector.tensor_tensor(
                        out=ocs[p2][:].rearrange("p h t -> p (h t)"), in0=osm[p2][:], in1=rr[:], op=ALU.mult),
                        r=[osB[p2], rrB], w=[ocB[p2]])
                    P.dma(DAP(self.oT, 768 * T + n * 128, hd), ocs[p2][:], r=[ocB[p2]], sb=ocB[p2])

            load(0)
            if NT > 1:
                load(1)
            for step in range(NT + 1):
                if step + 2 < NT:
                    load(step + 2)
                if step < NT:
                    stage1(step)
                if step >= 1:
                    stage2(step - 1)
            P.barrier()
            P.flush()
        P.release(ldB + osB + ocB + [cB])


Builder.hgrn = _hgrn


W_NAMES = ["norm1_g", "w_in", "pool_w", "pool_scale", "gqa_qnorm", "gqa_knorm", "gqa_sink", "hgrn_lb", "hgrn_onorm",
           "nat_qnorm", "nat_knorm", "nat_rpb", "w_br_pool", "w_br_gqa", "w_br_hgrn", "w_br_nat", "w_o", "norm2_g",
           "w_up", "w_down"]


def build_program(T, nseq, nlayers=DEPTH, dbg=None):
    B = Builder(T, nseq, dbg=dbg)
    B.declare()
    B.declare_mix()
    B.declare_p2()
    B.prologue()
    for s in range(nseq):
        for l in range(nlayers):
            src = B.x_in if l == 0 else B.xb
            dst = B.y_out if l == nlayers - 1 else B.xb
            B.proj2(l, s, src)
            B.pool(l)
            B.gqa(l)
            B.hgrn2(l)
            B.nat(l, s)
            B.merge(l, s, src, B.xa)
            B.ffn(l, s, B.xa, dst)
    return B


def _run(x_per_core, weights, T, nseq, nlayers=DEPTH):
    B = build_program(T, nseq, nlayers)
    consts = make_consts(T)
    in_maps = []
    for c in range(NCORES):
        m = {"x": np.ascontiguousarray(x_per_core[c], dtype=np.float32)}
        for k in W_NAMES:
            m[k] = np.ascontiguousarray(weights[k], dtype=np.float32)
        m.update(consts)
        in_maps.append(m)
    res = run_bass_kernel_spmd(B.nc, in_maps, core_ids=list(range(NCORES)))
    return [np.asarray(res.results[c]["y"]) for c in range(NCORES)]


def kernel(x_prompt, x_sample, **weights):
    x_prompt = np.asarray(x_prompt, dtype=np.float32)
    x_sample = np.asarray(x_sample, dtype=np.float32)
    seqs = [x_prompt[i] for i in range(x_prompt.shape[0])] + [x_sample[i] for i in range(x_sample.shape[0])]
    n = len(seqs)
    T = seqs[0].shape[0]
    slot = lambda c, j: (c + 8 * j) if (c + 8 * j) < n else c
    xs = [np.stack([seqs[slot(c, 0)], seqs[slot(c, 1)]], 0) for c in range(NCORES)]
    ys = _run(xs, weights, T, 2)
    out = [None] * n
    for c in range(NCORES):
        for j in range(2):
            if c + 8 * j < n:
                out[c + 8 * j] = ys[c][j]
    nb = x_prompt.shape[0]
    y_prompt = np.stack(out[:nb], 0).astype(np.float32)
    y_sample = np.stack(out[nb:], 0).astype(np.float32)
    return (y_prompt, y_sample)


def _interleave(*gens):
    gens = list(gens)
    while gens:
        for g in list(gens):
            try:
                next(g)
            except StopIteration:
                gens.remove(g)


def _proj2(self, l, s, src):
    nc, P, T = self.nc, self.P, self.T
    TT = 512
    W = self.w
    NCH = T // 32
    with ExitStack() as st:
        w1 = self.sb(st, "w1", [128, 8, 3072], BF16)
        wB = Buf("w1")
        ident = self.sb(st, "ident", [128, 128], BF16)
        eps_t = self.sb(st, "eps", [128, 1], F32)
        rmask = self.sb(st, "rmask", [128, 2, 512], F32)
        G18 = self.sb(st, "G18", [128, 18, 64], F32)
        lbr = self.sb(st, "lbr", [128, 2, 2, 2], F32)
        lbv = self.sb(st, "lbv", [128, 2, 2], F32)
        oml = self.sb(st, "oml", [128, 2, 2], F32)
        ong = self.sb(st, "ong", [128, 1], F32)
        cB = Buf("c")
        cs = [self.sb(st, f"cs{i}", [128, 2, 4, 32], F32) for i in range(2)]
        csB = P.bufs(2, "cs")
        xs = [self.sb(st, f"xs{i}", [128, D], F32) for i in range(2)]
        xsB = P.bufs(2, "xs")
        hT = self.sb(st, "hT", [128, 8, TT], BF16)
        hTB = Buf("hT")
        junk = self.sb(st, "junk", [128, D], BF16)
        ss = self.sb(st, "ss", [128, 4], F32)
        hb = self.sb(st, "hb", [128, D], BF16)
        B1 = [self.sb(st, f"B1{i}", [128, 1152], F32) for i in range(2)]
        B2 = [self.sb(st, f"B2{i}", [128, 1152], F32) for i in range(2)]
        B3 = [self.sb(st, f"B3{i}", [128, 640], F32) for i in range(2)]
        s18 = [self.sb(st, f"s18{i}", [128, 3, 18], F32) for i in range(2)]
        qr = [self.sb(st, f"qr{i}", [128, 1152], BF16) for i in range(2)]
        B1B, B2B, B3B, s18B, qrB = (P.bufs(2, n) for n in ("B1", "B2", "B3", "s18", "qr"))
        qTs = self.sb(st, "qTs", [128, 5, TT], BF16)
        nTs = self.sb(st, "nTs", [128, 4, TT], BF16)
        zps = self.sb(st, "zps", [128, 4, 256], BF16)
        gvs = self.sb(st, "gvs", [128, 4, 128], BF16)
        s3 = self.sb(st, "s3", [128, 4, 512], BF16)
        qTsB, nTsB, zpsB, gvsB, s3B = P.bufs(5, "stg")
        silq = self.sb(st, "silq", [128, 2, 512], F32)
        gtf = self.sb(st, "gtf", [128, 2, 512], F32)
        gtb = self.sb(st, "gtb", [128, 2, 512], BF16)
        silqB, gtfB, gtbB = P.bufs(3, "sg")
        NH = 6
        hh = [[self.sb(st, f"hh{d}{i}", [128, 2, 512], F32) for i in range(NH)] for d in range(2)]
        hhB = [P.bufs(NH, f"hh{d}") for d in range(2)]
        hq = [[self.sb(st, f"hq{d}{i}", [128, 2, 512], BF16) for i in range(3)] for d in range(2)]
        hqB = [P.bufs(3, f"hq{d}") for d in range(2)]
        a_s = [self.sb(st, f"a_s{d}", [128, 2, 16], F32) for d in range(2)]
        a_sB = P.bufs(2, "a_s")
        kps = [self.sb(st, f"kps{d}", [128, 4, 256], BF16) for d in range(2)]
        kpsB = P.bufs(2, "kps")
        self.pass_id += 1
        ps = st.enter_context(nc.psum_tensor(f"psall_{self.pass_id}", [128, 8, 512], F32))
        bB = P.bufs(8, "bank")
        tmp = (junk, Buf(), ss, Buf(), hb, Buf(), None, None, ident, eps_t, cB)

        P.dma(ident[:], self.c_ident.ap(), w=[cB], sb=cB)
        for j in range(2):
            P.dma(rmask[:, j, :], self.c_rmask.ap(), w=[cB], sb=cB)
        P.dma(G18[:, 0:8, :], DAP(W["gqa_qnorm"], l * 64, [(0, 128), (0, 8), (1, 64)]), w=[cB], sb=cB)
        P.dma(G18[:, 8:10, :], DAP(W["gqa_knorm"], l * 64, [(0, 128), (0, 2), (1, 64)]), w=[cB], sb=cB)
        P.dma(G18[:, 10:14, :], DAP(W["nat_qnorm"], l * 64, [(0, 128), (0, 4), (1, 64)]), w=[cB], sb=cB)
        P.dma(G18[:, 14:18, :], DAP(W["nat_knorm"], l * 64, [(0, 128), (0, 4), (1, 64)]), w=[cB], sb=cB)
        P.dma(lbr[:], DAP(W["hgrn_lb"], 0, [(1, 128), (512, 2), (256, 2), (128, 2)]), w=[cB], sb=cB, slow=True)
        for k2 in range(2):
            P.dma(ong[k2 * 64:(k2 + 1) * 64, :], DAP(W["hgrn_onorm"], l * 64, [(1, 64), (1, 1)]), w=[cB], sb=cB,
                  slow=True)
        P.op(P.dve, lambda: nc.vector.memset(eps_t[:], EPS), w=[cB])
        P.op(P.dve, lambda: nc.vector.tensor_scalar(out=G18[:, 0:8, :], in0=G18[:, 0:8, :], scalar1=0.125,
                                                    scalar2=None, op0=ALU.mult), r=[cB], w=[cB])
        P.op(P.dve, lambda: nc.vector.tensor_scalar(out=G18[:, 10:14, :], in0=G18[:, 10:14, :], scalar1=0.125,
                                                    scalar2=None, op0=ALU.mult), r=[cB], w=[cB])
        if l == 0:
            P.op(P.dve, lambda: nc.vector.memset(lbv[:], 0.0), w=[cB])
            P.op(P.dve, lambda: nc.vector.memset(oml[:], 1.0), w=[cB])
        else:
            P.op(P.dve, lambda: nc.vector.tensor_tensor(out=oml[:], in0=lbr[:, 1, :, :], in1=lbr[:, 0, :, :],
                                                        op=ALU.subtract), r=[cB], w=[cB])
            P.op(P.act, lambda: nc.scalar.activation(out=lbv[:], in_=oml[:], func=AF.Sigmoid), r=[cB], w=[cB])
            P.op(P.dve, lambda: nc.vector.tensor_scalar(out=oml[:], in0=lbv[:], scalar1=-1.0, scalar2=1.0,
                                                        op0=ALU.mult, op1=ALU.add), r=[cB], w=[cB])
        for k0 in range(0, 8, 4):
            P.dma(w1[:, k0:k0 + 4, :], DAP(self.wb_in, (l * D + k0 * 128) * 7168,
                                           [(7168, 128), (128 * 7168, 4), (1, 3072)]), w=[wB], sb=wB)
        ntile = T // TT
        nsub = T // 128

        def load_x(n):
            b = n % 2
            P.dma(xs[b][:], DAP(src, (s * T + n * 128) * D, [(D, 128), (1, D)]), w=[xsB[b]], sb=xsB[b])

        def load_cs(i):
            b = i % 2
            P.dma(cs[b][:, 0, :, :], DAP(self.c_cos, i * 4 * 32, [(self.NT * 32, 128), (32, 4), (1, 32)]),
                  w=[csB[b]], sb=csB[b])
            P.dma(cs[b][:, 1, :, :], DAP(self.c_sin, i * 4 * 32, [(self.NT * 32, 128), (32, 4), (1, 32)]),
                  w=[csB[b]], sb=csB[b])

        def mm(bank, c0, ncol, col0, lhs_cols):
            for kc in range(8):
                P.op(P.pe, lambda kc=kc: nc.tensor.matmul(
                    ps[:, bank, c0:c0 + ncol], lhsT=hT[:, kc, lhs_cols[0]:lhs_cols[1]],
                    rhs=w1[:, kc, col0:col0 + ncol], start=(kc == 0), stop=(kc == 7)), r=[wB, hTB], w=[bB[bank]])

        def mm_tok(su):
            base = (su % 2) * 4
            lc = (su * 128, (su + 1) * 128)
            mm(base + 0, 0, 512, 256, lc)
            mm(base + 1, 0, 128, 768, lc)
            mm(base + 1, 128, 384, 2304, lc)
            mm(base + 2, 0, 128, 2688, lc)
            mm(base + 2, 128, 128, 896, lc)
            mm(base + 2, 256, 256, 0, lc)
            mm(base + 3, 0, 256, 1536, lc)
            mm(base + 3, 256, 256, 2816, lc)

        def chain_tok(i, su):
            import os
            cut = int(os.environ.get("P2CUT", "99"))
            g = chain_tok_(i, su)
            n = 0
            for _ in g:
                n += 1
                if n >= cut:
                    return
                yield

        def chain_tok_(i, su):
            k = su % 2
            base = k * 4
            cb = i % 2
            Hb = [bB[base], bB[base + 1], bB[base + 2]]
            H = ps[:, base:base + 3, :].rearrange("p b c -> p (b c)")[:, 0:1152]
            for (bo, c0, c1) in ((0, 0, 512), (1, 512, 1024), (2, 1024, 1152)):
                P.op(P.act, lambda bo=bo, c0=c0, c1=c1: nc.scalar.activation(
                    out=B1[k][:, c0:c1], in_=ps[:, base + bo, 0:c1 - c0], func=AF.Square), r=[Hb[bo]], w=[B1B[k]])
            P.op(P.act, lambda: nc.scalar.copy(out=s3[:, su, :], in_=ps[:, base + 3, :]), r=[bB[base + 3]], w=[s3B])
            yield
            P.op(P.dve, lambda: nc.vector.tensor_reduce(out=s18[k][:, 0, :], in_=B1[k][:].rearrange(
                "p (h d) -> p h d", h=18), op=ALU.add, axis=AX.X), r=[B1B[k]], w=[s18B[k]])
            yield
            P.op(P.act, lambda: nc.scalar.activation(out=s18[k][:, 1, :], in_=s18[k][:, 0, :], func=AF.Ln,
                                                     scale=1.0 / 64, bias=eps_t[:, 0:1]), r=[s18B[k], cB], w=[s18B[k]])
            P.op(P.act, lambda: nc.scalar.activation(out=s18[k][:, 2, :], in_=s18[k][:, 1, :], func=AF.Exp,
                                                     scale=-0.5), r=[s18B[k]], w=[s18B[k]])
            P.op(P.act, lambda: nc.scalar.copy(out=zps[:, su, :], in_=ps[:, base + 2, 256:512]),
                 r=[bB[base + 2]], w=[zpsB])
            P.op(P.act, lambda: nc.scalar.copy(out=gvs[:, su, :], in_=ps[:, base + 2, 128:256]),
                 r=[bB[base + 2]], w=[gvsB])
            yield
            for (bo, h0, h1) in ((0, 0, 8), (1, 8, 16), (2, 16, 18)):
                nh = h1 - h0
                P.op(P.dve, lambda bo=bo, h0=h0, h1=h1, nh=nh: nc.vector.tensor_tensor(
                    out=B1[k][:, h0 * 64:h1 * 64].rearrange("p (h d) -> p h d", h=nh),
                    in0=ps[:, base + bo, 0:nh * 64].rearrange("p (h d) -> p h d", h=nh),
                    in1=s18[k][:, 2, h0:h1].unsqueeze(2).broadcast_to([128, nh, 64]), op=ALU.mult),
                    r=[Hb[bo], s18B[k]], w=[B1B[k]])
            yield
            P.op(P.pool, lambda: nc.gpsimd.tensor_tensor(
                out=B2[k][:], in0=B1[k][:], in1=G18[:].rearrange("p h d -> p (h d)"), op=ALU.mult),
                r=[B1B[k], cB], w=[B2B[k]])
            yield
            xg4 = B2[k][:, 0:640].rearrange("p (h t d) -> p h t d", h=10, t=2)
            t14 = B1[k][:, 0:640].rearrange("p (h t d) -> p h t d", h=10, t=2)
            t24 = B3[k][:, 0:640].rearrange("p (h t d) -> p h t d", h=10, t=2)
            qr4 = qr[k][:, 0:640].rearrange("p (h t d) -> p h t d", h=10, t=2)
            cb4 = cs[cb][:, 0, su, :].unsqueeze(1).unsqueeze(1).broadcast_to([128, 10, 2, 32])
            sb3 = cs[cb][:, 1, su, :].unsqueeze(1).broadcast_to([128, 10, 32])
            P.op(P.dve, lambda: nc.vector.tensor_tensor(out=t14, in0=xg4, in1=cb4, op=ALU.mult),
                 r=[B2B[k], csB[cb]], w=[B1B[k]])
            P.op(P.pool, lambda: nc.gpsimd.tensor_tensor(out=t24[:, :, 0, :], in0=xg4[:, :, 1, :], in1=sb3,
                                                         op=ALU.mult), r=[B2B[k], csB[cb]], w=[B3B[k]])
            P.op(P.pool, lambda: nc.gpsimd.tensor_tensor(out=t24[:, :, 1, :], in0=xg4[:, :, 0, :], in1=sb3,
                                                         op=ALU.mult), r=[B2B[k], csB[cb], B3B[k]], w=[B3B[k]])
            P.op(P.act, lambda: nc.scalar.copy(out=qr[k][:, 640:1152], in_=B2[k][:, 640:1152]),
                 r=[B2B[k]], w=[qrB[k]])
            yield
            P.op(P.dve, lambda: nc.vector.tensor_tensor(out=qr4[:, :, 0, :], in0=t14[:, :, 0, :], in1=t24[:, :, 0, :],
                                                        op=ALU.subtract), r=[B1B[k], B3B[k], qrB[k]], w=[qrB[k]])
            P.op(P.dve, lambda: nc.vector.tensor_tensor(out=qr4[:, :, 1, :], in0=t14[:, :, 1, :], in1=t24[:, :, 1, :],
                                                        op=ALU.add), r=[B1B[k], B3B[k], qrB[k]], w=[qrB[k]])
            yield
            tpA = ps[:, base, :].bitcast(BF16)
            tpB = ps[:, base + 1, :].bitcast(BF16)
            for pr in range(8):
                P.op(P.pe, lambda pr=pr: nc.tensor.transpose(tpA[:, pr * 128:(pr + 1) * 128],
                                                            qr[k][:, pr * 128:(pr + 1) * 128], ident[:]),
                     r=[qrB[k], cB], w=[bB[base]])
            P.op(P.pe, lambda: nc.tensor.transpose(tpB[:, 0:128], qr[k][:, 1024:1152], ident[:]),
                 r=[qrB[k], cB], w=[bB[base + 1]])
            yield
            P.op(P.act, lambda: nc.scalar.copy(out=qTs[:, :, su * 128:(su + 1) * 128],
                                               in_=tpA[:, 0:640].rearrange("p (c t) -> p c t", c=5)),
                 r=[bB[base]], w=[qTsB])
            P.op(P.act, lambda: nc.scalar.copy(out=nTs[:, 0:3, su * 128:(su + 1) * 128],
                                               in_=tpA[:, 640:1024].rearrange("p (c t) -> p c t", c=3)),
                 r=[bB[base]], w=[nTsB])
            P.op(P.act, lambda: nc.scalar.copy(out=nTs[:, 3, su * 128:(su + 1) * 128], in_=tpB[:, 0:128]),
                 r=[bB[base + 1]], w=[nTsB])
            yield

        def mm_feat(bank, col0):
            for kc in range(8):
                P.op(P.pe, lambda kc=kc: nc.tensor.matmul(
                    ps[:, bank, :], lhsT=w1[:, kc, col0:col0 + 128], rhs=hT[:, kc, :],
                    start=(kc == 0), stop=(kc == 7)), r=[wB, hTB], w=[bB[bank]])

        def chain_h(i, d):
            t0 = i * TT
            bk = 4 + 2 * d
            Z = ps[:, bk:bk + 2, :]
            Zb = [bB[bk], bB[bk + 1]]
            h_ = hh[d]
            hB_ = hhB[d]
            fl = lambda t: t[:].rearrange("p a b -> p (a b)")
            for hp in range(2):
                P.op(P.act, lambda hp=hp: nc.scalar.activation(out=h_[0][:, hp, :], in_=ps[:, bk + hp, :],
                                                               func=AF.Sigmoid), r=[Zb[hp]], w=[hB_[0]])
            yield
            for hp in range(2):
                P.op(P.dve, lambda hp=hp: nc.vector.tensor_scalar(
                    out=h_[1][:, hp, :], in0=h_[0][:, hp, :], scalar1=oml[:, d, hp:hp + 1], scalar2=lbv[:, d, hp:hp + 1],
                    op0=ALU.mult, op1=ALU.add), r=[hB_[0], cB], w=[hB_[1]])
            yield
            P.op(P.act, lambda: nc.scalar.activation(out=h_[2][:], in_=h_[1][:], func=AF.Ln), r=[hB_[1]], w=[hB_[2]])
            P.op(P.pool, lambda: nc.gpsimd.tensor_scalar(out=h_[3][:], in0=h_[1][:], scalar1=-1.0, scalar2=1.0,
                                                         op0=ALU.mult, op1=ALU.add), r=[hB_[1]], w=[hB_[3]])
            yield
            P.op(P.dve, lambda: nc.vector.tensor_tensor_scan(
                out=fl(h_[4]), data0=fl(rmask), data1=fl(h_[2]), initial=0.0, op0=ALU.mult, op1=ALU.add),
                r=[hB_[2], cB], w=[hB_[4]])
            yield
            bbi = 4
            if d == 1:
                P.op(P.pool, lambda: nc.gpsimd.tensor_tensor(out=fl(h_[2]), in0=fl(h_[2]), in1=fl(h_[4]),
                                                             op=ALU.subtract), r=[hB_[2], hB_[4]], w=[hB_[2]])
                yield
                P.op(P.dve, lambda: nc.vector.tensor_tensor(
                    out=fl(h_[5]).rearrange("p (n c) -> p n c", c=32),
                    in0=fl(h_[2]).rearrange("p (n c) -> p n c", c=32),
                    in1=fl(h_[4]).rearrange("p (n c) -> p n c", c=32)[:, :, 31:32].broadcast_to([128, 32, 32]),
                    op=ALU.add), r=[hB_[2], hB_[4]], w=[hB_[5]])
                yield
                bbi = 5
            P.op(P.act, lambda: nc.scalar.activation(out=h_[1][:], in_=h_[bbi][:], func=AF.Exp),
                 r=[hB_[bbi], hB_[3]], w=[hB_[1]])
            P.op(P.act, lambda: nc.scalar.activation(out=h_[0][:], in_=h_[bbi][:], func=AF.Exp, scale=-1.0),
                 r=[hB_[bbi]], w=[hB_[0]])
            yield
            P.op(P.dve, lambda: nc.vector.tensor_tensor(out=hq[d][0][:], in0=silq[:], in1=h_[1][:], op=ALU.mult),
                 r=[silqB, hB_[1]], w=[hqB[d][0]])
            P.op(P.pool, lambda: nc.gpsimd.tensor_tensor(out=h_[2][:], in0=h_[3][:], in1=h_[0][:], op=ALU.mult),
                 r=[hB_[3], hB_[0]], w=[hB_[2]])
            aidx = 31 if d == 0 else 0
            eb3 = fl(h_[1]).rearrange("p (n c) -> p n c", c=32)
            P.op(P.act, lambda: nc.scalar.copy(out=a_s[d][:].rearrange("p a n -> p (a n)").unsqueeze(2),
                                               in_=eb3[:, :, aidx:aidx + 1]), r=[hB_[1]], w=[a_sB[d]])
            yield
            P.op(P.act, lambda: nc.scalar.copy(out=hq[d][1][:], in_=h_[2][:]), r=[hB_[2]], w=[hqB[d][1]])
            P.op(P.dve, lambda: nc.vector.tensor_tensor(
                out=fl(hq[d][2]).rearrange("p (n c) -> p n c", c=32),
                in0=fl(h_[2]).rearrange("p (n c) -> p n c", c=32),
                in1=eb3[:, :, aidx:aidx + 1].broadcast_to([128, 32, 32]), op=ALU.mult),
                r=[hB_[2], hB_[1]], w=[hqB[d][2]])
            P.dma(DAP(self.aT, d * 256 * NCH + i * 16, [(NCH, 128), (128 * NCH, 2), (1, 16)]), a_s[d][:],
                  r=[a_sB[d]], sb=a_sB[d])
            P.dma(DAP(self.QT, d * 256 * T + t0, [(T, 128), (128 * T, 2), (1, TT)]), hq[d][0][:],
                  r=[hqB[d][0]], sb=hqB[d][0])
            yield
            P.dma(DAP(self.KT, d * 256 * T + t0, [(T, 128), (128 * T, 2), (1, TT)]), hq[d][1][:],
                  r=[hqB[d][1]], sb=hqB[d][1])
            tp = ps[:, bk, :].bitcast(BF16)
            for su in range(4):
                for hp in range(2):
                    P.op(P.pe, lambda su=su, hp=hp: nc.tensor.transpose(
                        tp[:, (su * 2 + hp) * 128:(su * 2 + hp + 1) * 128], hq[d][2][:, hp, su * 128:(su + 1) * 128],
                        ident[:]), r=[hqB[d][2], cB], w=[bB[bk]])
            yield
            P.op(P.act, lambda: nc.scalar.copy(out=kps[d][:].rearrange("p a b -> p (a b)"), in_=tp[:, 0:1024]),
                 r=[bB[bk]], w=[kpsB[d]])
            P.dma(DAP(self.Kp, (d * T + t0) * 256, [(256, 128), (128 * 256, 4), (1, 256)]), kps[d][:],
                  r=[kpsB[d]], sb=kpsB[d])
            yield

        def _igen(*gens):
            gens = list(gens)
            while gens:
                for g in list(gens):
                    try:
                        next(g)
                    except StopIteration:
                        gens.remove(g)
                yield

        def front(i):
            if i + 1 < ntile:
                load_cs(i + 1)
            for su in range(4):
                n = i * 4 + su
                if n + 1 < nsub:
                    load_x(n + 1)
                self.norm_transpose(xs[n % 2][:], xsB[n % 2], hT, hTB, su * 128,
                                    (junk, tmp[1], ss, tmp[3], hb, tmp[5], _BankView(ps, 7), bB[7], ident, eps_t, cB))
                yield

        def back(i):
            t0 = i * TT
            mm_tok(0)
            mm_tok(1)
            _interleave(chain_tok(i, 0), chain_tok(i, 1))
            mm_tok(2)
            mm_tok(3)
            _interleave(chain_tok(i, 2), chain_tok(i, 3))
            P.dma(DAP(self.zp, t0 * 256, [(256, 128), (128 * 256, 4), (1, 256)]), zps[:], r=[zpsB], sb=zpsB)
            P.dma(DAP(self.gv, t0 * 128, [(128, 128), (128 * 128, 4), (1, 128)]), gvs[:], r=[gvsB], sb=gvsB)
            P.dma(DAP(self.hv, t0 * 256, [(256, 128), (128 * 256, 4), (1, 256)]), s3[:, :, 0:256], r=[s3B], sb=s3B)
            P.dma(DAP(self.nv, t0 * 256, [(256, 128), (128 * 256, 4), (1, 256)]), s3[:, :, 256:512], r=[s3B], sb=s3B)
            P.dma(DAP(self.qT, t0, [(T, 128), (128 * T, 4), (1, TT)]), qTs[:, 0:4, :], r=[qTsB], sb=qTsB)
            P.dma(DAP(self.kT, t0, [(T, 128), (1, TT)]), qTs[:, 4, :], r=[qTsB], sb=qTsB)
            P.dma(DAP(self.nqkT, t0, [(T, 128), (128 * T, 4), (1, TT)]), nTs[:], r=[nTsB], sb=nTsB)
            for hp in range(2):
                mm_feat(0 + hp, 1792 + hp * 128)
            for hp in range(2):
                P.op(P.act, lambda hp=hp: nc.scalar.activation(out=silq[:, hp, :], in_=ps[:, hp, :], func=AF.Silu),
                     r=[bB[hp]], w=[silqB])
            for hp in range(2):
                mm_feat(2 + hp, 2048 + hp * 128)
            for hp in range(2):
                P.op(P.act, lambda hp=hp: nc.scalar.activation(out=gtf[:, hp, :], in_=ps[:, 2 + hp, :], func=AF.Silu),
                     r=[bB[2 + hp]], w=[gtfB])
            P.op(P.dve, lambda: nc.vector.tensor_scalar(out=gtb[:], in0=gtf[:], scalar1=ong[:, 0:1], scalar2=None,
                                                        op0=ALU.mult), r=[gtfB, cB], w=[gtbB])
            P.dma(DAP(self.gateT, t0, [(T, 128), (128 * T, 2), (1, TT)]), gtb[:], r=[gtbB], sb=gtbB)
            for d in range(2):
                for hp in range(2):
                    mm_feat(4 + 2 * d + hp, 1024 + d * 256 + hp * 128)

        load_x(0)
        load_cs(0)
        for _ in front(0):
            pass
        for i in range(ntile):
            back(i)
            gens = [chain_h(i, 0), chain_h(i, 1)]
            if i + 1 < ntile:
                gens.append(front(i + 1))
            _interleave(*gens)
        P.barrier()
        P.flush()
        P.release(xsB + csB + [wB, cB, zpsB, gvsB, s3B, qTsB, nTsB, gtbB] + a_sB + kpsB + hqB[0] + hqB[1])


class _BankView:
    def __init__(self, ps, b):
        self.ps, self.b = ps, b

    def __getitem__(self, idx):
        return self.ps[:, self.b, :]


Builder.proj2 = _proj2


def _hgrn2(self, l):
    nc, P, T, NT = self.nc, self.P, self.T, self.NT
    NCH = T // 32
    if not hasattr(self, "obw"):
        raise RuntimeError("declare obw first")
    with ExitStack() as st:
        hmask = self.sb(st, "hmask", [128, 2, 512], BF16)
        cm = self.sb(st, "cm", [128, 4], BF16)
        cB = Buf("c")
        banks = self.psum_banks(st)
        bB = P.bufs(8, "bank")
        P.dma(hmask[:], self.c_hmask.ap(), w=[cB], sb=cB)
        P.dma(cm[:], self.c_cm.ap(), w=[cB], sb=cB)
        NB = 4
        rel = [cB]

        def make_dir(d):
            order = list(range(NT)) if d == 0 else list(range(NT - 1, -1, -1))
            corder = [0, 1, 2, 3] if d == 0 else [3, 2, 1, 0]
            NBs = 2
            qt = [self.sb(st, f"qt{d}{i}", [64, 4, 512], BF16) for i in range(NBs)]
            kt = [self.sb(st, f"kt{d}{i}", [64, 4, 512], BF16) for i in range(NBs)]
            kp = [self.sb(st, f"kp{d}{i}", [128, 4, 256], BF16) for i in range(NBs)]
            vt = [self.sb(st, f"vt{d}{i}", [128, 4, 256], BF16) for i in range(NBs)]
            at = [self.sb(st, f"at{d}{i}", [64, 4, 16], F32) for i in range(NBs)]
            ldB = P.bufs(NBs, f"ld{d}")
            NST = NT // 4
            kpm = [self.sb(st, f"kpm{d}{i}", [128, 4, 256], BF16) for i in range(2)]
            kpmB = P.bufs(2, f"kpm{d}")
            Sf = [self.sb(st, f"Sf{d}{j}", [64, 4, 64], F32) for j in range(3)]
            SfhB = [P.bufs(2, f"Sfh{d}{j}") for j in range(3)]
            stmp = self.sb(st, f"stmp{d}", [64, 4, 64], F32)
            sthB = P.bufs(2, f"sth{d}")
            Sb = [self.sb(st, f"Sb{d}{i}", [64, 4, 4, 64], BF16) for i in range(2)]
            SbB = P.bufs(2, f"Sb{d}")
            Am = [self.sb(st, f"Am{d}{i}", [128, 4, 128], BF16) for i in range(2)]
            AmB = P.bufs(2, f"Am{d}")
            osm = [self.sb(st, f"osm{d}{i}", [64, 4, 512], F32) for i in range(2)]
            osB = P.bufs(2, f"osm{d}")
            Us = [self.sb(st, f"Us{d}{i}", [64, 1024], F32) for i in range(2)]
            UsB = P.bufs(2, f"Us{d}")
            rel.extend(ldB + osB)
            ub = [0, 1] if d == 0 else [4, 5]
            ab = 2 if d == 0 else 6
            ob = 3 if d == 0 else 7
            odst = self.ofw if d == 0 else self.obw
            cur = [0]
            hd = [(T, 64), (64 * T, 4), (1, 128)]

            def init():
                P.op(P.dve, lambda: nc.vector.memset(Sf[0][:], 0.0), w=SfhB[0])

            def load(g):
                sidx = g if d == 0 else NST - 1 - g
                b = g % NBs
                t0 = sidx * 512
                hd5 = [(T, 64), (64 * T, 4), (1, 512)]
                P.dma(qt[b][:], DAP(self.QT, d * 256 * T + t0, hd5), w=[ldB[b]], sb=ldB[b])
                P.dma(kt[b][:], DAP(self.KT, d * 256 * T + t0, hd5), w=[ldB[b]], sb=ldB[b])
                P.dma(kp[b][:], DAP(self.Kp, (d * T + t0) * 256, [(256, 128), (128 * 256, 4), (1, 256)]),
                      w=[ldB[b]], sb=ldB[b])
                P.dma(vt[b][:], DAP(self.hv, t0 * 256, [(256, 128), (128 * 256, 4), (1, 256)]), w=[ldB[b]], sb=ldB[b])
                P.dma(at[b][:], DAP(self.aT, d * 256 * NCH + sidx * 16, [(NCH, 64), (64 * NCH, 4), (1, 16)]),
                      w=[ldB[b]], sb=ldB[b])

            def stage1(step):
                b = (step // 4) % NBs
                lc = order[step] % 4
                p2 = step % 2
                P.op(P.pool, lambda: nc.gpsimd.tensor_tensor(
                    out=kpm[p2][:], in0=kp[b][:, lc, :].unsqueeze(1).broadcast_to([128, 4, 256]),
                    in1=cm[:].unsqueeze(2).broadcast_to([128, 4, 256]), op=ALU.mult), r=[ldB[b], cB], w=[kpmB[p2]])
                for c in range(4):
                    for h in range(4):
                        col = ((c % 2) * 4 + h) * 64
                        P.op(P.pe, lambda c=c, h=h, col=col: nc.tensor.matmul(
                            banks[ub[c // 2]][0:64, col:col + 64], lhsT=kpm[p2][:, c, h * 64:(h + 1) * 64],
                            rhs=vt[b][:, lc, h * 64:(h + 1) * 64], start=True, stop=True),
                            r=[kpmB[p2], ldB[b]], w=[bB[ub[c // 2]]])
                for j in range(2):
                    P.op(P.act, lambda j=j: nc.scalar.copy(out=Us[p2][:, j * 512:(j + 1) * 512], in_=banks[ub[j]][0:64, :]),
                         r=[bB[ub[j]]], w=[UsB[p2]])
                for h in range(4):
                    P.op(P.pe, lambda h=h: nc.tensor.matmul(
                        banks[ab][:, h * 128:(h + 1) * 128], lhsT=kt[b][:, h, lc * 128:(lc + 1) * 128],
                        rhs=qt[b][:, h, lc * 128:(lc + 1) * 128], start=True, stop=True), r=[ldB[b]], w=[bB[ab]])
                P.op(P.dve, lambda: nc.vector.tensor_tensor(
                    out=Am[p2][:].rearrange("p h t -> p (h t)"), in0=banks[ab][:], in1=hmask[:, d, :], op=ALU.mult),
                    r=[bB[ab], cB], w=[AmB[p2]])

            def chain(step):
                b = (step // 4) % NBs
                lc = order[step] % 4
                p2 = step % 2
                for c in corder:
                    cu = cur[0]
                    nx = (cu + 1) % 3
                    P.op(P.act, lambda c=c, cu=cu: nc.scalar.copy(out=Sb[p2][:, c, :, :], in_=Sf[cu][:]),
                         r=SfhB[cu], w=[SbB[p2]])
                    yield
                    ucol = (c % 2) * 256
                    for hh in range(2):
                        P.op(P.dve, lambda c=c, cu=cu, hh=hh: nc.vector.tensor_tensor(
                            out=stmp[:, 2 * hh:2 * hh + 2, :], in0=Sf[cu][:, 2 * hh:2 * hh + 2, :],
                            in1=at[b][:, 2 * hh:2 * hh + 2, lc * 4 + c:lc * 4 + c + 1].broadcast_to([64, 2, 64]),
                            op=ALU.mult),
                            r=[SfhB[cu][hh], ldB[b]], w=[sthB[hh]])
                        yield
                    for hh in range(2):
                        P.op(P.dve, lambda c=c, nx=nx, ucol=ucol, hh=hh: nc.vector.tensor_tensor(
                            out=Sf[nx][:, 2 * hh:2 * hh + 2, :].rearrange("p h v -> p (h v)"),
                            in0=Us[p2][:, c * 256 + hh * 128: c * 256 + (hh + 1) * 128],
                            in1=stmp[:, 2 * hh:2 * hh + 2, :].rearrange("p h v -> p (h v)"), op=ALU.add),
                            r=[sthB[hh], UsB[p2]], w=[SfhB[nx][hh]])
                        yield
                    cur[0] = nx

            def stage2(step):
                n = order[step]
                b = (step // 4) % NBs
                lc = n % 4
                g = step // 4
                p2 = step % 2
                for h in range(4):
                    P.op(P.pe, lambda h=h: nc.tensor.matmul(
                        banks[ob][0:64, h * 128:(h + 1) * 128], lhsT=vt[b][:, lc, h * 64:(h + 1) * 64],
                        rhs=Am[p2][:, h, :], start=True, stop=False), r=[ldB[b], AmB[p2]], w=[bB[ob]])
                    for c in range(4):
                        P.op(P.pe, lambda h=h, c=c: nc.tensor.matmul(
                            banks[ob][0:64, h * 128 + c * 32: h * 128 + (c + 1) * 32],
                            lhsT=Sb[p2][:, c, h, :], rhs=qt[b][:, h, lc * 128 + c * 32: lc * 128 + (c + 1) * 32],
                            start=False, stop=(c == 3)), r=[ldB[b], SbB[p2]], w=[bB[ob]])
                P.op(P.act, lambda: nc.scalar.copy(out=osm[g % 2][:, :, lc * 128:(lc + 1) * 128],
                                                   in_=banks[ob][0:64, :].rearrange("p (h t) -> p h t", h=4)),
                     r=[bB[ob]], w=[osB[g % 2]])
                if step % 4 == 3:
                    sidx = n // 4
                    P.dma(DAP(odst, sidx * 512, [(T, 64), (64 * T, 4), (1, 512)]), osm[g % 2][:],
                          r=[osB[g % 2]], sb=osB[g % 2])

            return init, load, stage1, stage2, chain

        dirs = [make_dir(0), make_dir(1)]
        assert NT % 4 == 0
        for (init, load, s1, s2, ch) in dirs:
            init()
            load(0)
        for step in range(NT + 1):
            for (init, load, s1, s2, ch) in dirs:
                if step % 4 == 1 and step // 4 + 1 < NT // 4:
                    load(step // 4 + 1)
            if step < NT:
                for (init, load, s1, s2, ch) in dirs:
                    s1(step)
                _interleave(dirs[0][4](step), dirs[1][4](step))
            for (init, load, s1, s2, ch) in dirs:
                if step >= 1:
                    s2(step - 1)
        P.barrier()
        P.flush()
        P.release(rel)

    TT = 512
    with ExitStack() as st:
        ones = self.sb(st, "ones", [128, 64], BF16)
        eps_t = self.sb(st, "eps", [128, 1], F32)
        cB = Buf("c")
        of = [self.sb(st, f"of{i}", [64, 4, TT], F32) for i in range(2)]
        ob_ = [self.sb(st, f"ob{i}", [64, 4, TT], F32) for i in range(2)]
        gt = [self.sb(st, f"gt{i}", [64, 4, TT], BF16) for i in range(2)]
        ldB = P.bufs(2, "ld")
        sqb = [self.sb(st, f"sqb{i}", [64, 4, TT], BF16) for i in range(2)]
        sqB = P.bufs(2, "sqb")
        rr = [self.sb(st, f"rr{i}", [64, 4, TT], F32) for i in range(2)]
        rrB = P.bufs(2, "rr")
        ocs = [self.sb(st, f"ocs{i}", [64, 4, TT], BF16) for i in range(2)]
        ocB = P.bufs(2, "ocs")
        banks = self.psum_banks(st)
        bB = P.bufs(8, "bank")
        P.op(P.dve, lambda: nc.vector.memset(ones[:], 1.0), w=[cB])
        P.op(P.dve, lambda: nc.vector.memset(eps_t[:], EPS), w=[cB])
        ntile = T // TT
        hd = [(T, 64), (64 * T, 4), (1, TT)]
        fl = lambda t: t[:].rearrange("p h t -> p (h t)")

        def load(i):
            b = i % 2
            P.dma(of[b][:], DAP(self.ofw, i * TT, hd), w=[ldB[b]], sb=ldB[b])
            P.dma(ob_[b][:], DAP(self.obw, i * TT, hd), w=[ldB[b]], sb=ldB[b])
            P.dma(gt[b][:], DAP(self.gateT, i * TT, hd), w=[ldB[b]], sb=ldB[b])

        def fin(i):
            b = i % 2
            P.op(P.dve, lambda: nc.vector.tensor_tensor(out=fl(of[b]), in0=fl(of[b]), in1=fl(ob_[b]), op=ALU.add),
                 r=[ldB[b]], w=[ldB[b]])
            yield
            P.op(P.act, lambda: nc.scalar.activation(out=fl(sqb[b]), in_=fl(of[b]), func=AF.Square),
                 r=[ldB[b]], w=[sqB[b]])
            yield
            for h in range(4):
                pb = b * 4 + h
                P.op(P.pe, lambda h=h, pb=pb: nc.tensor.matmul(banks[pb][0:64, :], lhsT=ones[0:64, 0:64],
                                                              rhs=sqb[b][:, h, :], start=True, stop=True),
                     r=[cB, sqB[b]], w=[bB[pb]])
            yield
            for h in range(4):
                pb = b * 4 + h
                P.op(P.act, lambda h=h, pb=pb: nc.scalar.activation(out=rr[b][:, h, :], in_=banks[pb][0:64, :],
                                                                    func=AF.Ln, scale=1.0 / 64, bias=eps_t[0:64, 0:1]),
                     r=[bB[pb], cB], w=[rrB[b]])
            P.op(P.act, lambda: nc.scalar.activation(out=fl(rr[b]), in_=fl(rr[b]), func=AF.Exp, scale=-0.5),
                 r=[rrB[b]], w=[rrB[b]])
            yield
            P.op(P.pool, lambda: nc.gpsimd.tensor_tensor(out=fl(rr[b]), in0=fl(rr[b]), in1=fl(gt[b]), op=ALU.mult),
                 r=[rrB[b], ldB[b]], w=[rrB[b]])
            yield
            P.op(P.dve, lambda: nc.vector.tensor_tensor(out=fl(ocs[b]), in0=fl(of[b]), in1=fl(rr[b]), op=ALU.mult),
                 r=[ldB[b], rrB[b]], w=[ocB[b]])
            P.dma(DAP(self.oT, 768 * T + i * TT, hd), ocs[b][:], r=[ocB[b]], sb=ocB[b])
            yield

        load(0)
        if ntile > 1:
            load(1)
        for i in range(0, ntile, 2):
            gens = [fin(i)] + ([fin(i + 1)] if i + 1 < ntile else [])
            _interleave(*gens)
            for j in (i + 2, i + 3):
                if j < ntile:
                    load(j)
        P.barrier()
        P.flush()
        P.release(ldB + ocB + [cB])


Builder.hgrn2 = _hgrn2
```

```python
import numpy as np
from contextlib import ExitStack
import concourse.bass as bass
import concourse.mybir as mybir
from concourse.bass_utils import run_bass_kernel_spmd

F32 = mybir.dt.float32
BF16 = mybir.dt.bfloat16
AF = mybir.ActivationFunctionType
ALU = mybir.AluOpType
AX = mybir.AxisListType

D = 1024
DEPTH = 2
NCORES = 8
EPS = 1e-6
NEGM = -30000.0
SAME_ENGINE_SYNC = True
SAME_ENGINE_DIST = 10 ** 9


class Sem:
    def __init__(self, handle, sid):
        self.h = handle
        self.cnt = 0
        self.id = sid


class Eng:
    def __init__(self, name, h, sem):
        self.name = name
        self.h = h
        self.sem = sem
        self.ms = 0
        self.waited = {}
        self.last = None
        self.nseq = 0


class Op:
    __slots__ = ("eng", "fn", "deps", "ms", "dma", "dsem", "dval", "marked", "seq")

    def __init__(self, eng, fn, deps, dma=False, dsem=None, dval=0):
        self.eng = eng
        self.fn = fn
        self.deps = deps
        self.ms = 0
        self.dma = dma
        self.dsem = dsem
        self.dval = dval
        self.marked = False
        self.seq = 0


class Buf:
    __slots__ = ("w", "rs", "dsem", "name")

    def __init__(self, name=""):
        self.w = None
        self.rs = {}
        self.dsem = None
        self.name = name


class Prog:
    def __init__(self, nc):
        self.nc = nc
        self.nsem = 0
        self.ops = []
        mk = lambda n, h: Eng(n, h, self.new_sem(n))
        self.pe = mk("pe", nc.tensor)
        self.act = mk("act", nc.scalar)
        self.dve = mk("dve", nc.vector)
        self.pool = mk("pool", nc.gpsimd)
        self.sp = mk("sp", nc.sync)
        self.engs = [self.pe, self.act, self.dve, self.pool, self.sp]
        self.dsems = []
        self.free_dsems = []
        self.gsem = self.new_dsem()
        self.n_ins = 0

    def new_sem(self, name):
        h = self.nc.alloc_semaphore(name=f"s_{name}_{self.nsem}")
        self.nsem += 1
        return Sem(h, self.nsem)

    def new_dsem(self):
        if self.free_dsems:
            return self.free_dsems.pop()
        s = self.new_sem("d")
        self.dsems.append(s)
        return s

    def bufs(self, n, name=""):
        return [Buf(f"{name}{i}") for i in range(n)]

    def release(self, bufs):
        for b in bufs:
            if b.dsem is not None:
                self.free_dsems.append(b.dsem)
                b.dsem = None

    def _deps(self, r, w):
        deps = []
        for b in r:
            if b.w is not None:
                deps.append(b.w)
        for b in w:
            if b.w is not None:
                deps.append(b.w)
            deps.extend(b.rs.values())
        return deps

    def _upd(self, op, key, r, w):
        for b in r:
            b.rs[key] = op
        for b in w:
            b.w = op
            b.rs = {}

    def op(self, eng, fn, r=(), w=()):
        o = Op(eng, fn, self._deps(r, w))
        eng.nseq += 1
        o.seq = eng.nseq
        self._upd(o, eng.name, r, w)
        self.ops.append(o)
        eng.last = o
        return o

    def dma(self, out, in_, r=(), w=(), sb=None, q=None, slow=False):
        q = q or self.sp
        if sb is None:
            ds = self.gsem
        else:
            if sb.dsem is None:
                sb.dsem = self.new_dsem()
            ds = sb.dsem
        ds.cnt += 16
        if slow:
            fn = lambda: q.h.dma_start(out=out, in_=in_, allow_slow_non_contiguous=True)
        else:
            fn = lambda: q.h.dma_start(out=out, in_=in_)
        o = Op(q, fn, self._deps(r, w), dma=True, dsem=ds, dval=ds.cnt)
        self._upd(o, ("d", ds.id), r, w)
        self.ops.append(o)
        ds.last = o
        return o

    def barrier(self):
        deps = [e.last for e in self.engs if e.last is not None]
        deps += [s.last for s in self.dsems if getattr(s, "last", None) is not None]
        for e in self.engs:
            o = Op(e, None, list(deps))
            self.ops.append(o)

    def _skip_same(self, o, d):
        if d.eng is not o.eng or o.dma or d.dma:
            return False
        if d.eng is self.pe or not SAME_ENGINE_SYNC:
            return True
        return (o.seq - d.seq) > SAME_ENGINE_DIST

    def flush(self):
        for o in self.ops:
            for d in o.deps:
                if not d.dma:
                    if self._skip_same(o, d):
                        continue
                    d.marked = True
        for o in self.ops:
            e = o.eng
            for d in o.deps:
                if d.dma:
                    sem, val = d.dsem, d.dval
                else:
                    if not d.marked:
                        continue
                    if self._skip_same(o, d):
                        continue
                    sem, val = d.eng.sem, d.ms
                if e.waited.get(sem.id, 0) >= val:
                    continue
                e.waited[sem.id] = val
                e.h.wait_ge(sem.h, val)
            if o.fn is None:
                continue
            ins = o.fn()
            self.n_ins += 1
            if o.dma:
                ins.then_inc(o.dsem.h, 16)
            elif o.marked:
                e.ms += 1
                o.ms = e.ms
                ins.then_inc(e.sem.h, 1)
        self.ops = []


def DAP(t, off, dims):
    return bass.AP(t, off, [[int(s), int(n)] for s, n in dims])


class Builder:
    def __init__(self, T, nseq, dbg=None):
        self.T = T
        self.NT = T // 128
        self.nseq = nseq
        self.dbg = dbg
        nc = bass.Bass("TRN2", target_bir_lowering=False)
        self.nc = nc
        self.P = Prog(nc)
        self.dram = {}
        self.pass_id = 0

    def din(self, name, shape, dt=F32):
        t = self.nc.dram_tensor(name, list(shape), dt, kind="ExternalInput")
        self.dram[name] = t
        return t

    def dscr(self, name, shape, dt, out=False):
        kind = "ExternalOutput" if (out or (self.dbg and name in self.dbg)) else "Internal"
        t = self.nc.dram_tensor(name, list(shape), dt, kind=kind)
        self.dram[name] = t
        return t

    def sb(self, st, name, shape, dt):
        self.pass_id += 1
        return st.enter_context(self.nc.sbuf_tensor(f"{name}_{self.pass_id}", list(shape), dt))

    def psum_banks(self, st):
        self.pass_id += 1
        banks = [st.enter_context(self.nc.psum_tensor(f"ps{i}_{self.pass_id}", [128, 512], F32)) for i in range(8)]
        return banks

    def declare(self):
        T, nseq = self.T, self.nseq
        L = DEPTH
        self.x_in = self.din("x", [nseq, T, D])
        self.y_out = self.dscr("y", [nseq, T, D], F32, out=True)
        w = {}
        w["norm1_g"] = self.din("norm1_g", [L, D])
        w["w_in"] = self.din("w_in", [L, D, 7168])
        w["pool_w"] = self.din("pool_w", [L, 4, 64, 64])
        w["pool_scale"] = self.din("pool_scale", [L, 256])
        w["gqa_qnorm"] = self.din("gqa_qnorm", [L, 64])
        w["gqa_knorm"] = self.din("gqa_knorm", [L, 64])
        w["gqa_sink"] = self.din("gqa_sink", [L, 8])
        w["hgrn_lb"] = self.din("hgrn_lb", [L, 2, 256])
        w["hgrn_onorm"] = self.din("hgrn_onorm", [L, 64])
        w["nat_qnorm"] = self.din("nat_qnorm", [L, 64])
        w["nat_knorm"] = self.din("nat_knorm", [L, 64])
        w["nat_rpb"] = self.din("nat_rpb", [L, 4, 15, 31])
        w["w_br_pool"] = self.din("w_br_pool", [L, 256, D])
        w["w_br_gqa"] = self.din("w_br_gqa", [L, 512, D])
        w["w_br_hgrn"] = self.din("w_br_hgrn", [L, 256, D])
        w["w_br_nat"] = self.din("w_br_nat", [L, 256, D])
        w["w_o"] = self.din("w_o", [L, D, D])
        w["norm2_g"] = self.din("norm2_g", [L, D])
        w["w_up"] = self.din("w_up", [L, D, 4096])
        w["w_down"] = self.din("w_down", [L, 4096, D])
        self.w = w
        self.c_ident = self.din("c_ident", [128, 128], BF16)
        self.wb_in = self.dscr("wb_in", [L, D, 7168], BF16)
        self.wb_br = self.dscr("wb_br", [L, 1280, D], BF16)
        self.wb_o = self.dscr("wb_o", [L, D, D], BF16)
        self.wb_up = self.dscr("wb_up", [L, D, 4096], BF16)
        self.wb_down = self.dscr("wb_down", [L, 4096, D], BF16)
        self.xa = self.dscr("xa", [nseq, T, D], F32)
        self.xb = self.dscr("xb", [nseq, T, D], F32)
        self.oT = self.dscr("oT", [1280, T], BF16)

    def prologue(self):
        nc, P = self.nc, self.P
        with ExitStack() as st:
            CW = 4096
            stg = [self.sb(st, f"wst{i}", [128, CW], F32) for i in range(3)]
            stgb = [self.sb(st, f"wsb{i}", [128, CW], BF16) for i in range(3)]
            sB = P.bufs(3, "wst")
            sBb = P.bufs(3, "wsb")
            gt = self.sb(st, "gt", [128, 2, DEPTH, 8], F32)
            gB = Buf("gt")
            for i, nm in enumerate(["norm1_g", "norm2_g"]):
                for l in range(DEPTH):
                    P.dma(gt[:, i, l, :], DAP(self.w[nm], l * D, [(1, 128), (128, 8)]), w=[gB], sb=gB, slow=True)
            jobs = []
            for l in range(DEPTH):
                jobs.append((self.w["w_in"], l * D * 7168, 7168, D, self.wb_in, l * D * 7168, (0, l)))
                srcs = [("w_br_pool", 256), ("w_br_gqa", 512), ("w_br_hgrn", 256), ("w_br_nat", 256)]
                ro = 0
                for nm, rows in srcs:
                    jobs.append((self.w[nm], l * rows * D, D, rows, self.wb_br, (l * 1280 + ro) * D, None))
                    ro += rows
                jobs.append((self.w["w_o"], l * D * D, D, D, self.wb_o, l * D * D, None))
                jobs.append((self.w["w_up"], l * D * 4096, 4096, D, self.wb_up, l * D * 4096, (1, l)))
                jobs.append((self.w["w_down"], l * 4096 * D, D, 4096, self.wb_down, l * 4096 * D, None))
            k = 0
            for (src, soff, ncols, nrows, dst, doff, gsel) in jobs:
                nrb_tot = nrows // 128
                if gsel is not None or ncols >= CW:
                    chunks = [(rb, 1, c0, min(CW, ncols - c0)) for rb in range(nrb_tot) for c0 in range(0, ncols, CW)]
                else:
                    per = CW // ncols
                    chunks = [(rb, min(per, nrb_tot - rb), 0, ncols) for rb in range(0, nrb_tot, per)]
                for (rb, nrb, c0, cw) in chunks:
                    i = k % 3
                    n = nrb * cw
                    sap = DAP(src, soff + rb * 128 * ncols + c0, [(ncols, 128), (128 * ncols, nrb), (1, cw)])
                    dap = DAP(dst, doff + rb * 128 * ncols + c0, [(ncols, 128), (128 * ncols, nrb), (1, cw)])
                    v32 = stg[i][:, 0:n].rearrange("p (r c) -> p r c", r=nrb)
                    v16 = stgb[i][:, 0:n].rearrange("p (r c) -> p r c", r=nrb)
                    P.dma(v32, sap, w=[sB[i]], sb=sB[i])
                    if gsel is not None:
                        sc = gt[:, gsel[0], gsel[1], rb:rb + 1]
                        rr = [sB[i], gB]
                    else:
                        sc = 1.0
                        rr = [sB[i]]
                    if k % 2 == 0:
                        P.op(P.act, lambda o=stgb[i][:, 0:n], a=stg[i][:, 0:n], s=sc:
                             nc.scalar.activation(out=o, in_=a, func=AF.Copy, scale=s), r=rr, w=[sBb[i]])
                    else:
                        if gsel is not None:
                            P.op(P.dve, lambda o=stgb[i][:, 0:n], a=stg[i][:, 0:n], s=sc:
                                 nc.vector.tensor_scalar(out=o, in0=a, scalar1=s, scalar2=None, op0=ALU.mult),
                                 r=rr, w=[sBb[i]])
                        else:
                            P.op(P.dve, lambda o=stgb[i][:, 0:n], a=stg[i][:, 0:n]:
                                 nc.vector.tensor_copy(out=o, in_=a), r=rr, w=[sBb[i]])
                    P.dma(dap, v16, r=[sBb[i]], sb=sBb[i])
                    k += 1
            P.barrier()
            P.flush()
            P.release(sB + sBb + [gB])

    def norm_transpose(self, x_ap, xB, hT, hTB, col0, tmp):
        nc, P = self.nc, self.P
        junk, jB, ss, ssB, hb, hbB, tps, tpsB, ident, eps_t, cB = tmp
        P.op(P.act, lambda: nc.scalar.activation(out=junk[:], in_=x_ap, func=AF.Square, accum_out=ss[:, 0:1]),
             r=[xB], w=[jB, ssB])
        P.op(P.act, lambda: nc.scalar.activation(out=ss[:, 1:2], in_=ss[:, 0:1], func=AF.Sqrt, scale=1.0 / D,
                                                 bias=eps_t[:, 0:1]), r=[ssB, cB], w=[ssB])
        P.op(P.dve, lambda: nc.vector.reciprocal(out=ss[:, 2:3], in_=ss[:, 1:2]), r=[ssB], w=[ssB])
        P.op(P.act, lambda: nc.scalar.activation(out=hb[:], in_=x_ap, func=AF.Copy, scale=ss[:, 2:3]),
             r=[xB, ssB], w=[hbB])
        tp = tps[:].bitcast(BF16)
        for c in range(8):
            P.op(P.pe, lambda c=c: nc.tensor.transpose(tp[:, c * 128:(c + 1) * 128], hb[:, c * 128:(c + 1) * 128],
                                                      ident[:]), r=[hbB, cB], w=[tpsB])
        P.op(P.dve, lambda: nc.vector.tensor_copy(out=hT[:, :, col0:col0 + 128],
                                                  in_=tp.rearrange("p (c t) -> p c t", c=8)),
             r=[tpsB], w=[hTB])

    def ffn(self, l, s, src, dst):
        nc, P, T = self.nc, self.P, self.T
        TT = 256
        with ExitStack() as st:
            wup = self.sb(st, "wup", [128, 8, 4096], BF16)
            wdn = self.sb(st, "wdn", [128, 32, 1024], BF16)
            wupB, wdnB = Buf("wup"), Buf("wdn")
            ident = self.sb(st, "ident", [128, 128], BF16)
            eps_t = self.sb(st, "eps", [128, 1], F32)
            cB = Buf("c")
            xt = [self.sb(st, f"xt{i}", [128, 2, D], F32) for i in range(2)]
            xB = P.bufs(2, "xt")
            hT = self.sb(st, "hT", [128, 8, TT], BF16)
            hTB = Buf("hT")
            hid = self.sb(st, "hid", [128, 32, TT], BF16)
            hidB = Buf("hid")
            junk = self.sb(st, "junk", [128, D], BF16)
            ss = self.sb(st, "ss", [128, 4], F32)
            hb = self.sb(st, "hb", [128, D], BF16)
            rl = [self.sb(st, f"rl{i}", [128, TT], F32) for i in range(2)]
            rlB = P.bufs(2, "rl")
            ot = [self.sb(st, f"ot{i}", [128, 2, D], F32) for i in range(2)]
            otB = P.bufs(2, "ot")
            banks = self.psum_banks(st)
            bB = P.bufs(8, "bank")
            tmp = (junk, Buf(), ss, Buf(), hb, Buf(), banks[7], bB[7], ident, eps_t, cB)
            P.dma(ident[:], self.c_ident.ap(), w=[cB], sb=cB)
            P.op(P.dve, lambda: nc.vector.memset(eps_t[:], EPS), w=[cB])
            for k0 in range(0, 8, 4):
                P.dma(wup[:, k0:k0 + 4, :], DAP(self.wb_up, (l * D + k0 * 128) * 4096,
                                                 [(4096, 128), (128 * 4096, 4), (1, 4096)]), w=[wupB], sb=wupB)
            for f0 in range(0, 32, 8):
                P.dma(wdn[:, f0:f0 + 8, :], DAP(self.wb_down, (l * 4096 + f0 * 128) * D,
                                                 [(D, 128), (128 * D, 8), (1, D)]), w=[wdnB], sb=wdnB)
            ntile = T // TT

            def load(i):
                b = i % 2
                P.dma(xt[b][:], DAP(src, (s * T + i * TT) * D, [(D, 128), (128 * D, 2), (1, D)]), w=[xB[b]], sb=xB[b])

            load(0)
            for i in range(ntile):
                b = i % 2
                if i + 1 < ntile:
                    load(i + 1)
                for su in range(2):
                    self.norm_transpose(xt[b][:, su, :], xB[b], hT, hTB, su * 128, tmp)
                for fc in range(32):
                    pb = fc % 3
                    for kc in range(8):
                        P.op(P.pe, lambda fc=fc, kc=kc, pb=pb: nc.tensor.matmul(
                            banks[pb][:, 0:TT], lhsT=wup[:, kc, fc * 128:(fc + 1) * 128], rhs=hT[:, kc, :],
                            start=(kc == 0), stop=(kc == 7)), r=[wupB, hTB], w=[bB[pb]])
                    rb = fc % 2
                    P.op(P.act, lambda pb=pb, rb=rb: nc.scalar.activation(out=rl[rb][:], in_=banks[pb][:, 0:TT],
                                                                      func=AF.Relu), r=[bB[pb]], w=[rlB[rb]])
                    P.op(P.dve, lambda fc=fc, rb=rb: nc.vector.tensor_tensor(out=hid[:, fc, :], in0=rl[rb][:],
                                                                           in1=rl[rb][:], op=ALU.mult),
                         r=[rlB[rb]], w=[hidB])
                for su in range(2):
                    for hf in range(2):
                        pb = 3 + (su * 2 + hf)
                        for fc in range(32):
                            P.op(P.pe, lambda fc=fc, su=su, hf=hf, pb=pb: nc.tensor.matmul(
                                banks[pb][:], lhsT=hid[:, fc, su * 128:(su + 1) * 128],
                                rhs=wdn[:, fc, hf * 512:(hf + 1) * 512], start=(fc == 0), stop=(fc == 31)),
                                r=[hidB, wdnB], w=[bB[pb]])
                        P.op(P.dve, lambda su=su, hf=hf, pb=pb, b=b: nc.vector.tensor_tensor(
                            out=ot[b][:, su, hf * 512:(hf + 1) * 512], in0=banks[pb][:],
                            in1=xt[b][:, su, hf * 512:(hf + 1) * 512], op=ALU.add), r=[bB[pb], xB[b]], w=[otB[b]])
                P.dma(DAP(dst, (s * T + i * TT) * D, [(D, 128), (128 * D, 2), (1, D)]), ot[b][:], r=[otB[b]],
                      sb=otB[b])
            P.barrier()
            P.flush()
            P.release(xB + otB + [wupB, wdnB, cB])


def _bf16(a):
    import ml_dtypes
    return np.asarray(a, dtype=np.float32).astype(ml_dtypes.bfloat16)


def make_consts(T):
    c = {}
    c["c_ident"] = _bf16(np.eye(128))
    return c


def _merge(self, l, s, src, dst):
    nc, P, T = self.nc, self.P, self.T
    TT = 512
    with ExitStack() as st:
        wg = self.sb(st, "wg", [128, 8, 4096], BF16)
        wbr = self.sb(st, "wbr", [128, 10, 1024], BF16)
        wo = self.sb(st, "wo", [128, 8, 1024], BF16)
        wB = Buf("w")
        ident = self.sb(st, "ident", [128, 128], BF16)
        eps_t = self.sb(st, "eps", [128, 1], F32)
        cB = Buf("c")
        xt = [self.sb(st, f"xt{i}", [128, 4, D], F32) for i in range(2)]
        xB = P.bufs(2, "xt")
        ot = [self.sb(st, f"oTt{i}", [128, 10, TT], BF16) for i in range(2)]
        oB = P.bufs(2, "oTt")
        hT = self.sb(st, "hT", [128, 8, TT], BF16)
        hTB = Buf("hT")
        mg = self.sb(st, "mg", [128, 8, TT], BF16)
        mgB = Buf("mg")
        junk = self.sb(st, "junk", [128, D], BF16)
        ss = self.sb(st, "ss", [128, 4], F32)
        hb = self.sb(st, "hb", [128, D], BF16)
        sg = [self.sb(st, f"sg{i}", [128, TT], F32) for i in range(2)]
        sgB = P.bufs(2, "sg")
        tt = [self.sb(st, f"tt{i}", [128, TT], F32) for i in range(2)]
        ttB = P.bufs(2, "tt")
        acc = self.sb(st, "acc", [128, TT], F32)
        accB = Buf("acc")
        banks = self.psum_banks(st)
        bB = P.bufs(8, "bank")
        tmp = (junk, Buf(), ss, Buf(), hb, Buf(), banks[7], bB[7], ident, eps_t, cB)
        P.dma(ident[:], self.c_ident.ap(), w=[cB], sb=cB)
        P.op(P.dve, lambda: nc.vector.memset(eps_t[:], EPS), w=[cB])
        for k0 in range(0, 8, 4):
            P.dma(wg[:, k0:k0 + 4, :], DAP(self.wb_in, (l * D + k0 * 128) * 7168 + 3072,
                                           [(7168, 128), (128 * 7168, 4), (1, 4096)]), w=[wB], sb=wB)
        P.dma(wbr[:], DAP(self.wb_br, l * 1280 * D, [(D, 128), (128 * D, 10), (1, D)]), w=[wB], sb=wB)
        P.dma(wo[:], DAP(self.wb_o, l * D * D, [(D, 128), (128 * D, 8), (1, D)]), w=[wB], sb=wB)
        ntile = T // TT
        brk = [(0, 2), (2, 6), (6, 8), (8, 10)]

        def load(i):
            b = i % 2
            P.dma(xt[b][:], DAP(src, (s * T + i * TT) * D, [(D, 128), (128 * D, 4), (1, D)]), w=[xB[b]], sb=xB[b])
            P.dma(ot[b][:], DAP(self.oT, i * TT, [(T, 128), (128 * T, 10), (1, TT)]), w=[oB[b]], sb=oB[b])

        load(0)
        k = 0
        for i in range(ntile):
            b = i % 2
            if i + 1 < ntile:
                load(i + 1)
            for su in range(4):
                self.norm_transpose(xt[b][:, su, :], xB[b], hT, hTB, su * 128, tmp)
            for dc in range(8):
                for br in range(4):
                    gb = k % 2
                    bb = 2 + k % 2
                    k += 1
                    for kc in range(8):
                        P.op(P.pe, lambda kc=kc, br=br, dc=dc, gb=gb: nc.tensor.matmul(
                            banks[gb][:], lhsT=wg[:, kc, br * 1024 + dc * 128: br * 1024 + (dc + 1) * 128],
                            rhs=hT[:, kc, :], start=(kc == 0), stop=(kc == 7)), r=[wB, hTB], w=[bB[gb]])
                    k0, k1 = brk[br]
                    for kc in range(k0, k1):
                        P.op(P.pe, lambda kc=kc, dc=dc, bb=bb, b=b, k0=k0, k1=k1: nc.tensor.matmul(
                            banks[bb][:], lhsT=wbr[:, kc, dc * 128:(dc + 1) * 128], rhs=ot[b][:, kc, :],
                            start=(kc == k0), stop=(kc == k1 - 1)), r=[wB, oB[b]], w=[bB[bb]])
                    P.op(P.act, lambda gb=gb: nc.scalar.activation(out=sg[gb][:], in_=banks[gb][:], func=AF.Sigmoid),
                         r=[bB[gb]], w=[sgB[gb]])
                    if br == 0:
                        P.op(P.dve, lambda gb=gb, bb=bb: nc.vector.tensor_tensor(
                            out=acc[:], in0=banks[bb][:], in1=sg[gb][:], op=ALU.mult),
                            r=[bB[bb], sgB[gb]], w=[accB])
                    else:
                        P.op(P.dve, lambda gb=gb, bb=bb: nc.vector.tensor_tensor(
                            out=tt[gb][:], in0=banks[bb][:], in1=sg[gb][:], op=ALU.mult),
                            r=[bB[bb], sgB[gb]], w=[ttB[gb]])
                        if br < 3:
                            P.op(P.pool, lambda gb=gb: nc.gpsimd.tensor_tensor(
                                out=acc[:], in0=acc[:], in1=tt[gb][:], op=ALU.add), r=[ttB[gb], accB], w=[accB])
                        else:
                            P.op(P.pool, lambda gb=gb, dc=dc: nc.gpsimd.tensor_tensor(
                                out=mg[:, dc, :], in0=acc[:], in1=tt[gb][:], op=ALU.add),
                                r=[ttB[gb], accB], w=[mgB])
            for su in range(4):
                for hf in range(2):
                    pb = 4 + (su * 2 + hf) % 3
                    for dc in range(8):
                        P.op(P.pe, lambda dc=dc, su=su, hf=hf, pb=pb: nc.tensor.matmul(
                            banks[pb][:], lhsT=mg[:, dc, su * 128:(su + 1) * 128],
                            rhs=wo[:, dc, hf * 512:(hf + 1) * 512], start=(dc == 0), stop=(dc == 7)),
                            r=[mgB, wB], w=[bB[pb]])
                    P.op(P.dve, lambda su=su, hf=hf, pb=pb, b=b: nc.vector.tensor_tensor(
                        out=xt[b][:, su, hf * 512:(hf + 1) * 512], in0=banks[pb][:],
                        in1=xt[b][:, su, hf * 512:(hf + 1) * 512], op=ALU.add), r=[bB[pb], xB[b]], w=[xB[b]])
            P.dma(DAP(dst, (s * T + i * TT) * D, [(D, 128), (128 * D, 4), (1, D)]), xt[b][:], r=[xB[b]], sb=xB[b])
        P.barrier()
        P.flush()
        P.release(xB + oB + [wB, cB])


Builder.merge = _merge


def _declare_mix(self):
    T = self.T
    self.c_cos = self.din("c_cos", [128, self.NT, 32])
    self.c_sin = self.din("c_sin", [128, self.NT, 32])
    self.c_rmask = self.din("c_rmask", [128, 512])
    self.zp = self.dscr("zp", [T, 256], BF16)
    self.qT = self.dscr("qT", [512, T], BF16)
    self.kT = self.dscr("kT", [128, T], BF16)
    self.gv = self.dscr("gv", [T, 128], BF16)
    self.nqkT = self.dscr("nqkT", [512, T], BF16)
    self.nv = self.dscr("nv", [T, 256], BF16)
    self.hv = self.dscr("hv", [T, 256], BF16)
    self.QT = self.dscr("QT", [2, 256, T], BF16)
    self.KT = self.dscr("KT", [2, 256, T], BF16)
    self.Kp = self.dscr("Kp", [2, T, 256], BF16)
    self.aT = self.dscr("aT", [2, 256, T // 32], F32)
    self.gateT = self.dscr("gateT", [256, T], BF16)
    self.ofw = self.dscr("ofw", [256, T], F32)
    self.obw = self.dscr("obw", [256, T], F32)


Builder.declare_mix = _declare_mix


def _proj(self, l, s, src):
    nc, P, T = self.nc, self.P, self.T
    TT = 512
    W = self.w
    with ExitStack() as st:
        w1 = self.sb(st, "w1", [128, 8, 3072], BF16)
        wB = Buf("w1")
        ident = self.sb(st, "ident", [128, 128], BF16)
        eps_t = self.sb(st, "eps", [128, 1], F32)
        cosT = self.sb(st, "cosT", [128, self.NT, 32], F32)
        sinT = self.sb(st, "sinT", [128, self.NT, 32], F32)
        rmask = self.sb(st, "rmask", [128, 512], F32)
        GQ = self.sb(st, "GQ", [128, 10, 64], F32)
        GN = self.sb(st, "GN", [128, 8, 64], F32)
        lbr = self.sb(st, "lbr", [128, 2, 2, 2], F32)
        lbv = self.sb(st, "lbv", [128, 2, 2], F32)
        oml = self.sb(st, "oml", [128, 2, 2], F32)
        ong = self.sb(st, "ong", [128, 1], F32)
        cB = Buf("c")
        xt = [self.sb(st, f"xt{i}", [128, 4, D], F32) for i in range(2)]
        xB = P.bufs(2, "xt")
        hT = self.sb(st, "hT", [128, 8, TT], BF16)
        hTB = Buf("hT")
        junk = self.sb(st, "junk", [128, D], BF16)
        ss = self.sb(st, "ss", [128, 4], F32)
        hb = self.sb(st, "hb", [128, D], BF16)
        sq = self.sb(st, "sq", [128, 640], F32)
        st8 = self.sb(st, "st8", [128, 3, 16], F32)
        xn = self.sb(st, "xn", [128, 640], F32)
        xg = self.sb(st, "xg", [128, 640], F32)
        t1 = self.sb(st, "t1", [128, 640], F32)
        t2 = self.sb(st, "t2", [128, 640], F32)
        qr = self.sb(st, "qr", [128, 640], BF16)
        sqB, st8B, xnB, xgB, t1B, t2B, qrB = P.bufs(7, "tm")
        qTs = self.sb(st, "qTs", [128, 5, TT], BF16)
        nTs = self.sb(st, "nTs", [128, 4, TT], BF16)
        zps = self.sb(st, "zps", [128, 4, 256], BF16)
        gvs = self.sb(st, "gvs", [128, 4, 128], BF16)
        s3 = self.sb(st, "s3", [128, 4, 512], BF16)
        qTsB, nTsB, zpsB, gvsB, s3B = P.bufs(5, "stg")
        hf = [self.sb(st, f"hf{i}", [128, TT], F32) for i in range(8)]
        hfB = P.bufs(8, "hf")
        hbf = [self.sb(st, f"hbf{i}", [128, TT], BF16) for i in range(4)]
        hbB = P.bufs(4, "hbf")
        a_s = self.sb(st, "a_s", [128, 16], F32)
        a_sB = Buf("a_s")
        kps = self.sb(st, "kps", [128, 4, 128], BF16)
        kpsB = Buf("kps")
        banks = self.psum_banks(st)
        bB = P.bufs(8, "bank")
        tmp = (junk, Buf(), ss, Buf(), hb, Buf(), banks[7], bB[7], ident, eps_t, cB)

        P.dma(ident[:], self.c_ident.ap(), w=[cB], sb=cB)
        P.dma(cosT[:], self.c_cos.ap(), w=[cB], sb=cB)
        P.dma(sinT[:], self.c_sin.ap(), w=[cB], sb=cB)
        P.dma(rmask[:], self.c_rmask.ap(), w=[cB], sb=cB)
        P.dma(GQ[:, 0:8, :], DAP(W["gqa_qnorm"], l * 64, [(0, 128), (0, 8), (1, 64)]), w=[cB], sb=cB)
        P.dma(GQ[:, 8:10, :], DAP(W["gqa_knorm"], l * 64, [(0, 128), (0, 2), (1, 64)]), w=[cB], sb=cB)
        P.dma(GN[:, 0:4, :], DAP(W["nat_qnorm"], l * 64, [(0, 128), (0, 4), (1, 64)]), w=[cB], sb=cB)
        P.dma(GN[:, 4:8, :], DAP(W["nat_knorm"], l * 64, [(0, 128), (0, 4), (1, 64)]), w=[cB], sb=cB)
        P.dma(lbr[:], DAP(W["hgrn_lb"], 0, [(1, 128), (512, 2), (256, 2), (128, 2)]), w=[cB], sb=cB, slow=True)
        for hh in range(2):
            P.dma(ong[hh * 64:(hh + 1) * 64, :], DAP(W["hgrn_onorm"], l * 64, [(1, 64), (1, 1)]), w=[cB], sb=cB,
                  slow=True)
        P.op(P.dve, lambda: nc.vector.memset(eps_t[:], EPS), w=[cB])
        P.op(P.dve, lambda: nc.vector.tensor_scalar(out=GQ[:, 0:8, :], in0=GQ[:, 0:8, :], scalar1=0.125,
                                                    scalar2=None, op0=ALU.mult), r=[cB], w=[cB])
        P.op(P.dve, lambda: nc.vector.tensor_scalar(out=GN[:, 0:4, :], in0=GN[:, 0:4, :], scalar1=0.125,
                                                    scalar2=None, op0=ALU.mult), r=[cB], w=[cB])
        if l == 0:
            P.op(P.dve, lambda: nc.vector.memset(lbv[:], 0.0), w=[cB])
            P.op(P.dve, lambda: nc.vector.memset(oml[:], 1.0), w=[cB])
        else:
            P.op(P.dve, lambda: nc.vector.tensor_tensor(out=oml[:], in0=lbr[:, 1, :, :], in1=lbr[:, 0, :, :],
                                                        op=ALU.subtract), r=[cB], w=[cB])
            P.op(P.act, lambda: nc.scalar.activation(out=lbv[:], in_=oml[:], func=AF.Sigmoid), r=[cB], w=[cB])
            P.op(P.dve, lambda: nc.vector.tensor_scalar(out=oml[:], in0=lbv[:], scalar1=-1.0, scalar2=1.0,
                                                        op0=ALU.mult, op1=ALU.add), r=[cB], w=[cB])
        for kc in range(8):
            P.dma(w1[:, kc, :], DAP(self.wb_in, (l * D + kc * 128) * 7168, [(7168, 128), (1, 3072)]), w=[wB], sb=wB)
        ntile = T // TT

        def load(i):
            b = i % 2
            P.dma(xt[b][:], DAP(src, (s * T + i * TT) * D, [(D, 128), (128 * D, 4), (1, D)]), w=[xB[b]], sb=xB[b])

        def mm_tok(bank, c0, c1, col0, ncol, su):
            for kc in range(8):
                P.op(P.pe, lambda kc=kc: nc.tensor.matmul(
                    banks[bank][:, c0:c1], lhsT=hT[:, kc, su * 128:(su + 1) * 128],
                    rhs=w1[:, kc, col0:col0 + ncol], start=(kc == 0), stop=(kc == 7)),
                    r=[wB, hTB], w=[bB[bank]])

        def headnorm(src_ap, srcB, H, gtab, o_off):
            n = H * 64
            P.op(P.act, lambda: nc.scalar.activation(out=sq[:, 0:n], in_=src_ap, func=AF.Square), r=[srcB], w=[sqB])
            P.op(P.dve, lambda: nc.vector.tensor_reduce(out=st8[:, 0, 0:H], in_=sq[:, 0:n].rearrange(
                "p (h d) -> p h d", h=H), op=ALU.add, axis=AX.X), r=[sqB], w=[st8B])
            P.op(P.act, lambda: nc.scalar.activation(out=st8[:, 1, 0:H], in_=st8[:, 0, 0:H], func=AF.Sqrt,
                                                     scale=1.0 / 64, bias=eps_t[:, 0:1]), r=[st8B, cB], w=[st8B])
            P.op(P.dve, lambda: nc.vector.reciprocal(out=st8[:, 2, 0:H], in_=st8[:, 1, 0:H]), r=[st8B], w=[st8B])
            P.op(P.dve, lambda: nc.vector.tensor_tensor(
                out=xn[:, 0:n].rearrange("p (h d) -> p h d", h=H), in0=src_ap.rearrange("p (h d) -> p h d", h=H),
                in1=st8[:, 2, 0:H].unsqueeze(2).broadcast_to([128, H, 64]), op=ALU.mult),
                r=[srcB, st8B], w=[xnB])
            P.op(P.pool, lambda: nc.gpsimd.tensor_tensor(
                out=xg[:, o_off * 64:(o_off + H) * 64], in0=xn[:, 0:n],
                in1=gtab.rearrange("p h d -> p (h d)"), op=ALU.mult), r=[xnB, cB], w=[xgB])

        load(0)
        for i in range(ntile):
            b = i % 2
            if i + 1 < ntile:
                load(i + 1)
            for su in range(4):
                self.norm_transpose(xt[b][:, su, :], xB[b], hT, hTB, su * 128, tmp)
            for su in range(4):
                nt = i * 4 + su
                mm_tok(0, 0, 512, 256, 512, su)
                mm_tok(1, 0, 256, 0, 256, su)
                mm_tok(1, 256, 512, 768, 256, su)
                mm_tok(2, 0, 512, 2304, 512, su)
                mm_tok(3, 0, 256, 1536, 256, su)
                mm_tok(3, 256, 512, 2816, 256, su)
                P.op(P.act, lambda su=su: nc.scalar.copy(out=zps[:, su, :], in_=banks[1][:, 0:256]),
                     r=[bB[1]], w=[zpsB])
                P.op(P.act, lambda su=su: nc.scalar.copy(out=gvs[:, su, :], in_=banks[1][:, 384:512]),
                     r=[bB[1]], w=[gvsB])
                P.op(P.act, lambda su=su: nc.scalar.copy(out=s3[:, su, :], in_=banks[3][:]), r=[bB[3]], w=[s3B])
                headnorm(banks[0][:, 0:512], bB[0], 8, GQ[:, 0:8, :], 0)
                headnorm(banks[1][:, 256:384], bB[1], 2, GQ[:, 8:10, :], 8)
                xg4 = xg[:, 0:640].rearrange("p (h t d) -> p h t d", h=10, t=2)
                t14 = t1[:, 0:640].rearrange("p (h t d) -> p h t d", h=10, t=2)
                t24 = t2[:, 0:640].rearrange("p (h t d) -> p h t d", h=10, t=2)
                qr4 = qr[:, 0:640].rearrange("p (h t d) -> p h t d", h=10, t=2)
                cb4 = cosT[:, nt, :].unsqueeze(1).unsqueeze(1).broadcast_to([128, 10, 2, 32])
                sb3 = sinT[:, nt, :].unsqueeze(1).broadcast_to([128, 10, 32])
                P.op(P.dve, lambda xg4=xg4, t14=t14, cb4=cb4: nc.vector.tensor_tensor(
                    out=t14, in0=xg4, in1=cb4, op=ALU.mult), r=[xgB, cB], w=[t1B])
                P.op(P.pool, lambda xg4=xg4, t24=t24, sb3=sb3: nc.gpsimd.tensor_tensor(
                    out=t24[:, :, 0, :], in0=xg4[:, :, 1, :], in1=sb3, op=ALU.mult), r=[xgB, cB], w=[t2B])
                P.op(P.pool, lambda xg4=xg4, t24=t24, sb3=sb3: nc.gpsimd.tensor_tensor(
                    out=t24[:, :, 1, :], in0=xg4[:, :, 0, :], in1=sb3, op=ALU.mult), r=[xgB, cB, t2B], w=[t2B])
                P.op(P.dve, lambda t14=t14, t24=t24, qr4=qr4: nc.vector.tensor_tensor(
                    out=qr4[:, :, 0, :], in0=t14[:, :, 0, :], in1=t24[:, :, 0, :], op=ALU.subtract),
                    r=[t1B, t2B], w=[qrB])
                P.op(P.dve, lambda t14=t14, t24=t24, qr4=qr4: nc.vector.tensor_tensor(
                    out=qr4[:, :, 1, :], in0=t14[:, :, 1, :], in1=t24[:, :, 1, :], op=ALU.add),
                    r=[t1B, t2B, qrB], w=[qrB])
                tp = banks[6][:].bitcast(BF16)
                for pr in range(5):
                    P.op(P.pe, lambda pr=pr, tp=tp: nc.tensor.transpose(
                        tp[:, pr * 128:(pr + 1) * 128], qr[:, pr * 128:(pr + 1) * 128], ident[:]),
                        r=[qrB, cB], w=[bB[6]])
                P.op(P.act, lambda su=su, tp=tp: nc.scalar.copy(
                    out=qTs[:, :, su * 128:(su + 1) * 128], in_=tp[:, 0:640].rearrange("p (c t) -> p c t", c=5)),
                    r=[bB[6]], w=[qTsB])
                headnorm(banks[2][:, 0:512], bB[2], 8, GN[:], 0)
                P.op(P.act, lambda: nc.scalar.copy(out=qr[:, 0:512], in_=xg[:, 0:512]), r=[xgB], w=[qrB])
                for pr in range(4):
                    P.op(P.pe, lambda pr=pr, tp=tp: nc.tensor.transpose(
                        tp[:, pr * 128:(pr + 1) * 128], qr[:, pr * 128:(pr + 1) * 128], ident[:]),
                        r=[qrB, cB], w=[bB[6]])
                P.op(P.act, lambda su=su, tp=tp: nc.scalar.copy(
                    out=nTs[:, :, su * 128:(su + 1) * 128], in_=tp[:, 0:512].rearrange("p (c t) -> p c t", c=4)),
                    r=[bB[6]], w=[nTsB])
            t0 = i * TT
            P.dma(DAP(self.zp, t0 * 256, [(256, 128), (128 * 256, 4), (1, 256)]), zps[:], r=[zpsB], sb=zpsB)
            P.dma(DAP(self.gv, t0 * 128, [(128, 128), (128 * 128, 4), (1, 128)]), gvs[:], r=[gvsB], sb=gvsB)
            P.dma(DAP(self.hv, t0 * 256, [(256, 128), (128 * 256, 4), (1, 256)]), s3[:, :, 0:256], r=[s3B], sb=s3B)
            P.dma(DAP(self.nv, t0 * 256, [(256, 128), (128 * 256, 4), (1, 256)]), s3[:, :, 256:512], r=[s3B], sb=s3B)
            P.dma(DAP(self.qT, t0, [(T, 128), (128 * T, 4), (1, TT)]), qTs[:, 0:4, :], r=[qTsB], sb=qTsB)
            P.dma(DAP(self.kT, t0, [(T, 128), (1, TT)]), qTs[:, 4, :], r=[qTsB], sb=qTsB)
            P.dma(DAP(self.nqkT, t0, [(T, 128), (128 * T, 4), (1, TT)]), nTs[:], r=[nTsB], sb=nTsB)
            def mm_feat(bank, col0):
                for kc in range(8):
                    P.op(P.pe, lambda kc=kc: nc.tensor.matmul(
                        banks[bank][:], lhsT=w1[:, kc, col0:col0 + 128], rhs=hT[:, kc, :],
                        start=(kc == 0), stop=(kc == 7)), r=[wB, hTB], w=[bB[bank]])

            for hp in range(2):
                silq, silqB = hf[0], hfB[0]
                mm_feat(4, 1792 + hp * 128)
                P.op(P.act, lambda: nc.scalar.activation(out=hf[0][:], in_=banks[4][:], func=AF.Silu),
                     r=[bB[4]], w=[hfB[0]])
                mm_feat(5, 2048 + hp * 128)
                P.op(P.act, lambda: nc.scalar.activation(out=hf[1][:], in_=banks[5][:], func=AF.Silu),
                     r=[bB[5]], w=[hfB[1]])
                P.op(P.dve, lambda: nc.vector.tensor_scalar(out=hbf[0][:], in0=hf[1][:], scalar1=ong[:, 0:1],
                                                            scalar2=None, op0=ALU.mult), r=[hfB[1], cB], w=[hbB[0]])
                P.dma(DAP(self.gateT, hp * 128 * T + t0, [(T, 128), (1, TT)]), hbf[0][:], r=[hbB[0]], sb=hbB[0])
                for d in range(2):
                    bk = 4 + d
                    mm_feat(bk, 1024 + d * 256 + hp * 128)
                    sig, f_, lg, ky, bb, eb, enb, kt = hf[1], hf[2], hf[3], hf[4], hf[5], hf[6], hf[7], hf[1]
                    P.op(P.act, lambda bk=bk: nc.scalar.activation(out=hf[1][:], in_=banks[bk][:], func=AF.Sigmoid),
                         r=[bB[bk]], w=[hfB[1]])
                    P.op(P.dve, lambda d=d, hp=hp: nc.vector.tensor_scalar(
                        out=hf[2][:], in0=hf[1][:], scalar1=oml[:, d, hp:hp + 1], scalar2=lbv[:, d, hp:hp + 1],
                        op0=ALU.mult, op1=ALU.add), r=[hfB[1], cB], w=[hfB[2]])
                    P.op(P.act, lambda: nc.scalar.activation(out=hf[3][:], in_=hf[2][:], func=AF.Ln),
                         r=[hfB[2]], w=[hfB[3]])
                    P.op(P.pool, lambda: nc.gpsimd.tensor_scalar(out=hf[4][:], in0=hf[2][:], scalar1=-1.0,
                                                                 scalar2=1.0, op0=ALU.mult, op1=ALU.add),
                         r=[hfB[2]], w=[hfB[4]])
                    P.op(P.dve, lambda: nc.vector.tensor_tensor_scan(
                        out=hf[5][:], data0=rmask[:], data1=hf[3][:], initial=0.0, op0=ALU.mult, op1=ALU.add),
                        r=[hfB[3], cB], w=[hfB[5]])
                    if d == 1:
                        P.op(P.pool, lambda: nc.gpsimd.tensor_tensor(out=hf[3][:], in0=hf[3][:], in1=hf[5][:],
                                                                     op=ALU.subtract), r=[hfB[3], hfB[5]], w=[hfB[3]])
                        P.op(P.dve, lambda: nc.vector.tensor_tensor(
                            out=hf[5][:].rearrange("p (n c) -> p n c", c=32),
                            in0=hf[3][:].rearrange("p (n c) -> p n c", c=32),
                            in1=hf[5][:].rearrange("p (n c) -> p n c", c=32)[:, :, 31:32].broadcast_to([128, 16, 32]),
                            op=ALU.add), r=[hfB[3], hfB[5]], w=[hfB[5]])
                    P.op(P.act, lambda: nc.scalar.activation(out=hf[6][:], in_=hf[5][:], func=AF.Exp),
                         r=[hfB[5]], w=[hfB[6]])
                    P.op(P.act, lambda: nc.scalar.activation(out=hf[7][:], in_=hf[5][:], func=AF.Exp, scale=-1.0),
                         r=[hfB[5]], w=[hfB[7]])
                    P.op(P.dve, lambda: nc.vector.tensor_tensor(out=hbf[1][:], in0=hf[0][:], in1=hf[6][:],
                                                                op=ALU.mult), r=[hfB[0], hfB[6]], w=[hbB[1]])
                    P.op(P.pool, lambda: nc.gpsimd.tensor_tensor(out=hf[1][:], in0=hf[4][:], in1=hf[7][:],
                                                                 op=ALU.mult), r=[hfB[4], hfB[7]], w=[hfB[1]])
                    P.op(P.act, lambda: nc.scalar.copy(out=hbf[2][:], in_=hf[1][:]), r=[hfB[1]], w=[hbB[2]])
                    aidx = 31 if d == 0 else 0
                    eb3 = hf[6][:].rearrange("p (n c) -> p n c", c=32)
                    P.op(P.dve, lambda eb3=eb3, aidx=aidx: nc.vector.tensor_tensor(
                        out=hbf[3][:].rearrange("p (n c) -> p n c", c=32),
                        in0=hf[1][:].rearrange("p (n c) -> p n c", c=32),
                        in1=eb3[:, :, aidx:aidx + 1].broadcast_to([128, 16, 32]), op=ALU.mult),
                        r=[hfB[1], hfB[6]], w=[hbB[3]])
                    P.op(P.act, lambda eb3=eb3, aidx=aidx: nc.scalar.copy(
                        out=a_s[:].unsqueeze(2), in_=eb3[:, :, aidx:aidx + 1]), r=[hfB[6]], w=[a_sB])
                    NCH = T // 32
                    P.dma(DAP(self.aT, (d * 256 + hp * 128) * NCH + i * 16, [(NCH, 128), (1, 16)]), a_s[:],
                          r=[a_sB], sb=a_sB)
                    P.dma(DAP(self.QT, (d * 256 + hp * 128) * T + t0, [(T, 128), (1, TT)]), hbf[1][:],
                          r=[hbB[1]], sb=hbB[1])
                    P.dma(DAP(self.KT, (d * 256 + hp * 128) * T + t0, [(T, 128), (1, TT)]), hbf[2][:],
                          r=[hbB[2]], sb=hbB[2])
                    tp = banks[6][:].bitcast(BF16)
                    for su in range(4):
                        P.op(P.pe, lambda su=su, tp=tp: nc.tensor.transpose(
                            tp[:, su * 128:(su + 1) * 128], hbf[3][:, su * 128:(su + 1) * 128], ident[:]),
                            r=[hbB[3], cB], w=[bB[6]])
                    P.op(P.act, lambda tp=tp: nc.scalar.copy(
                        out=kps[:], in_=tp[:, 0:512].rearrange("p (c t) -> p c t", c=4)), r=[bB[6]], w=[kpsB])
                    P.dma(DAP(self.Kp, (d * T + t0) * 256 + hp * 128, [(256, 128), (128 * 256, 4), (1, 128)]),
                          kps[:], r=[kpsB], sb=kpsB)
        P.barrier()
        P.flush()
        P.release(xB + [wB, cB, zpsB, gvsB, s3B, qTsB, nTsB, a_sB, kpsB] + hbB)


Builder.proj = _proj


def make_consts(T):
    NT = T // 128
    c = {}
    c["c_ident"] = _bf16(np.eye(128))
    pos = (np.arange(NT)[None, :] * 128 + np.arange(128)[:, None]).astype(np.float32)
    half = 32
    inv = (10000.0 ** (-np.arange(half, dtype=np.float32) / half)).astype(np.float32)
    ang = pos[:, :, None] * inv[None, None, :]
    c["c_cos"] = np.cos(ang).astype(np.float32)
    c["c_sin"] = np.sin(ang).astype(np.float32)
    rm = np.ones((128, 512), np.float32)
    rm[:, ::32] = 0.0
    c["c_rmask"] = rm
    return c


def _declare_p2(self):
    self.c_poolA = self.din("c_poolA", [128, 36, 128], BF16)
    self.c_gmask = self.din("c_gmask", [128, 2, 512], BF16)
    self.c_nmask = self.din("c_nmask", [128, 25, 128], BF16)
    self.c_identf = self.din("c_identf", [128, 128], F32)
    self.c_hmask = self.din("c_hmask", [128, 2, 512], BF16)
    self.c_cm = self.din("c_cm", [128, 4], BF16)
    self.c_bd = self.din("c_bd", [128, 128], BF16)
    self.rep2 = self.dscr("rep2", [60, 64, 128], F32)
    self.EtD = self.dscr("EtD", [DEPTH, 128, 12800], BF16)


Builder.declare_p2 = _declare_p2


def _pool(self, l):
    nc, P, T, NT = self.nc, self.P, self.T, self.NT
    TT = 512
    W = self.w
    with ExitStack() as st:
        A = self.sb(st, "A", [128, 36, 128], BF16)
        pwf = self.sb(st, "pwf", [128, 2, 128], F32)
        pwb = self.sb(st, "pwb", [128, 2, 128], BF16)
        psc = self.sb(st, "psc", [128, 2], F32)
        cB = Buf("c")
        zt = [self.sb(st, f"zt{i}", [128, 6, 256], BF16) for i in range(2)]
        zB = P.bufs(2, "zt")
        mx = self.sb(st, "mx", [128, 2, TT], BF16)
        mxB = Buf("mx")
        oas = [self.sb(st, f"oas{i}", [128, 2, TT], BF16) for i in range(2)]
        oaB = P.bufs(2, "oas")
        banks = self.psum_banks(st)
        bB = P.bufs(8, "bank")
        P.dma(A[:], self.c_poolA.ap(), w=[cB], sb=cB)
        P.op(P.dve, lambda: nc.vector.memset(pwf[:], 0.0), w=[cB])
        for g in range(4):
            gp, gg = g // 2, g % 2
            P.dma(pwf[gg * 64:(gg + 1) * 64, gp, gg * 64:(gg + 1) * 64],
                  DAP(W["pool_w"], (l * 4 + g) * 4096, [(64, 64), (1, 64)]), r=[cB], w=[cB], sb=cB)
        P.dma(psc[:], DAP(W["pool_scale"], l * 256, [(1, 128), (128, 2)]), w=[cB], sb=cB, slow=True)
        P.op(P.dve, lambda: nc.vector.tensor_copy(out=pwb[:], in_=pwf[:]), r=[cB], w=[cB])
        ntile = T // TT

        def load(i):
            b = i % 2
            n0 = i * 4 - 1
            lo = max(n0, 0)
            hi = min(n0 + 6, NT)
            P.dma(zt[b][:, lo - n0:hi - n0, :], DAP(self.zp, lo * 128 * 256, [(256, 128), (128 * 256, hi - lo), (1, 256)]),
                  w=[zB[b]], sb=zB[b])

        load(0)
        for i in range(ntile):
            b = i % 2
            if i + 1 < ntile:
                load(i + 1)
            for gp in range(2):
                for su in range(4):
                    n = i * 4 + su
                    cls = 0 if n == 0 else (2 if n == NT - 1 else 1)
                    for gg in range(2):
                        g = gp * 2 + gg
                        js = [j for j in (-1, 0, 1) if 0 <= n + j < NT]
                        for j in js:
                            P.op(P.pe, lambda gp=gp, su=su, gg=gg, g=g, j=j, cls=cls, b=b, js=js: nc.tensor.matmul(
                                banks[gp][gg * 64:(gg + 1) * 64, su * 128:(su + 1) * 128],
                                lhsT=zt[b][:, su + 1 + j, g * 64:(g + 1) * 64], rhs=A[:, cls * 12 + g * 3 + j + 1, :],
                                start=(j == js[0]), stop=(j == js[-1])), r=[zB[b], cB], w=[bB[gp]])
                P.op(P.act, lambda gp=gp: nc.scalar.copy(out=mx[:, gp, :], in_=banks[gp][:]), r=[bB[gp]], w=[mxB])
                P.op(P.pe, lambda gp=gp: nc.tensor.matmul(banks[2 + gp][:], lhsT=pwb[:, gp, :], rhs=mx[:, gp, :],
                                                          start=True, stop=True), r=[mxB, cB], w=[bB[2 + gp]])
                P.op(P.act, lambda gp=gp, b=b: nc.scalar.activation(out=oas[b][:, gp, :], in_=banks[2 + gp][:],
                                                                   func=AF.Copy, scale=psc[:, gp:gp + 1]),
                     r=[bB[2 + gp], cB], w=[oaB[b]])
            P.dma(DAP(self.oT, i * TT, [(T, 128), (128 * T, 2), (1, TT)]), oas[b][:], r=[oaB[b]], sb=oaB[b])
        P.barrier()
        P.flush()
        P.release(zB + oaB + [cB])


Builder.pool = _pool


def _gqa(self, l):
    nc, P, T, NT = self.nc, self.P, self.T, self.NT
    TT = 512
    W = self.w
    with ExitStack() as st:
        ident = self.sb(st, "ident", [128, 128], BF16)
        gmask = self.sb(st, "gmask", [128, 2, 512], BF16)
        ones = self.sb(st, "ones", [128, 64], BF16)
        skf = self.sb(st, "skf", [1, 8], F32)
        esr = self.sb(st, "esr", [1, 8, 128], BF16)
        cB = Buf("c")
        qt = [self.sb(st, f"qt{i}", [64, 8, TT], BF16) for i in range(2)]
        kt = [self.sb(st, f"kt{i}", [64, 2, 6 * 128], BF16) for i in range(2)]
        vt = [self.sb(st, f"vt{i}", [128, 6, 128], BF16) for i in range(2)]
        qB, kB, vB = P.bufs(2, "q"), P.bufs(2, "k"), P.bufs(2, "v")
        NP = 6
        pt = [self.sb(st, f"pt{i}", [128, 512], BF16) for i in range(NP)]
        ptB = P.bufs(NP, "pt")
        rec = [self.sb(st, f"rec{i}", [64, 512], F32) for i in range(2)]
        recB = P.bufs(2, "rec")
        obs = [self.sb(st, f"obs{i}", [64, 8, TT], BF16) for i in range(2)]
        obB = P.bufs(2, "obs")
        banks = self.psum_banks(st)
        bB = P.bufs(8, "bank")
        P.dma(ident[:], self.c_ident.ap(), w=[cB], sb=cB)
        P.dma(gmask[:], self.c_gmask.ap(), w=[cB], sb=cB)
        P.dma(skf[:], DAP(W["gqa_sink"], l * 8, [(8, 1), (1, 8)]), w=[cB], sb=cB)
        P.op(P.dve, lambda: nc.vector.memset(ones[:], 1.0), w=[cB])
        P.op(P.act, lambda: nc.scalar.activation(out=skf[:], in_=skf[:], func=AF.Exp), r=[cB], w=[cB])
        P.op(P.dve, lambda: nc.vector.tensor_copy(out=esr[:], in_=skf[:].unsqueeze(2).broadcast_to([1, 8, 128])),
             r=[cB], w=[cB])
        ntile = T // TT

        def load(i):
            b = i % 2
            n0 = i * 4 - 1
            lo, hi = max(n0, 0), min(n0 + 6, NT)
            P.dma(qt[b][:], DAP(self.qT, i * TT, [(T, 64), (64 * T, 8), (1, TT)]), w=[qB[b]], sb=qB[b])
            P.dma(kt[b][:, :, (lo - n0) * 128:(hi - n0) * 128],
                  DAP(self.kT, lo * 128, [(T, 64), (64 * T, 2), (1, (hi - lo) * 128)]), w=[kB[b]], sb=kB[b])
            P.dma(vt[b][:, lo - n0:hi - n0, :], DAP(self.gv, lo * 128 * 128, [(128, 128), (128 * 128, hi - lo), (1, 128)]),
                  w=[vB[b]], sb=vB[b])

        load(0)
        cnt = 0
        pcnt = 0
        for i in range(ntile):
            b = i % 2
            if i + 1 < ntile:
                load(i + 1)
            for su in range(4):
                n = i * 4 + su
                js = [j for j in (-1, 0, 1) if 0 <= n + j < NT]
                for g in range(2):
                    pts = []
                    for j in js:
                        slot = su + 1 + j
                        sbk = pcnt % 4
                        pi = pcnt % NP
                        pcnt += 1
                        P.op(P.pe, lambda g=g, slot=slot, sbk=sbk, b=b, su=su, j=j: nc.tensor.matmul(
                            banks[sbk][:], lhsT=kt[b][:, g, slot * 128:(slot + 1) * 128],
                            rhs=qt[b][:, g * 4:(g + 1) * 4, su * 128:(su + 1) * 128], start=True, stop=(j == 0)),
                            r=[kB[b], qB[b]], w=[bB[sbk]])
                        if j != 0:
                            P.op(P.pe, lambda sbk=sbk, j=j: nc.tensor.matmul(
                                banks[sbk][:], lhsT=ident[:], rhs=gmask[:, 0 if j < 0 else 1, :], start=False, stop=True),
                                r=[cB], w=[bB[sbk]])
                        P.op(P.act, lambda sbk=sbk, pi=pi: nc.scalar.activation(out=pt[pi][:], in_=banks[sbk][:],
                                                                              func=AF.Exp), r=[bB[sbk]], w=[ptB[pi]])
                        pts.append((slot, pi))
                    ob = 4 + (cnt % 2) * 2
                    rb = cnt % 2
                    cnt += 1
                    for idx, (slot, pi) in enumerate(pts):
                        P.op(P.pe, lambda ob=ob, slot=slot, pi=pi, g=g, b=b, idx=idx, npt=len(pts): nc.tensor.matmul(
                            banks[ob][0:64, :], lhsT=vt[b][:, slot, g * 64:(g + 1) * 64], rhs=pt[pi][:],
                            start=(idx == 0), stop=(idx == npt - 1)), r=[vB[b], ptB[pi]], w=[bB[ob]])
                    for idx, (slot, pi) in enumerate(pts):
                        P.op(P.pe, lambda ob=ob, pi=pi, idx=idx: nc.tensor.matmul(
                            banks[ob + 1][0:64, :], lhsT=ones[:, 0:64], rhs=pt[pi][:], start=(idx == 0), stop=False),
                            r=[cB, ptB[pi]], w=[bB[ob + 1]])
                    P.op(P.pe, lambda ob=ob, g=g: nc.tensor.matmul(
                        banks[ob + 1][0:64, :], lhsT=ones[0:1, 0:64], rhs=esr[0:1, g * 4:(g + 1) * 4, :],
                        start=False, stop=True), r=[cB], w=[bB[ob + 1]])
                    P.op(P.act, lambda ob=ob, rb=rb: nc.scalar.activation(out=rec[rb][:], in_=banks[ob + 1][0:64, :],
                                                                      func=AF.Ln), r=[bB[ob + 1]], w=[recB[rb]])
                    P.op(P.act, lambda rb=rb: nc.scalar.activation(out=rec[rb][:], in_=rec[rb][:], func=AF.Exp,
                                                               scale=-1.0), r=[recB[rb]], w=[recB[rb]])
                    P.op(P.dve, lambda ob=ob, rb=rb, g=g, su=su, b=b: nc.vector.tensor_tensor(
                        out=obs[b][:, g * 4:(g + 1) * 4, su * 128:(su + 1) * 128],
                        in0=banks[ob][0:64, :].rearrange("p (h t) -> p h t", h=4),
                        in1=rec[rb][:].rearrange("p (h t) -> p h t", h=4), op=ALU.mult),
                        r=[bB[ob], recB[rb]], w=[obB[b]])
            P.dma(DAP(self.oT, 256 * T + i * TT, [(T, 64), (64 * T, 8), (1, TT)]), obs[b][:], r=[obB[b]], sb=obB[b])
        P.barrier()
        P.flush()
        P.release(qB + kB + vB + obB + [cB])


Builder.gqa = _gqa


def _nat(self, l, s=0):
    nc, P, T, NT = self.nc, self.P, self.T, self.NT
    W = self.w
    build = (s == 0)
    with ExitStack() as st:
        ident = self.sb(st, "ident", [128, 128], BF16)
        identf = self.sb(st, "identf", [128, 128], F32)
        nmask = self.sb(st, "nmask", [128, 25, 128], BF16)
        ones = self.sb(st, "ones", [128, 64], BF16)
        R = self.sb(st, "R", [60, 128], F32)
        Tb = self.sb(st, "Tb", [128, 9, 4, 128], BF16)
        bt = [self.sb(st, f"bt{i}", [128, 128], F32) for i in range(2)]
        btB = P.bufs(2, "bt")
        cB, rB, tbB = Buf("c"), Buf("R"), Buf("Tb")
        Et = self.sb(st, "Et", [128, 25, 4, 128], BF16)
        etB = Buf("Et")
        etmp = [self.sb(st, f"etmp{i}", [128, 512], F32) for i in range(2)]
        etmpB = P.bufs(2, "etmp")
        pe_ = [self.sb(st, f"pe{i}", [128, 512], BF16) for i in range(4)]
        peB = P.bufs(4, "pe")
        NB = 2
        qt = [self.sb(st, f"qt{i}", [64, 4, 512], BF16) for i in range(NB)]
        kt = [self.sb(st, f"kt{i}", [64, 4, 1024], BF16) for i in range(NB)]
        vt = [self.sb(st, f"vt{i}", [128, 8, 256], BF16) for i in range(NB)]
        qB, kB, vB = P.bufs(NB, "q"), P.bufs(NB, "k"), P.bufs(NB, "v")
        NP = 10
        pt = [self.sb(st, f"pt{i}", [128, 512], BF16) for i in range(NP)]
        ptB = P.bufs(NP, "pt")
        rec = [self.sb(st, f"rec{i}", [64, 512], F32) for i in range(2)]
        recB = P.bufs(2, "rec")
        ods = [self.sb(st, f"ods{i}", [64, 4, 512], BF16) for i in range(2)]
        odB = P.bufs(2, "ods")
        banks = self.psum_banks(st)
        bB = P.bufs(8, "bank")
        P.dma(ident[:], self.c_ident.ap(), w=[cB], sb=cB)
        P.dma(identf[:], self.c_identf.ap(), w=[cB], sb=cB)
        P.dma(nmask[:], self.c_nmask.ap(), w=[cB], sb=cB)
        P.op(P.dve, lambda: nc.vector.memset(ones[:], 1.0), w=[cB])
        if not build:
            P.dma(Et[:].rearrange("p a h t -> p (a h t)"), DAP(self.EtD, l * 128 * 12800, [(12800, 128), (1, 12800)]),
                  w=[etB], sb=etB)
        P.op(P.dve, lambda: nc.vector.memset(R[:], 0.0), w=[rB])
        if build:
            P.dma(R[:, 48:79], DAP(W["nat_rpb"], l * 60 * 31, [(31, 60), (1, 31)]), r=[rB], w=[rB], sb=rB)
            P.dma(self.rep2.ap(), R[:].unsqueeze(1).broadcast_to([60, 64, 128]), r=[rB], sb=rB)
            P.barrier()
            k = 0
            for dl in range(-4, 5):
                for h in range(4):
                    i2 = k % 2
                    k += 1
                    P.op(P.dve, lambda i2=i2: nc.vector.memset(bt[i2][:], 0.0), w=[btB[i2]])
                    for qr in range(2):
                        for kr in range(2):
                            dr = 2 * dl + kr - qr + 7
                            if 0 <= dr < 15:
                                P.dma(bt[i2][qr * 64:(qr + 1) * 64, kr * 64:(kr + 1) * 64],
                                      DAP(self.rep2, (h * 15 + dr) * 8192 + 63, [(127, 64), (1, 64)]),
                                      r=[btB[i2]], w=[btB[i2]], sb=btB[i2])
                    P.op(P.pe, lambda i2=i2: nc.tensor.transpose(banks[7][:, 0:128], bt[i2][:], identf[:]),
                         r=[btB[i2], cB], w=[bB[7]])
                    P.op(P.act, lambda dl=dl, h=h: nc.scalar.copy(out=Tb[:, dl + 4, h, :], in_=banks[7][:, 0:128]),
                         r=[bB[7]], w=[tbB])

            reps = {0: 0, 1: 1, 2: 2, 3: NT - 2, 4: NT - 1}
            for cls in range(5):
                m_ = reps[cls]
                kt0_ = min(max(m_ - 2, 0), NT - 5)
                for slot in range(5):
                    dl = kt0_ + slot - m_
                    e2 = (cls * 5 + slot) % 2
                    P.op(P.dve, lambda dl=dl, cls=cls, slot=slot, e2=e2: nc.vector.tensor_tensor(
                        out=etmp[e2][:].rearrange("p (h t) -> p h t", h=4), in0=Tb[:, dl + 4, :, :],
                        in1=nmask[:, cls * 5 + slot, :].unsqueeze(1).broadcast_to([128, 4, 128]), op=ALU.add),
                        r=[tbB, cB], w=[etmpB[e2]])
                    P.op(P.act, lambda cls=cls, slot=slot, e2=e2: nc.scalar.activation(
                        out=Et[:, cls * 5 + slot, :, :].rearrange("p h t -> p (h t)"), in_=etmp[e2][:], func=AF.Exp),
                        r=[etmpB[e2]], w=[etB])
            P.dma(DAP(self.EtD, l * 128 * 12800, [(12800, 128), (1, 12800)]), Et[:].rearrange("p a h t -> p (a h t)"),
                  r=[etB], sb=etB)

        kt0f = lambda m: min(max(m - 2, 0), NT - 5)

        def load(g):
            b = g % NB
            lo = kt0f(4 * g)
            hi = kt0f(4 * g + 3) + 5
            nk = hi - lo
            P.dma(qt[b][:], DAP(self.nqkT, g * 512, [(T, 64), (64 * T, 4), (1, 512)]), w=[qB[b]], sb=qB[b])
            P.dma(kt[b][:, :, 0:nk * 128], DAP(self.nqkT, 256 * T + lo * 128, [(T, 64), (64 * T, 4), (1, nk * 128)]),
                  w=[kB[b]], sb=kB[b])
            P.dma(vt[b][:, 0:nk, :], DAP(self.nv, lo * 128 * 256, [(256, 128), (128 * 256, nk), (1, 256)]),
                  w=[vB[b]], sb=vB[b])

        assert NT % 4 == 0
        load(0)
        pcnt = 0
        for m in range(NT):
            g = m // 4
            b = g % NB
            lq = m % 4
            if lq == 1 and g + 1 < NT // 4:
                load(g + 1)
            kt0 = kt0f(m)
            ko = kt0 - kt0f(4 * g)
            cls = 0 if m == 0 else 1 if m == 1 else 3 if m == NT - 2 else 4 if m == NT - 1 else 2
            pis = []
            for slot in range(5):
                dl = kt0 + slot - m
                sbk = pcnt % 5
                pi = pcnt % NP
                pcnt += 1
                for h in range(4):
                    P.op(P.pe, lambda h=h, slot=slot, sbk=sbk, b=b, ko=ko, lq=lq: nc.tensor.matmul(
                        banks[sbk][:, h * 128:(h + 1) * 128], lhsT=kt[b][:, h, (ko + slot) * 128:(ko + slot + 1) * 128],
                        rhs=qt[b][:, h, lq * 128:(lq + 1) * 128], start=True, stop=True), r=[kB[b], qB[b]], w=[bB[sbk]])
                p4 = pcnt % 4
                P.op(P.act, lambda sbk=sbk, p4=p4: nc.scalar.activation(out=pe_[p4][:], in_=banks[sbk][:], func=AF.Exp),
                     r=[bB[sbk]], w=[peB[p4]])
                P.op(P.dve, lambda pi=pi, p4=p4, cls=cls, slot=slot: nc.vector.tensor_tensor(
                    out=pt[pi][:], in0=pe_[p4][:], in1=Et[:, cls * 5 + slot, :, :].rearrange("p h t -> p (h t)"),
                    op=ALU.mult), r=[peB[p4], etB], w=[ptB[pi]])
                pis.append(pi)
            rb = m % 2
            for h in range(4):
                for slot in range(5):
                    P.op(P.pe, lambda h=h, slot=slot, b=b, pi=pis[slot], ko=ko: nc.tensor.matmul(
                        banks[5][0:64, h * 128:(h + 1) * 128], lhsT=vt[b][:, ko + slot, h * 64:(h + 1) * 64],
                        rhs=pt[pi][:, h * 128:(h + 1) * 128], start=(slot == 0), stop=(slot == 4)),
                        r=[vB[b], ptB[pis[slot]]], w=[bB[5]])
            for slot in range(5):
                P.op(P.pe, lambda slot=slot, pi=pis[slot]: nc.tensor.matmul(
                    banks[6][0:64, :], lhsT=ones[:, 0:64], rhs=pt[pi][:], start=(slot == 0), stop=(slot == 4)),
                    r=[cB, ptB[pis[slot]]], w=[bB[6]])
            P.op(P.act, lambda rb=rb: nc.scalar.activation(out=rec[rb][:], in_=banks[6][0:64, :], func=AF.Ln),
                 r=[bB[6]], w=[recB[rb]])
            P.op(P.act, lambda rb=rb: nc.scalar.activation(out=rec[rb][:], in_=rec[rb][:], func=AF.Exp, scale=-1.0),
                 r=[recB[rb]], w=[recB[rb]])
            P.op(P.dve, lambda rb=rb, g=g, lq=lq: nc.vector.tensor_tensor(
                out=ods[g % 2][:, :, lq * 128:(lq + 1) * 128], in0=banks[5][0:64, :].rearrange("p (h t) -> p h t", h=4),
                in1=rec[rb][:].rearrange("p (h t) -> p h t", h=4), op=ALU.mult),
                r=[bB[5], recB[rb]], w=[odB[g % 2]])
            if lq == 3:
                P.dma(DAP(self.oT, 1024 * T + g * 512, [(T, 64), (64 * T, 4), (1, 512)]), ods[g % 2][:],
                      r=[odB[g % 2]], sb=odB[g % 2])
        P.barrier()
        P.flush()
        P.release(qB + kB + vB + odB + btB + [cB, rB])


Builder.nat = _nat


def _nat_masks(T):
    NT = T // 128
    rows = T // 64
    kr = 8
    out = np.full((25, 128, 128), NEGM, np.float32)
    reps = {0: 0, 1: 1, 2: 2, 3: NT - 2, 4: NT - 1}
    for cls, m in reps.items():
        kt0 = min(max(m - 2, 0), NT - 5)
        for slot in range(5):
            ktile = kt0 + slot
            for qr in range(2):
                r = 2 * m + qr
                row0 = min(max(r - kr // 2, 0), rows - kr)
                for krr in range(2):
                    krow = 2 * ktile + krr
                    if not (row0 <= krow < row0 + kr):
                        continue
                    for qc in range(64):
                        qc0 = min(max(qc - 8, 0), 64 - 16)
                        out[cls * 5 + slot, krr * 64 + qc0: krr * 64 + qc0 + 16, qr * 64 + qc] = 0.0
    return out


def _pool_A(T):
    out = np.zeros((3, 4, 3, 128, 128), np.float32)
    for cls in range(3):
        for g, w in enumerate((2, 4, 8, 16)):
            for t in range(128):
                lo, hi = t - w // 2, t + w // 2
                if cls == 0:
                    lo = max(lo, 0)
                if cls == 2:
                    hi = min(hi, 128)
                cnt = hi - lo
                for srel in range(lo, hi):
                    j = srel // 128
                    out[cls, g, j + 1, srel - j * 128, t] += 1.0 / cnt
                out[cls, g, 1, t, t] -= 1.0
    return out.reshape(36, 128, 128)


_make_consts_p1 = make_consts


def make_consts(T):
    c = _make_consts_p1(T)
    c["c_poolA"] = _bf16(_pool_A(T).transpose(1, 0, 2))
    gm = np.zeros((2, 128, 4, 128), np.float32)
    jj = np.arange(128)[:, None]
    ii = np.arange(128)[None, :]
    gm[0] = np.where(jj >= ii, 0.0, NEGM)[:, None, :]
    gm[1] = np.where(jj <= ii, 0.0, NEGM)[:, None, :]
    c["c_gmask"] = _bf16(gm.reshape(2, 128, 512).transpose(1, 0, 2))
    c["c_nmask"] = _bf16(_nat_masks(T).transpose(1, 0, 2))
    c["c_identf"] = np.eye(128, dtype=np.float32)
    s = np.arange(128)[:, None]
    t = np.arange(128)[None, :]
    same = (s // 32) == (t // 32)
    hm = np.stack([np.tile((same & (s <= t)).astype(np.float32), (1, 4)),
                   np.tile((same & (s >= t)).astype(np.float32), (1, 4))], 0)
    c["c_hmask"] = _bf16(hm.transpose(1, 0, 2))
    c["c_cm"] = _bf16((np.arange(128)[:, None] // 32 == np.arange(4)[None, :]).astype(np.float32))
    bd = np.zeros((128, 128), np.float32)
    bd[:64, :64] = 1
    bd[64:, 64:] = 1
    c["c_bd"] = _bf16(bd)
    return c


def _hgrn(self, l):
    nc, P, T, NT = self.nc, self.P, self.T, self.NT
    NCH = T // 32
    with ExitStack() as st:
        hmask = self.sb(st, "hmask", [128, 2, 512], BF16)
        cm = self.sb(st, "cm", [128, 4], BF16)
        ones = self.sb(st, "ones", [128, 64], BF16)
        eps_t = self.sb(st, "eps", [128, 1], F32)
        cB = Buf("c")
        NB = 4
        qt = [self.sb(st, f"qt{i}", [64, 4, 128], BF16) for i in range(NB)]
        kt = [self.sb(st, f"kt{i}", [64, 4, 128], BF16) for i in range(NB)]
        kp = [self.sb(st, f"kp{i}", [128, 256], BF16) for i in range(NB)]
        vt = [self.sb(st, f"vt{i}", [128, 256], BF16) for i in range(NB)]
        at = [self.sb(st, f"at{i}", [64, 4, 4], F32) for i in range(NB)]
        of = [self.sb(st, f"of{i}", [64, 4, 128], F32) for i in range(NB)]
        gt = [self.sb(st, f"gt{i}", [64, 4, 128], BF16) for i in range(NB)]
        ldB = P.bufs(NB, "ld")
        kpm = [self.sb(st, f"kpm{i}", [128, 4, 256], BF16) for i in range(2)]
        kpmB = P.bufs(2, "kpm")
        Sf = [self.sb(st, f"Sf{j}", [64, 4, 64], F32) for j in range(2)]
        SfB = P.bufs(2, "Sf")
        SfhB = [P.bufs(4, f"Sfh{j}") for j in range(2)]
        stmp = self.sb(st, "stmp", [64, 4, 64], F32)
        stB = Buf("stmp")
        sthB = P.bufs(4, "sth")
        Sb = [self.sb(st, f"Sb{i}", [64, 4, 4, 64], BF16) for i in range(2)]
        SbB = P.bufs(2, "Sb")
        Am = [self.sb(st, f"Am{i}", [128, 4, 128], BF16) for i in range(2)]
        AmB = P.bufs(2, "Am")
        osm = [self.sb(st, f"osm{i}", [64, 512], F32) for i in range(2)]
        osB = P.bufs(2, "osm")
        sqb = self.sb(st, "sqb", [64, 512], BF16)
        sqB = Buf("sqb")
        rr = self.sb(st, "rr", [64, 512], F32)
        rrB = Buf("rr")
        rr2 = self.sb(st, "rr2", [64, 512], F32)
        rr2B = Buf("rr2")
        ocs = [self.sb(st, f"ocs{i}", [64, 4, 128], BF16) for i in range(2)]
        ocB = P.bufs(2, "ocs")
        banks = self.psum_banks(st)
        bB = P.bufs(8, "bank")
        P.dma(hmask[:], self.c_hmask.ap(), w=[cB], sb=cB)
        P.dma(cm[:], self.c_cm.ap(), w=[cB], sb=cB)
        P.op(P.dve, lambda: nc.vector.memset(ones[:], 1.0), w=[cB])
        P.op(P.dve, lambda: nc.vector.memset(eps_t[:], EPS), w=[cB])

        for d in range(2):
            order = list(range(NT)) if d == 0 else list(range(NT - 1, -1, -1))
            corder = [0, 1, 2, 3] if d == 0 else [3, 2, 1, 0]
            cur = [0]
            P.op(P.dve, lambda: nc.vector.memset(Sf[0][:], 0.0), w=SfhB[0])

            def load(step):
                n = order[step]
                b = step % NB
                t0 = n * 128
                hd = [(T, 64), (64 * T, 4), (1, 128)]
                P.dma(qt[b][:], DAP(self.QT, d * 256 * T + t0, hd), w=[ldB[b]], sb=ldB[b])
                P.dma(kt[b][:], DAP(self.KT, d * 256 * T + t0, hd), w=[ldB[b]], sb=ldB[b])
                P.dma(kp[b][:], DAP(self.Kp, (d * T + t0) * 256, [(256, 128), (1, 256)]), w=[ldB[b]], sb=ldB[b])
                P.dma(vt[b][:], DAP(self.hv, t0 * 256, [(256, 128), (1, 256)]), w=[ldB[b]], sb=ldB[b])
                P.dma(at[b][:], DAP(self.aT, d * 256 * NCH + n * 4, [(NCH, 64), (64 * NCH, 4), (1, 4)]),
                      w=[ldB[b]], sb=ldB[b])
                if d == 1:
                    P.dma(of[b][:], DAP(self.ofw, t0, hd), w=[ldB[b]], sb=ldB[b])
                    P.dma(gt[b][:], DAP(self.gateT, t0, hd), w=[ldB[b]], sb=ldB[b])

            def stage1(step):
                b = step % NB
                p2 = step % 2
                ub = [0, 1] if p2 == 0 else [2, 3]
                ab = 4
                P.op(P.pool, lambda: nc.gpsimd.tensor_tensor(
                    out=kpm[p2][:], in0=kp[b][:].unsqueeze(1).broadcast_to([128, 4, 256]),
                    in1=cm[:].unsqueeze(2).broadcast_to([128, 4, 256]), op=ALU.mult), r=[ldB[b], cB], w=[kpmB[p2]])
                for c in range(4):
                    for h in range(4):
                        col = ((c % 2) * 4 + h) * 64
                        P.op(P.pe, lambda c=c, h=h, col=col: nc.tensor.matmul(
                            banks[ub[c // 2]][0:64, col:col + 64], lhsT=kpm[p2][:, c, h * 64:(h + 1) * 64],
                            rhs=vt[b][:, h * 64:(h + 1) * 64], start=True, stop=True),
                            r=[kpmB[p2], ldB[b]], w=[bB[ub[c // 2]]])
                for h in range(4):
                    P.op(P.pe, lambda h=h: nc.tensor.matmul(
                        banks[ab][:, h * 128:(h + 1) * 128], lhsT=kt[b][:, h, :], rhs=qt[b][:, h, :],
                        start=True, stop=True), r=[ldB[b]], w=[bB[ab]])
                P.op(P.dve, lambda: nc.vector.tensor_tensor(
                    out=Am[p2][:].rearrange("p h t -> p (h t)"), in0=banks[ab][:], in1=hmask[:, d, :], op=ALU.mult),
                    r=[bB[ab], cB], w=[AmB[p2]])
                for c in corder:
                    cu = cur[0]
                    nx = 1 - cu
                    P.op(P.act, lambda c=c, cu=cu: nc.scalar.copy(out=Sb[p2][:, c, :, :], in_=Sf[cu][:]),
                         r=SfhB[cu], w=[SbB[p2]])
                    ucol = (c % 2) * 256
                    for h in range(4):
                        P.op(P.dve, lambda c=c, cu=cu, h=h: nc.vector.tensor_tensor(
                            out=stmp[:, h, :], in0=Sf[cu][:, h, :], in1=at[b][:, h, c:c + 1].broadcast_to([64, 64]),
                            op=ALU.mult), r=[SfhB[cu][h], ldB[b]], w=[sthB[h]])
                    for h in range(4):
                        P.op(P.dve, lambda c=c, nx=nx, ucol=ucol, h=h: nc.vector.tensor_tensor(
                            out=Sf[nx][:, h, :], in0=banks[ub[c // 2]][0:64, ucol + h * 64: ucol + (h + 1) * 64],
                            in1=stmp[:, h, :], op=ALU.add), r=[sthB[h], bB[ub[c // 2]]], w=[SfhB[nx][h]])
                    cur[0] = nx

            def stage2(step):
                n = order[step]
                b = step % NB
                p2 = step % 2
                ob = 6 + p2
                for h in range(4):
                    P.op(P.pe, lambda h=h: nc.tensor.matmul(
                        banks[ob][0:64, h * 128:(h + 1) * 128], lhsT=vt[b][:, h * 64:(h + 1) * 64],
                        rhs=Am[p2][:, h, :], start=True, stop=False), r=[ldB[b], AmB[p2]], w=[bB[ob]])
                    for c in range(4):
                        P.op(P.pe, lambda h=h, c=c: nc.tensor.matmul(
                            banks[ob][0:64, h * 128 + c * 32: h * 128 + (c + 1) * 32],
                            lhsT=Sb[p2][:, c, h, :], rhs=qt[b][:, h, c * 32:(c + 1) * 32],
                            start=False, stop=(c == 3)), r=[ldB[b], SbB[p2]], w=[bB[ob]])
                hd = [(T, 64), (64 * T, 4), (1, 128)]
                if d == 0:
                    P.op(P.act, lambda: nc.scalar.copy(out=osm[p2][:], in_=banks[ob][0:64, :]), r=[bB[ob]], w=[osB[p2]])
                    P.dma(DAP(self.ofw, n * 128, hd), osm[p2][:].rearrange("p (h t) -> p h t", h=4),
                          r=[osB[p2]], sb=osB[p2])
                else:
                    P.op(P.dve, lambda: nc.vector.tensor_tensor(
                        out=osm[p2][:], in0=banks[ob][0:64, :], in1=of[b][:].rearrange("p h t -> p (h t)"),
                        op=ALU.add), r=[bB[ob], ldB[b]], w=[osB[p2]])
                    P.op(P.act, lambda: nc.scalar.activation(out=sqb[:], in_=osm[p2][:], func=AF.Square),
                         r=[osB[p2]], w=[sqB])
                    P.op(P.pe, lambda: nc.tensor.matmul(banks[5][0:64, :], lhsT=ones[0:64, 0:64], rhs=sqb[:],
                                                        start=True, stop=True), r=[cB, sqB], w=[bB[5]])
                    P.op(P.act, lambda: nc.scalar.activation(out=rr[:], in_=banks[5][0:64, :], func=AF.Ln,
                                                             scale=1.0 / 64, bias=eps_t[0:64, 0:1]),
                         r=[bB[5], cB], w=[rrB])
                    P.op(P.act, lambda: nc.scalar.activation(out=rr2[:], in_=rr[:], func=AF.Exp, scale=-0.5),
                         r=[rrB], w=[rr2B])
                    P.op(P.pool, lambda: nc.gpsimd.tensor_tensor(
                        out=rr[:], in0=rr2[:], in1=gt[b][:].rearrange("p h t -> p (h t)"), op=ALU.mult),
                        r=[rr2B, ldB[b]], w=[rrB])
                    P.op(P.dve, lambda: nc.vector.tensor_tensor(
                        out=ocs[p2][:].rearrange("p h t -> p (h t)"), in0=osm[p2][:], in1=rr[:], op=ALU.mult),
                        r=[osB[p2], rrB], w=[ocB[p2]])
                    P.dma(DAP(self.oT, 768 * T + n * 128, hd), ocs[p2][:], r=[ocB[p2]], sb=ocB[p2])

            load(0)
            if NT > 1:
                load(1)
            for step in range(NT + 1):
                if step + 2 < NT:
                    load(step + 2)
                if step < NT:
                    stage1(step)
                if step >= 1:
                    stage2(step - 1)
            P.barrier()
            P.flush()
        P.release(ldB + osB + ocB + [cB])


Builder.hgrn = _hgrn


W_NAMES = ["norm1_g", "w_in", "pool_w", "pool_scale", "gqa_qnorm", "gqa_knorm", "gqa_sink", "hgrn_lb", "hgrn_onorm",
           "nat_qnorm", "nat_knorm", "nat_rpb", "w_br_pool", "w_br_gqa", "w_br_hgrn", "w_br_nat", "w_o", "norm2_g",
           "w_up", "w_down"]


def build_program(T, nseq, nlayers=DEPTH, dbg=None):
    B = Builder(T, nseq, dbg=dbg)
    B.declare()
    B.declare_mix()
    B.declare_p2()
    B.prologue()
    for s in range(nseq):
        for l in range(nlayers):
            src = B.x_in if l == 0 else B.xb
            dst = B.y_out if l == nlayers - 1 else B.xb
            B.proj2(l, s, src)
            B.pool(l)
            B.gqa(l)
            B.hgrn2(l)
            B.nat(l, s)
            B.merge(l, s, src, B.xa)
            B.ffn(l, s, B.xa, dst)
    return B


def _run(x_per_core, weights, T, nseq, nlayers=DEPTH):
    B = build_program(T, nseq, nlayers)
    consts = make_consts(T)
    in_maps = []
    for c in range(NCORES):
        m = {"x": np.ascontiguousarray(x_per_core[c], dtype=np.float32)}
        for k in W_NAMES:
            m[k] = np.ascontiguousarray(weights[k], dtype=np.float32)
        m.update(consts)
        in_maps.append(m)
    res = run_bass_kernel_spmd(B.nc, in_maps, core_ids=list(range(NCORES)))
    return [np.asarray(res.results[c]["y"]) for c in range(NCORES)]


def kernel(x_prompt, x_sample, **weights):
    x_prompt = np.asarray(x_prompt, dtype=np.float32)
    x_sample = np.asarray(x_sample, dtype=np.float32)
    seqs = [x_prompt[i] for i in range(x_prompt.shape[0])] + [x_sample[i] for i in range(x_sample.shape[0])]
    n = len(seqs)
    T = seqs[0].shape[0]
    slot = lambda c, j: (c + 8 * j) if (c + 8 * j) < n else c
    xs = [np.stack([seqs[slot(c, 0)], seqs[slot(c, 1)]], 0) for c in range(NCORES)]
    ys = _run(xs, weights, T, 2)
    out = [None] * n
    for c in range(NCORES):
        for j in range(2):
            if c + 8 * j < n:
                out[c + 8 * j] = ys[c][j]
    nb = x_prompt.shape[0]
    y_prompt = np.stack(out[:nb], 0).astype(np.float32)
    y_sample = np.stack(out[nb:], 0).astype(np.float32)
    return (y_prompt, y_sample)


def _interleave(*gens):
    gens = list(gens)
    while gens:
        for g in list(gens):
            try:
                next(g)
            except StopIteration:
                gens.remove(g)


def _proj2(self, l, s, src):
    nc, P, T = self.nc, self.P, self.T
    TT = 512
    W = self.w
    NCH = T // 32
    with ExitStack() as st:
        w1 = self.sb(st, "w1", [128, 8, 3072], BF16)
        wB = Buf("w1")
        ident = self.sb(st, "ident", [128, 128], BF16)
        eps_t = self.sb(st, "eps", [128, 1], F32)
        rmask = self.sb(st, "rmask", [128, 2, 512], F32)
        G18 = self.sb(st, "G18", [128, 18, 64], F32)
        lbr = self.sb(st, "lbr", [128, 2, 2, 2], F32)
        lbv = self.sb(st, "lbv", [128, 2, 2], F32)
        oml = self.sb(st, "oml", [128, 2, 2], F32)
        ong = self.sb(st, "ong", [128, 1], F32)
        cB = Buf("c")
        cs = [self.sb(st, f"cs{i}", [128, 2, 4, 32], F32) for i in range(2)]
        csB = P.bufs(2, "cs")
        xs = [self.sb(st, f"xs{i}", [128, D], F32) for i in range(2)]
        xsB = P.bufs(2, "xs")
        hT = self.sb(st, "hT", [128, 8, TT], BF16)
        hTB = Buf("hT")
        junk = self.sb(st, "junk", [128, D], BF16)
        ss = self.sb(st, "ss", [128, 4], F32)
        hb = self.sb(st, "hb", [128, D], BF16)
        B1 = [self.sb(st, f"B1{i}", [128, 1152], F32) for i in range(2)]
        B2 = [self.sb(st, f"B2{i}", [128, 1152], F32) for i in range(2)]
        B3 = [self.sb(st, f"B3{i}", [128, 640], F32) for i in range(2)]
        s18 = [self.sb(st, f"s18{i}", [128, 3, 18], F32) for i in range(2)]
        qr = [self.sb(st, f"qr{i}", [128, 1152], BF16) for i in range(2)]
        B1B, B2B, B3B, s18B, qrB = (P.bufs(2, n) for n in ("B1", "B2", "B3", "s18", "qr"))
        qTs = self.sb(st, "qTs", [128, 5, TT], BF16)
        nTs = self.sb(st, "nTs", [128, 4, TT], BF16)
        zps = self.sb(st, "zps", [128, 4, 256], BF16)
        gvs = self.sb(st, "gvs", [128, 4, 128], BF16)
        s3 = self.sb(st, "s3", [128, 4, 512], BF16)
        qTsB, nTsB, zpsB, gvsB, s3B = P.bufs(5, "stg")
        silq = self.sb(st, "silq", [128, 2, 512], F32)
        gtf = self.sb(st, "gtf", [128, 2, 512], F32)
        gtb = self.sb(st, "gtb", [128, 2, 512], BF16)
        silqB, gtfB, gtbB = P.bufs(3, "sg")
        NH = 6
        hh = [[self.sb(st, f"hh{d}{i}", [128, 2, 512], F32) for i in range(NH)] for d in range(2)]
        hhB = [P.bufs(NH, f"hh{d}") for d in range(2)]
        hq = [[self.sb(st, f"hq{d}{i}", [128, 2, 512], BF16) for i in range(3)] for d in range(2)]
        hqB = [P.bufs(3, f"hq{d}") for d in range(2)]
        a_s = [self.sb(st, f"a_s{d}", [128, 2, 16], F32) for d in range(2)]
        a_sB = P.bufs(2, "a_s")
        kps = [self.sb(st, f"kps{d}", [128, 4, 256], BF16) for d in range(2)]
        kpsB = P.bufs(2, "kps")
        self.pass_id += 1
        ps = st.enter_context(nc.psum_tensor(f"psall_{self.pass_id}", [128, 8, 512], F32))
        bB = P.bufs(8, "bank")
        tmp = (junk, Buf(), ss, Buf(), hb, Buf(), None, None, ident, eps_t, cB)

        P.dma(ident[:], self.c_ident.ap(), w=[cB], sb=cB)
        for j in range(2):
            P.dma(rmask[:, j, :], self.c_rmask.ap(), w=[cB], sb=cB)
        P.dma(G18[:, 0:8, :], DAP(W["gqa_qnorm"], l * 64, [(0, 128), (0, 8), (1, 64)]), w=[cB], sb=cB)
        P.dma(G18[:, 8:10, :], DAP(W["gqa_knorm"], l * 64, [(0, 128), (0, 2), (1, 64)]), w=[cB], sb=cB)
        P.dma(G18[:, 10:14, :], DAP(W["nat_qnorm"], l * 64, [(0, 128), (0, 4), (1, 64)]), w=[cB], sb=cB)
        P.dma(G18[:, 14:18, :], DAP(W["nat_knorm"], l * 64, [(0, 128), (0, 4), (1, 64)]), w=[cB], sb=cB)
        P.dma(lbr[:], DAP(W["hgrn_lb"], 0, [(1, 128), (512, 2), (256, 2), (128, 2)]), w=[cB], sb=cB, slow=True)
        for k2 in range(2):
            P.dma(ong[k2 * 64:(k2 + 1) * 64, :], DAP(W["hgrn_onorm"], l * 64, [(1, 64), (1, 1)]), w=[cB], sb=cB,
                  slow=True)
        P.op(P.dve, lambda: nc.vector.memset(eps_t[:], EPS), w=[cB])
        P.op(P.dve, lambda: nc.vector.tensor_scalar(out=G18[:, 0:8, :], in0=G18[:, 0:8, :], scalar1=0.125,
                                                    scalar2=None, op0=ALU.mult), r=[cB], w=[cB])
        P.op(P.dve, lambda: nc.vector.tensor_scalar(out=G18[:, 10:14, :], in0=G18[:, 10:14, :], scalar1=0.125,
                                                    scalar2=None, op0=ALU.mult), r=[cB], w=[cB])
        if l == 0:
            P.op(P.dve, lambda: nc.vector.memset(lbv[:], 0.0), w=[cB])
            P.op(P.dve, lambda: nc.vector.memset(oml[:], 1.0), w=[cB])
        else:
            P.op(P.dve, lambda: nc.vector.tensor_tensor(out=oml[:], in0=lbr[:, 1, :, :], in1=lbr[:, 0, :, :],
                                                        op=ALU.subtract), r=[cB], w=[cB])
            P.op(P.act, lambda: nc.scalar.activation(out=lbv[:], in_=oml[:], func=AF.Sigmoid), r=[cB], w=[cB])
            P.op(P.dve, lambda: nc.vector.tensor_scalar(out=oml[:], in0=lbv[:], scalar1=-1.0, scalar2=1.0,
                                                        op0=ALU.mult, op1=ALU.add), r=[cB], w=[cB])
        for k0 in range(0, 8, 4):
            P.dma(w1[:, k0:k0 + 4, :], DAP(self.wb_in, (l * D + k0 * 128) * 7168,
                                           [(7168, 128), (128 * 7168, 4), (1, 3072)]), w=[wB], sb=wB)
        ntile = T // TT
        nsub = T // 128

        def load_x(n):
            b = n % 2
            P.dma(xs[b][:], DAP(src, (s * T + n * 128) * D, [(D, 128), (1, D)]), w=[xsB[b]], sb=xsB[b])

        def load_cs(i):
            b = i % 2
            P.dma(cs[b][:, 0, :, :], DAP(self.c_cos, i * 4 * 32, [(self.NT * 32, 128), (32, 4), (1, 32)]),
                  w=[csB[b]], sb=csB[b])
            P.dma(cs[b][:, 1, :, :], DAP(self.c_sin, i * 4 * 32, [(self.NT * 32, 128), (32, 4), (1, 32)]),
                  w=[csB[b]], sb=csB[b])

        def mm(bank, c0, ncol, col0, lhs_cols):
            for kc in range(8):
                P.op(P.pe, lambda kc=kc: nc.tensor.matmul(
                    ps[:, bank, c0:c0 + ncol], lhsT=hT[:, kc, lhs_cols[0]:lhs_cols[1]],
                    rhs=w1[:, kc, col0:col0 + ncol], start=(kc == 0), stop=(kc == 7)), r=[wB, hTB], w=[bB[bank]])

        def mm_tok(su):
            base = (su % 2) * 4
            lc = (su * 128, (su + 1) * 128)
            mm(base + 0, 0, 512, 256, lc)
            mm(base + 1, 0, 128, 768, lc)
            mm(base + 1, 128, 384, 2304, lc)
            mm(base + 2, 0, 128, 2688, lc)
            mm(base + 2, 128, 128, 896, lc)
            mm(base + 2, 256, 256, 0, lc)
            mm(base + 3, 0, 256, 1536, lc)
            mm(base + 3, 256, 256, 2816, lc)

        def chain_tok(i, su):
            import os
            cut = int(os.environ.get("P2CUT", "99"))
            g = chain_tok_(i, su)
            n = 0
            for _ in g:
                n += 1
                if n >= cut:
                    return
                yield

        def chain_tok_(i, su):
            k = su % 2
            base = k * 4
            cb = i % 2
            Hb = [bB[base], bB[base + 1], bB[base + 2]]
            H = ps[:, base:base + 3, :].rearrange("p b c -> p (b c)")[:, 0:1152]
            for (bo, c0, c1) in ((0, 0, 512), (1, 512, 1024), (2, 1024, 1152)):
                P.op(P.act, lambda bo=bo, c0=c0, c1=c1: nc.scalar.activation(
                    out=B1[k][:, c0:c1], in_=ps[:, base + bo, 0:c1 - c0], func=AF.Square), r=[Hb[bo]], w=[B1B[k]])
            P.op(P.act, lambda: nc.scalar.copy(out=s3[:, su, :], in_=ps[:, base + 3, :]), r=[bB[base + 3]], w=[s3B])
            yield
            P.op(P.dve, lambda: nc.vector.tensor_reduce(out=s18[k][:, 0, :], in_=B1[k][:].rearrange(
                "p (h d) -> p h d", h=18), op=ALU.add, axis=AX.X), r=[B1B[k]], w=[s18B[k]])
            yield
            P.op(P.act, lambda: nc.scalar.activation(out=s18[k][:, 1, :], in_=s18[k][:, 0, :], func=AF.Ln,
                                                     scale=1.0 / 64, bias=eps_t[:, 0:1]), r=[s18B[k], cB], w=[s18B[k]])
            P.op(P.act, lambda: nc.scalar.activation(out=s18[k][:, 2, :], in_=s18[k][:, 1, :], func=AF.Exp,
                                                     scale=-0.5), r=[s18B[k]], w=[s18B[k]])
            P.op(P.act, lambda: nc.scalar.copy(out=zps[:, su, :], in_=ps[:, base + 2, 256:512]),
                 r=[bB[base + 2]], w=[zpsB])
            P.op(P.act, lambda: nc.scalar.copy(out=gvs[:, su, :], in_=ps[:, base + 2, 128:256]),
                 r=[bB[base + 2]], w=[gvsB])
            yield
            for (bo, h0, h1) in ((0, 0, 8), (1, 8, 16), (2, 16, 18)):
                nh = h1 - h0
                P.op(P.dve, lambda bo=bo, h0=h0, h1=h1, nh=nh: nc.vector.tensor_tensor(
                    out=B1[k][:, h0 * 64:h1 * 64].rearrange("p (h d) -> p h d", h=nh),
                    in0=ps[:, base + bo, 0:nh * 64].rearrange("p (h d) -> p h d", h=nh),
                    in1=s18[k][:, 2, h0:h1].unsqueeze(2).broadcast_to([128, nh, 64]), op=ALU.mult),
                    r=[Hb[bo], s18B[k]], w=[B1B[k]])
            yield
            P.op(P.pool, lambda: nc.gpsimd.tensor_tensor(
                out=B2[k][:], in0=B1[k][:], in1=G18[:].rearrange("p h d -> p (h d)"), op=ALU.mult),
                r=[B1B[k], cB], w=[B2B[k]])
            yield
            xg4 = B2[k][:, 0:640].rearrange("p (h t d) -> p h t d", h=10, t=2)
            t14 = B1[k][:, 0:640].rearrange("p (h t d) -> p h t d", h=10, t=2)
            t24 = B3[k][:, 0:640].rearrange("p (h t d) -> p h t d", h=10, t=2)
            qr4 = qr[k][:, 0:640].rearrange("p (h t d) -> p h t d", h=10, t=2)
            cb4 = cs[cb][:, 0, su, :].unsqueeze(1).unsqueeze(1).broadcast_to([128, 10, 2, 32])
            sb3 = cs[cb][:, 1, su, :].unsqueeze(1).broadcast_to([128, 10, 32])
            P.op(P.dve, lambda: nc.vector.tensor_tensor(out=t14, in0=xg4, in1=cb4, op=ALU.mult),
                 r=[B2B[k], csB[cb]], w=[B1B[k]])
            P.op(P.pool, lambda: nc.gpsimd.tensor_tensor(out=t24[:, :, 0, :], in0=xg4[:, :, 1, :], in1=sb3,
                                                         op=ALU.mult), r=[B2B[k], csB[cb]], w=[B3B[k]])
            P.op(P.pool, lambda: nc.gpsimd.tensor_tensor(out=t24[:, :, 1, :], in0=xg4[:, :, 0, :], in1=sb3,
                                                         op=ALU.mult), r=[B2B[k], csB[cb], B3B[k]], w=[B3B[k]])
            P.op(P.act, lambda: nc.scalar.copy(out=qr[k][:, 640:1152], in_=B2[k][:, 640:1152]),
                 r=[B2B[k]], w=[qrB[k]])
            yield
            P.op(P.dve, lambda: nc.vector.tensor_tensor(out=qr4[:, :, 0, :], in0=t14[:, :, 0, :], in1=t24[:, :, 0, :],
                                                        op=ALU.subtract), r=[B1B[k], B3B[k], qrB[k]], w=[qrB[k]])
            P.op(P.dve, lambda: nc.vector.tensor_tensor(out=qr4[:, :, 1, :], in0=t14[:, :, 1, :], in1=t24[:, :, 1, :],
                                                        op=ALU.add), r=[B1B[k], B3B[k], qrB[k]], w=[qrB[k]])
            yield
            tpA = ps[:, base, :].bitcast(BF16)
            tpB = ps[:, base + 1, :].bitcast(BF16)
            for pr in range(8):
                P.op(P.pe, lambda pr=pr: nc.tensor.transpose(tpA[:, pr * 128:(pr + 1) * 128],
                                                            qr[k][:, pr * 128:(pr + 1) * 128], ident[:]),
                     r=[qrB[k], cB], w=[bB[base]])
            P.op(P.pe, lambda: nc.tensor.transpose(tpB[:, 0:128], qr[k][:, 1024:1152], ident[:]),
                 r=[qrB[k], cB], w=[bB[base + 1]])
            yield
            P.op(P.act, lambda: nc.scalar.copy(out=qTs[:, :, su * 128:(su + 1) * 128],
                                               in_=tpA[:, 0:640].rearrange("p (c t) -> p c t", c=5)),
                 r=[bB[base]], w=[qTsB])
            P.op(P.act, lambda: nc.scalar.copy(out=nTs[:, 0:3, su * 128:(su + 1) * 128],
                                               in_=tpA[:, 640:1024].rearrange("p (c t) -> p c t", c=3)),
                 r=[bB[base]], w=[nTsB])
            P.op(P.act, lambda: nc.scalar.copy(out=nTs[:, 3, su * 128:(su + 1) * 128], in_=tpB[:, 0:128]),
                 r=[bB[base + 1]], w=[nTsB])
            yield

        def mm_feat(bank, col0):
            for kc in range(8):
                P.op(P.pe, lambda kc=kc: nc.tensor.matmul(
                    ps[:, bank, :], lhsT=w1[:, kc, col0:col0 + 128], rhs=hT[:, kc, :],
                    start=(kc == 0), stop=(kc == 7)), r=[wB, hTB], w=[bB[bank]])

        def chain_h(i, d):
            t0 = i * TT
            bk = 4 + 2 * d
            Z = ps[:, bk:bk + 2, :]
            Zb = [bB[bk], bB[bk + 1]]
            h_ = hh[d]
            hB_ = hhB[d]
            fl = lambda t: t[:].rearrange("p a b -> p (a b)")
            for hp in range(2):
                P.op(P.act, lambda hp=hp: nc.scalar.activation(out=h_[0][:, hp, :], in_=ps[:, bk + hp, :],
                                                               func=AF.Sigmoid), r=[Zb[hp]], w=[hB_[0]])
            yield
            for hp in range(2):
                P.op(P.dve, lambda hp=hp: nc.vector.tensor_scalar(
                    out=h_[1][:, hp, :], in0=h_[0][:, hp, :], scalar1=oml[:, d, hp:hp + 1], scalar2=lbv[:, d, hp:hp + 1],
                    op0=ALU.mult, op1=ALU.add), r=[hB_[0], cB], w=[hB_[1]])
            yield
            P.op(P.act, lambda: nc.scalar.activation(out=h_[2][:], in_=h_[1][:], func=AF.Ln), r=[hB_[1]], w=[hB_[2]])
            P.op(P.pool, lambda: nc.gpsimd.tensor_scalar(out=h_[3][:], in0=h_[1][:], scalar1=-1.0, scalar2=1.0,
                                                         op0=ALU.mult, op1=ALU.add), r=[hB_[1]], w=[hB_[3]])
            yield
            P.op(P.dve, lambda: nc.vector.tensor_tensor_scan(
                out=fl(h_[4]), data0=fl(rmask), data1=fl(h_[2]), initial=0.0, op0=ALU.mult, op1=ALU.add),
                r=[hB_[2], cB], w=[hB_[4]])
            yield
            bbi = 4
            if d == 1:
                P.op(P.pool, lambda: nc.gpsimd.tensor_tensor(out=fl(h_[2]), in0=fl(h_[2]), in1=fl(h_[4]),
                                                             op=ALU.subtract), r=[hB_[2], hB_[4]], w=[hB_[2]])
                yield
                P.op(P.dve, lambda: nc.vector.tensor_tensor(
                    out=fl(h_[5]).rearrange("p (n c) -> p n c", c=32),
                    in0=fl(h_[2]).rearrange("p (n c) -> p n c", c=32),
                    in1=fl(h_[4]).rearrange("p (n c) -> p n c", c=32)[:, :, 31:32].broadcast_to([128, 32, 32]),
                    op=ALU.add), r=[hB_[2], hB_[4]], w=[hB_[5]])
                yield
                bbi = 5
            P.op(P.act, lambda: nc.scalar.activation(out=h_[1][:], in_=h_[bbi][:], func=AF.Exp),
                 r=[hB_[bbi], hB_[3]], w=[hB_[1]])
            P.op(P.act, lambda: nc.scalar.activation(out=h_[0][:], in_=h_[bbi][:], func=AF.Exp, scale=-1.0),
                 r=[hB_[bbi]], w=[hB_[0]])
            yield
            P.op(P.dve, lambda: nc.vector.tensor_tensor(out=hq[d][0][:], in0=silq[:], in1=h_[1][:], op=ALU.mult),
                 r=[silqB, hB_[1]], w=[hqB[d][0]])
            P.op(P.pool, lambda: nc.gpsimd.tensor_tensor(out=h_[2][:], in0=h_[3][:], in1=h_[0][:], op=ALU.mult),
                 r=[hB_[3], hB_[0]], w=[hB_[2]])
            aidx = 31 if d == 0 else 0
            eb3 = fl(h_[1]).rearrange("p (n c) -> p n c", c=32)
            P.op(P.act, lambda: nc.scalar.copy(out=a_s[d][:].rearrange("p a n -> p (a n)").unsqueeze(2),
                                               in_=eb3[:, :, aidx:aidx + 1]), r=[hB_[1]], w=[a_sB[d]])
            yield
            P.op(P.act, lambda: nc.scalar.copy(out=hq[d][1][:], in_=h_[2][:]), r=[hB_[2]], w=[hqB[d][1]])
            P.op(P.dve, lambda: nc.vector.tensor_tensor(
                out=fl(hq[d][2]).rearrange("p (n c) -> p n c", c=32),
                in0=fl(h_[2]).rearrange("p (n c) -> p n c", c=32),
                in1=eb3[:, :, aidx:aidx + 1].broadcast_to([128, 32, 32]), op=ALU.mult),
                r=[hB_[2], hB_[1]], w=[hqB[d][2]])
            P.dma(DAP(self.aT, d * 256 * NCH + i * 16, [(NCH, 128), (128 * NCH, 2), (1, 16)]), a_s[d][:],
                  r=[a_sB[d]], sb=a_sB[d])
            P.dma(DAP(self.QT, d * 256 * T + t0, [(T, 128), (128 * T, 2), (1, TT)]), hq[d][0][:],
                  r=[hqB[d][0]], sb=hqB[d][0])
            yield
            P.dma(DAP(self.KT, d * 256 * T + t0, [(T, 128), (128 * T, 2), (1, TT)]), hq[d][1][:],
                  r=[hqB[d][1]], sb=hqB[d][1])
            tp = ps[:, bk, :].bitcast(BF16)
            for su in range(4):
                for hp in range(2):
                    P.op(P.pe, lambda su=su, hp=hp: nc.tensor.transpose(
                        tp[:, (su * 2 + hp) * 128:(su * 2 + hp + 1) * 128], hq[d][2][:, hp, su * 128:(su + 1) * 128],
                        ident[:]), r=[hqB[d][2], cB], w=[bB[bk]])
            yield
            P.op(P.act, lambda: nc.scalar.copy(out=kps[d][:].rearrange("p a b -> p (a b)"), in_=tp[:, 0:1024]),
                 r=[bB[bk]], w=[kpsB[d]])
            P.dma(DAP(self.Kp, (d * T + t0) * 256, [(256, 128), (128 * 256, 4), (1, 256)]), kps[d][:],
                  r=[kpsB[d]], sb=kpsB[d])
            yield

        def _igen(*gens):
            gens = list(gens)
            while gens:
                for g in list(gens):
                    try:
                        next(g)
                    except StopIteration:
                        gens.remove(g)
                yield

        def front(i):
            if i + 1 < ntile:
                load_cs(i + 1)
            for su in range(4):
                n = i * 4 + su
                if n + 1 < nsub:
                    load_x(n + 1)
                self.norm_transpose(xs[n % 2][:], xsB[n % 2], hT, hTB, su * 128,
                                    (junk, tmp[1], ss, tmp[3], hb, tmp[5], _BankView(ps, 7), bB[7], ident, eps_t, cB))
                yield

        def back(i):
            t0 = i * TT
            mm_tok(0)
            mm_tok(1)
            _interleave(chain_tok(i, 0), chain_tok(i, 1))
            mm_tok(2)
            mm_tok(3)
            _interleave(chain_tok(i, 2), chain_tok(i, 3))
            P.dma(DAP(self.zp, t0 * 256, [(256, 128), (128 * 256, 4), (1, 256)]), zps[:], r=[zpsB], sb=zpsB)
            P.dma(DAP(self.gv, t0 * 128, [(128, 128), (128 * 128, 4), (1, 128)]), gvs[:], r=[gvsB], sb=gvsB)
            P.dma(DAP(self.hv, t0 * 256, [(256, 128), (128 * 256, 4), (1, 256)]), s3[:, :, 0:256], r=[s3B], sb=s3B)
            P.dma(DAP(self.nv, t0 * 256, [(256, 128), (128 * 256, 4), (1, 256)]), s3[:, :, 256:512], r=[s3B], sb=s3B)
            P.dma(DAP(self.qT, t0, [(T, 128), (128 * T, 4), (1, TT)]), qTs[:, 0:4, :], r=[qTsB], sb=qTsB)
            P.dma(DAP(self.kT, t0, [(T, 128), (1, TT)]), qTs[:, 4, :], r=[qTsB], sb=qTsB)
            P.dma(DAP(self.nqkT, t0, [(T, 128), (128 * T, 4), (1, TT)]), nTs[:], r=[nTsB], sb=nTsB)
            for hp in range(2):
                mm_feat(0 + hp, 1792 + hp * 128)
            for hp in range(2):
                P.op(P.act, lambda hp=hp: nc.scalar.activation(out=silq[:, hp, :], in_=ps[:, hp, :], func=AF.Silu),
                     r=[bB[hp]], w=[silqB])
            for hp in range(2):
                mm_feat(2 + hp, 2048 + hp * 128)
            for hp in range(2):
                P.op(P.act, lambda hp=hp: nc.scalar.activation(out=gtf[:, hp, :], in_=ps[:, 2 + hp, :], func=AF.Silu),
                     r=[bB[2 + hp]], w=[gtfB])
            P.op(P.dve, lambda: nc.vector.tensor_scalar(out=gtb[:], in0=gtf[:], scalar1=ong[:, 0:1], scalar2=None,
                                                        op0=ALU.mult), r=[gtfB, cB], w=[gtbB])
            P.dma(DAP(self.gateT, t0, [(T, 128), (128 * T, 2), (1, TT)]), gtb[:], r=[gtbB], sb=gtbB)
            for d in range(2):
                for hp in range(2):
                    mm_feat(4 + 2 * d + hp, 1024 + d * 256 + hp * 128)

        load_x(0)
        load_cs(0)
        for _ in front(0):
            pass
        for i in range(ntile):
            back(i)
            gens = [chain_h(i, 0), chain_h(i, 1)]
            if i + 1 < ntile:
                gens.append(front(i + 1))
            _interleave(*gens)
        P.barrier()
        P.flush()
        P.release(xsB + csB + [wB, cB, zpsB, gvsB, s3B, qTsB, nTsB, gtbB] + a_sB + kpsB + hqB[0] + hqB[1])


class _BankView:
    def __init__(self, ps, b):
        self.ps, self.b = ps, b

    def __getitem__(self, idx):
        return self.ps[:, self.b, :]


Builder.proj2 = _proj2


def _hgrn2(self, l):
    nc, P, T, NT = self.nc, self.P, self.T, self.NT
    NCH = T // 32
    if not hasattr(self, "obw"):
        raise RuntimeError("declare obw first")
    with ExitStack() as st:
        hmask = self.sb(st, "hmask", [128, 2, 512], BF16)
        cm = self.sb(st, "cm", [128, 4], BF16)
        cB = Buf("c")
        banks = self.psum_banks(st)
        bB = P.bufs(8, "bank")
        P.dma(hmask[:], self.c_hmask.ap(), w=[cB], sb=cB)
        P.dma(cm[:], self.c_cm.ap(), w=[cB], sb=cB)
        NB = 4
        rel = [cB]

        def make_dir(d):
            order = list(range(NT)) if d == 0 else list(range(NT - 1, -1, -1))
            corder = [0, 1, 2, 3] if d == 0 else [3, 2, 1, 0]
            NBs = 2
            qt = [self.sb(st, f"qt{d}{i}", [64, 4, 512], BF16) for i in range(NBs)]
            kt = [self.sb(st, f"kt{d}{i}", [64, 4, 512], BF16) for i in range(NBs)]
            kp = [self.sb(st, f"kp{d}{i}", [128, 4, 256], BF16) for i in range(NBs)]
            vt = [self.sb(st, f"vt{d}{i}", [128, 4, 256], BF16) for i in range(NBs)]
            at = [self.sb(st, f"at{d}{i}", [64, 4, 16], F32) for i in range(NBs)]
            ldB = P.bufs(NBs, f"ld{d}")
            NST = NT // 4
            kpm = [self.sb(st, f"kpm{d}{i}", [128, 4, 256], BF16) for i in range(2)]
            kpmB = P.bufs(2, f"kpm{d}")
            Sf = [self.sb(st, f"Sf{d}{j}", [64, 4, 64], F32) for j in range(3)]
            SfhB = [P.bufs(2, f"Sfh{d}{j}") for j in range(3)]
            stmp = self.sb(st, f"stmp{d}", [64, 4, 64], F32)
            sthB = P.bufs(2, f"sth{d}")
            Sb = [self.sb(st, f"Sb{d}{i}", [64, 4, 4, 64], BF16) for i in range(2)]
            SbB = P.bufs(2, f"Sb{d}")
            Am = [self.sb(st, f"Am{d}{i}", [128, 4, 128], BF16) for i in range(2)]
            AmB = P.bufs(2, f"Am{d}")
            osm = [self.sb(st, f"osm{d}{i}", [64, 4, 512], F32) for i in range(2)]
            osB = P.bufs(2, f"osm{d}")
            Us = [self.sb(st, f"Us{d}{i}", [64, 1024], F32) for i in range(2)]
            UsB = P.bufs(2, f"Us{d}")
            rel.extend(ldB + osB)
            ub = [0, 1] if d == 0 else [4, 5]
            ab = 2 if d == 0 else 6
            ob = 3 if d == 0 else 7
            odst = self.ofw if d == 0 else self.obw
            cur = [0]
            hd = [(T, 64), (64 * T, 4), (1, 128)]

            def init():
                P.op(P.dve, lambda: nc.vector.memset(Sf[0][:], 0.0), w=SfhB[0])

            def load(g):
                sidx = g if d == 0 else NST - 1 - g
                b = g % NBs
                t0 = sidx * 512
                hd5 = [(T, 64), (64 * T, 4), (1, 512)]
                P.dma(qt[b][:], DAP(self.QT, d * 256 * T + t0, hd5), w=[ldB[b]], sb=ldB[b])
                P.dma(kt[b][:], DAP(self.KT, d * 256 * T + t0, hd5), w=[ldB[b]], sb=ldB[b])
                P.dma(kp[b][:], DAP(self.Kp, (d * T + t0) * 256, [(256, 128), (128 * 256, 4), (1, 256)]),
                      w=[ldB[b]], sb=ldB[b])
                P.dma(vt[b][:], DAP(self.hv, t0 * 256, [(256, 128), (128 * 256, 4), (1, 256)]), w=[ldB[b]], sb=ldB[b])
                P.dma(at[b][:], DAP(self.aT, d * 256 * NCH + sidx * 16, [(NCH, 64), (64 * NCH, 4), (1, 16)]),
                      w=[ldB[b]], sb=ldB[b])

            def stage1(step):
                b = (step // 4) % NBs
                lc = order[step] % 4
                p2 = step % 2
                P.op(P.pool, lambda: nc.gpsimd.tensor_tensor(
                    out=kpm[p2][:], in0=kp[b][:, lc, :].unsqueeze(1).broadcast_to([128, 4, 256]),
                    in1=cm[:].unsqueeze(2).broadcast_to([128, 4, 256]), op=ALU.mult), r=[ldB[b], cB], w=[kpmB[p2]])
                for c in range(4):
                    for h in range(4):
                        col = ((c % 2) * 4 + h) * 64
                        P.op(P.pe, lambda c=c, h=h, col=col: nc.tensor.matmul(
                            banks[ub[c // 2]][0:64, col:col + 64], lhsT=kpm[p2][:, c, h * 64:(h + 1) * 64],
                            rhs=vt[b][:, lc, h * 64:(h + 1) * 64], start=True, stop=True),
                            r=[kpmB[p2], ldB[b]], w=[bB[ub[c // 2]]])
                for j in range(2):
                    P.op(P.act, lambda j=j: nc.scalar.copy(out=Us[p2][:, j * 512:(j + 1) * 512], in_=banks[ub[j]][0:64, :]),
                         r=[bB[ub[j]]], w=[UsB[p2]])
                for h in range(4):
                    P.op(P.pe, lambda h=h: nc.tensor.matmul(
                        banks[ab][:, h * 128:(h + 1) * 128], lhsT=kt[b][:, h, lc * 128:(lc + 1) * 128],
                        rhs=qt[b][:, h, lc * 128:(lc + 1) * 128], start=True, stop=True), r=[ldB[b]], w=[bB[ab]])
                P.op(P.dve, lambda: nc.vector.tensor_tensor(
                    out=Am[p2][:].rearrange("p h t -> p (h t)"), in0=banks[ab][:], in1=hmask[:, d, :], op=ALU.mult),
                    r=[bB[ab], cB], w=[AmB[p2]])

            def chain(step):
                b = (step // 4) % NBs
                lc = order[step] % 4
                p2 = step % 2
                for c in corder:
                    cu = cur[0]
                    nx = (cu + 1) % 3
                    P.op(P.act, lambda c=c, cu=cu: nc.scalar.copy(out=Sb[p2][:, c, :, :], in_=Sf[cu][:]),
                         r=SfhB[cu], w=[SbB[p2]])
                    yield
                    ucol = (c % 2) * 256
                    for hh in range(2):
                        P.op(P.dve, lambda c=c, cu=cu, hh=hh: nc.vector.tensor_tensor(
                            out=stmp[:, 2 * hh:2 * hh + 2, :], in0=Sf[cu][:, 2 * hh:2 * hh + 2, :],
                            in1=at[b][:, 2 * hh:2 * hh + 2, lc * 4 + c:lc * 4 + c + 1].broadcast_to([64, 2, 64]),
                            op=ALU.mult),
                            r=[SfhB[cu][hh], ldB[b]], w=[sthB[hh]])
                        yield
                    for hh in range(2):
                        P.op(P.dve, lambda c=c, nx=nx, ucol=ucol, hh=hh: nc.vector.tensor_tensor(
                            out=Sf[nx][:, 2 * hh:2 * hh + 2, :].rearrange("p h v -> p (h v)"),
                            in0=Us[p2][:, c * 256 + hh * 128: c * 256 + (hh + 1) * 128],
                            in1=stmp[:, 2 * hh:2 * hh + 2, :].rearrange("p h v -> p (h v)"), op=ALU.add),
                            r=[sthB[hh], UsB[p2]], w=[SfhB[nx][hh]])
                        yield
                    cur[0] = nx

            def stage2(step):
                n = order[step]
                b = (step // 4) % NBs
                lc = n % 4
                g = step // 4
                p2 = step % 2
                for h in range(4):
                    P.op(P.pe, lambda h=h: nc.tensor.matmul(
                        banks[ob][0:64, h * 128:(h + 1) * 128], lhsT=vt[b][:, lc, h * 64:(h + 1) * 64],
                        rhs=Am[p2][:, h, :], start=True, stop=False), r=[ldB[b], AmB[p2]], w=[bB[ob]])
                    for c in range(4):
                        P.op(P.pe, lambda h=h, c=c: nc.tensor.matmul(
                            banks[ob][0:64, h * 128 + c * 32: h * 128 + (c + 1) * 32],
                            lhsT=Sb[p2][:, c, h, :], rhs=qt[b][:, h, lc * 128 + c * 32: lc * 128 + (c + 1) * 32],
                            start=False, stop=(c == 3)), r=[ldB[b], SbB[p2]], w=[bB[ob]])
                P.op(P.act, lambda: nc.scalar.copy(out=osm[g % 2][:, :, lc * 128:(lc + 1) * 128],
                                                   in_=banks[ob][0:64, :].rearrange("p (h t) -> p h t", h=4)),
                     r=[bB[ob]], w=[osB[g % 2]])
                if step % 4 == 3:
                    sidx = n // 4
                    P.dma(DAP(odst, sidx * 512, [(T, 64), (64 * T, 4), (1, 512)]), osm[g % 2][:],
                          r=[osB[g % 2]], sb=osB[g % 2])

            return init, load, stage1, stage2, chain

        dirs = [make_dir(0), make_dir(1)]
        assert NT % 4 == 0
        for (init, load, s1, s2, ch) in dirs:
            init()
            load(0)
        for step in range(NT + 1):
            for (init, load, s1, s2, ch) in dirs:
                if step % 4 == 1 and step // 4 + 1 < NT // 4:
                    load(step // 4 + 1)
            if step < NT:
                for (init, load, s1, s2, ch) in dirs:
                    s1(step)
                _interleave(dirs[0][4](step), dirs[1][4](step))
            for (init, load, s1, s2, ch) in dirs:
                if step >= 1:
                    s2(step - 1)
        P.barrier()
        P.flush()
        P.release(rel)

    TT = 512
    with ExitStack() as st:
        ones = self.sb(st, "ones", [128, 64], BF16)
        eps_t = self.sb(st, "eps", [128, 1], F32)
        cB = Buf("c")
        of = [self.sb(st, f"of{i}", [64, 4, TT], F32) for i in range(2)]
        ob_ = [self.sb(st, f"ob{i}", [64, 4, TT], F32) for i in range(2)]
        gt = [self.sb(st, f"gt{i}", [64, 4, TT], BF16) for i in range(2)]
        ldB = P.bufs(2, "ld")
        sqb = self.sb(st, "sqb", [64, 4, TT], BF16)
        sqB = Buf("sqb")
        rr = self.sb(st, "rr", [64, 4, TT], F32)
        rrB = Buf("rr")
        ocs = [self.sb(st, f"ocs{i}", [64, 4, TT], BF16) for i in range(2)]
        ocB = P.bufs(2, "ocs")
        banks = self.psum_banks(st)
        bB = P.bufs(8, "bank")
        P.op(P.dve, lambda: nc.vector.memset(ones[:], 1.0), w=[cB])
        P.op(P.dve, lambda: nc.vector.memset(eps_t[:], EPS), w=[cB])
        ntile = T // TT
        hd = [(T, 64), (64 * T, 4), (1, TT)]

        def load(i):
            b = i % 2
            P.dma(of[b][:], DAP(self.ofw, i * TT, hd), w=[ldB[b]], sb=ldB[b])
            P.dma(ob_[b][:], DAP(self.obw, i * TT, hd), w=[ldB[b]], sb=ldB[b])
            P.dma(gt[b][:], DAP(self.gateT, i * TT, hd), w=[ldB[b]], sb=ldB[b])

        load(0)
        for i in range(ntile):
            b = i % 2
            if i + 1 < ntile:
                load(i + 1)
            fl = lambda t: t[:].rearrange("p h t -> p (h t)")
            P.op(P.dve, lambda b=b: nc.vector.tensor_tensor(out=fl(of[b]), in0=fl(of[b]), in1=fl(ob_[b]), op=ALU.add),
                 r=[ldB[b]], w=[ldB[b]])
            P.op(P.act, lambda b=b: nc.scalar.activation(out=fl(sqb), in_=fl(of[b]), func=AF.Square),
                 r=[ldB[b]], w=[sqB])
            for h in range(4):
                pb = (i % 2) * 4 + h
                P.op(P.pe, lambda h=h, pb=pb: nc.tensor.matmul(banks[pb][0:64, :], lhsT=ones[0:64, 0:64],
                                                              rhs=sqb[:, h, :], start=True, stop=True),
                     r=[cB, sqB], w=[bB[pb]])
                P.op(P.act, lambda h=h, pb=pb: nc.scalar.activation(out=rr[:, h, :], in_=banks[pb][0:64, :], func=AF.Ln,
                                                                    scale=1.0 / 64, bias=eps_t[0:64, 0:1]),
                     r=[bB[pb], cB], w=[rrB])
            P.op(P.act, lambda: nc.scalar.activation(out=fl(rr), in_=fl(rr), func=AF.Exp, scale=-0.5),
                 r=[rrB], w=[rrB])
            P.op(P.pool, lambda b=b: nc.gpsimd.tensor_tensor(out=fl(rr), in0=fl(rr), in1=fl(gt[b]), op=ALU.mult),
                 r=[rrB, ldB[b]], w=[rrB])
            P.op(P.dve, lambda b=b: nc.vector.tensor_tensor(out=fl(ocs[b]), in0=fl(of[b]), in1=fl(rr), op=ALU.mult),
                 r=[ldB[b], rrB], w=[ocB[b]])
            P.dma(DAP(self.oT, 768 * T + i * TT, hd), ocs[b][:], r=[ocB[b]], sb=ocB[b])
        P.barrier()
        P.flush()
        P.release(ldB + ocB + [cB])


Builder.hgrn2 = _hgrn2
```
